# Optimizing a Trainium2 kernel written in Bass

```python
import functools
import numpy as np
import jax
import jax.numpy as jnp
from jax import lax

D_MODEL = 4096
BATCH = 4
SEQ = 2048
DEPTH = 2

GRID_W = 64
CTX_LEN = 256
CHUNK = 128
EPS = 1e-6
ROPE_BASE = 10000.0
N_EVEN = (DEPTH + 1) // 2
N_ODD = DEPTH // 2
RET_HEADS = 8
RET_DV = D_MODEL // (2 * RET_HEADS)
RET_DK = RET_DV // 2
MLSTM_HEADS = 8
MLSTM_DV = D_MODEL // (2 * MLSTM_HEADS)
MLSTM_DK = MLSTM_DV // 2
CONV_K = 3
A_QK = RET_HEADS * RET_DK
A_V = RET_HEADS * RET_DV
B_QK = MLSTM_HEADS * MLSTM_DK
B_V = MLSTM_HEADS * MLSTM_DV
N_GATE = 4 * MLSTM_HEADS
IN_SIZES = (A_QK, A_QK, A_V, A_V, 2 * B_QK, B_V, B_V, N_GATE)
IN_COLS = sum(IN_SIZES)
MIX_W = A_V + B_V
RWKV_N = 64
RWKV_HEADS = D_MODEL // RWKV_N
DECAY_LORA = max(32, int(round(1.8 * D_MODEL ** 0.5 / 32)) * 32)
AAA_LORA = DECAY_LORA
GATE_LORA = max(32, int(round(0.6 * D_MODEL ** 0.8 / 32)) * 32)
LNX_EPS = 64e-5
D_FF = 4 * D_MODEL

kernel_name = 'hybrid_retention_mlstm_rwkv7_prefix_dit'


def rms_norm(x, g):
    xf = x.astype(jnp.float32)
    y = xf * lax.rsqrt(jnp.mean(jnp.square(xf), axis=-1, keepdims=True) + EPS)
    return (y * g.astype(jnp.float32)).astype(x.dtype)


def modulate(h, g, shift, scale):
    return rms_norm(h, g) * (1 + scale) + shift


def to_heads(t, n_heads):
    b, s, _ = t.shape
    return t.reshape(b, s, n_heads, -1).transpose(0, 2, 1, 3)


def from_heads(t):
    b, h, s, d = t.shape
    return t.transpose(0, 2, 1, 3).reshape(b, s, h * d)


def head_rms_norm(t, g):
    tf = t.astype(jnp.float32)
    y = tf * lax.rsqrt(jnp.mean(jnp.square(tf), axis=-1, keepdims=True) + EPS)
    return from_heads(y) * g.astype(jnp.float32)


def rope_1d(t, pos):
    half = t.shape[-1] // 2
    inv = jnp.power(ROPE_BASE, -jnp.arange(half, dtype=jnp.float32) / half)
    ang = pos[:, None] * inv[None, :]
    cos, sin = jnp.cos(ang), jnp.sin(ang)
    tf = t.astype(jnp.float32)
    t1, t2 = tf[..., :half], tf[..., half:]
    return jnp.concatenate([t1 * cos - t2 * sin, t1 * sin + t2 * cos], axis=-1).astype(t.dtype)


def axial_rope(t, rows, cols):
    half = t.shape[-1] // 2
    return jnp.concatenate([rope_1d(t[..., :half], rows), rope_1d(t[..., half:], cols)], axis=-1)


def to_chunks(t):
    b, h, s = t.shape[:3]
    t = t.reshape((b, h, s // CHUNK, CHUNK) + t.shape[3:])
    return jnp.moveaxis(t, 2, 0)


def from_chunks(t):
    t = jnp.moveaxis(t, 0, 2)
    b, h, n, l = t.shape[:4]
    return t.reshape((b, h, n * l) + t.shape[4:])


def retention_log_decay():
    h = jnp.arange(RET_HEADS, dtype=jnp.float32) / max(RET_HEADS - 1, 1)
    return jnp.log1p(-jnp.exp2(-(5.0 + 7.0 * h)))


def retention_chunked(q, k, v, state, log_gamma):
    dtype = v.dtype
    q, k, v = q.astype(jnp.float32), k.astype(jnp.float32) * RET_DK ** -0.5, v.astype(jnp.float32)
    pos = jnp.arange(CHUNK, dtype=jnp.float32)
    diff = pos[:, None] - pos[None, :]
    intra = jnp.where(diff >= 0, jnp.exp(jnp.maximum(diff, 0.0)[None] * log_gamma[:, None, None]), 0.0)
    q_dec = jnp.exp((pos[None, :] + 1.0) * log_gamma[:, None])[..., None]
    k_dec = jnp.exp((CHUNK - 1.0 - pos[None, :]) * log_gamma[:, None])[..., None]
    c_dec = jnp.exp(CHUNK * log_gamma)[:, None, None]

    def step(s, blk):
        qc, kc, vc = blk
        sc = jnp.einsum('bhid,bhjd->bhij', qc, kc) * intra
        o = jnp.einsum('bhij,bhjv->bhiv', sc, vc) + jnp.einsum('bhid,bhdv->bhiv', qc * q_dec, s)
        s = s * c_dec + jnp.einsum('bhjd,bhjv->bhdv', kc * k_dec, vc)
        return s, o

    s_fin, o = lax.scan(step, state, (to_chunks(q), to_chunks(k), to_chunks(v)))
    return from_chunks(o).astype(dtype), s_fin


def mlstm_chunked(q, k, v, i_pre, log_f, state):
    dtype = v.dtype
    q, k, v = q.astype(jnp.float32), k.astype(jnp.float32) * MLSTM_DK ** -0.5, v.astype(jnp.float32)
    causal = jnp.tril(jnp.ones((CHUNK, CHUNK), dtype=bool))

    def step(carry, blk):
        c_mem, n_mem, m = carry
        qc, kc, vc, ic, fc = blk
        b = jnp.cumsum(fc, axis=-1)
        d_log = jnp.where(causal, b[..., :, None] - b[..., None, :] + ic[..., None, :], -jnp.inf)
        m_inter = b + m[..., None]
        m_row = jnp.maximum(jnp.max(d_log, axis=-1), m_inter)
        w_intra = jnp.exp(d_log - m_row[..., None])
        w_inter = jnp.exp(m_inter - m_row)
        sc = jnp.einsum('bhid,bhjd->bhij', qc, kc) * w_intra
        num = jnp.einsum('bhij,bhjv->bhiv', sc, vc) + w_inter[..., None] * jnp.einsum('bhid,bhdv->bhiv', qc, c_mem)
        den = jnp.sum(sc, axis=-1) + w_inter * jnp.einsum('bhid,bhd->bhi', qc, n_mem)
        h = num / jnp.maximum(jnp.abs(den), jnp.exp(-m_row))[..., None]
        b_last = b[..., -1]
        g = b_last[..., None] - b + ic
        m_new = jnp.maximum(b_last + m, jnp.max(g, axis=-1))
        w_k = jnp.exp(g - m_new[..., None])
        w_c = jnp.exp(b_last + m - m_new)
        c_mem = w_c[..., None, None] * c_mem + jnp.einsum('bhjd,bhjv->bhdv', kc * w_k[..., None], vc)
        n_mem = w_c[..., None] * n_mem + jnp.einsum('bhjd,bhj->bhd', kc, w_k)
        return (c_mem, n_mem, m_new), h

    blks = (to_chunks(q), to_chunks(k), to_chunks(v),
            to_chunks(i_pre.astype(jnp.float32)), to_chunks(log_f.astype(jnp.float32)))
    s_fin, h = lax.scan(step, state, blks)
    return from_chunks(h).astype(dtype), s_fin


def rwkv7_scan(r, w, k, v, a, b, state):
    dtype = v.dtype
    tm = lambda t: jnp.moveaxis(t.astype(jnp.float32), 2, 0)

    def step(s, inp):
        rt, wt, kt, vt, at, bt = inp
        sa = jnp.einsum('bhvk,bhk->bhv', s, at)
        s = s * wt[:, :, None, :] + sa[..., None] * bt[:, :, None, :] + vt[..., None] * kt[:, :, None, :]
        return s, jnp.einsum('bhvk,bhk->bhv', s, rt)

    s_fin, y = lax.scan(step, state, (tm(r), tm(w), tm(k), tm(v), tm(a), tm(b)))
    return jnp.moveaxis(y, 0, 2).astype(dtype), s_fin


def run_bidir(scan_f, scan_b, ctx_f, lat_f, ctx_b, lat_b, init):
    rev = lambda t: jnp.flip(t, axis=2)
    yc_f, st_f = scan_f(*ctx_f, init)
    yl_f, _ = scan_f(*lat_f, st_f)
    yc_b, st_b = scan_b(*[rev(t) for t in ctx_b], init)
    yl_b, _ = scan_b(*[rev(t) for t in lat_b], st_b)
    return yc_f + rev(yc_b), yl_f + rev(yl_b)


def dwconv_grid(t, w):
    b, s, ch = t.shape
    y = lax.conv_general_dilated(t.reshape(b, s // GRID_W, GRID_W, ch), w[:, :, None, :], (1, 1), 'SAME',
                                 dimension_numbers=('NHWC', 'HWIO', 'NHWC'), feature_group_count=ch)
    return y.reshape(b, s, ch)


def dwconv_seq(t, w):
    ch = t.shape[-1]
    return lax.conv_general_dilated(t, w[CONV_K // 2][:, None, :], (1,), 'SAME',
                                    dimension_numbers=('NWC', 'WIO', 'NWC'), feature_group_count=ch)


def qshift_grid(t):
    b, s, d = t.shape
    g = t.reshape(b, s // GRID_W, GRID_W, d)
    q = d // 4
    left = jnp.pad(g[:, :, :-1, :q], ((0, 0), (0, 0), (1, 0), (0, 0)))
    right = jnp.pad(g[:, :, 1:, q:2 * q], ((0, 0), (0, 0), (0, 1), (0, 0)))
    up = jnp.pad(g[:, :-1, :, 2 * q:3 * q], ((0, 0), (1, 0), (0, 0), (0, 0)))
    down = jnp.pad(g[:, 1:, :, 3 * q:], ((0, 0), (0, 1), (0, 0), (0, 0)))
    return jnp.concatenate([left, right, up, down], axis=-1).reshape(b, s, d)


def shift_seq(t):
    h = t.shape[-1] // 2
    prev = jnp.pad(t[:, :-1, :h], ((0, 0), (1, 0), (0, 0)))
    nxt = jnp.pad(t[:, 1:, h:], ((0, 0), (0, 1), (0, 0)))
    return jnp.concatenate([prev, nxt], axis=-1)


def even_mixer(h_ctx, h_lat, w_in, b_gate, conv_qk, gn_ret, gn_ml, w_out, with_ctx):
    bsz, s_lat, _ = h_lat.shape
    pos = jnp.arange(s_lat)
    rows = (pos // GRID_W).astype(jnp.float32)
    cols = (pos % GRID_W).astype(jnp.float32)
    offsets = [int(o) for o in np.cumsum(IN_SIZES)[:-1]]

    def project(h, conv, rotate):
        aq, ak, av, ag, bqk, bv, bo, gates = jnp.split(h @ w_in, offsets, axis=-1)
        bq, bk = jnp.split(jax.nn.silu(conv(bqk, conv_qk)), 2, axis=-1)
        aq, ak = to_heads(aq, RET_HEADS), to_heads(ak, RET_HEADS)
        if rotate:
            aq, ak = axial_rope(aq, rows, cols), axial_rope(ak, rows, cols)
        g4 = (gates + b_gate).astype(jnp.float32)
        g4 = g4.reshape(h.shape[0], h.shape[1], 4, MLSTM_HEADS).transpose(2, 0, 3, 1)
        qkv_b = (to_heads(bq, MLSTM_HEADS), to_heads(bk, MLSTM_HEADS), to_heads(bv, MLSTM_HEADS))
        ret_args = (aq, ak, to_heads(av, RET_HEADS))
        ml_f = qkv_b + (g4[0], jax.nn.log_sigmoid(g4[1]))
        ml_b = qkv_b + (g4[2], jax.nn.log_sigmoid(g4[3]))
        return ret_args, ml_f, ml_b, ag, bo

    ret_c, mlf_c, mlb_c, ag_c, bo_c = project(h_ctx, dwconv_seq, False)
    ret_l, mlf_l, mlb_l, ag_l, bo_l = project(h_lat, dwconv_grid, True)

    lg = retention_log_decay()
    ret_init = jnp.zeros((bsz, RET_HEADS, RET_DK, RET_DV), jnp.float32)
    yr_c, yr_l = run_bidir(functools.partial(retention_chunked, log_gamma=lg),
                           functools.partial(retention_chunked, log_gamma=lg[::-1]),
                           ret_c, ret_l, ret_c, ret_l, ret_init)
    ml_init = (jnp.zeros((bsz, MLSTM_HEADS, MLSTM_DK, MLSTM_DV), jnp.float32),
               jnp.zeros((bsz, MLSTM_HEADS, MLSTM_DK), jnp.float32),
               jnp.zeros((bsz, MLSTM_HEADS), jnp.float32))
    ym_c, ym_l = run_bidir(mlstm_chunked, mlstm_chunked, mlf_c, mlf_l, mlb_c, mlb_l, ml_init)

    def merge(yr, ym, ag, bo):
        ret = head_rms_norm(yr, gn_ret) * jax.nn.silu(ag.astype(jnp.float32))
        ml = head_rms_norm(ym, gn_ml) * jax.nn.sigmoid(bo.astype(jnp.float32))
        return jnp.concatenate([ret, ml], axis=-1).astype(ag.dtype) @ w_out

    out_ctx = merge(yr_c, ym_c, ag_c, bo_c) if with_ctx else None
    return out_ctx, merge(yr_l, ym_l, ag_l, bo_l)


def rwkv_mixer(h_ctx, h_lat, mu, w_rkv, w0, w1, w2, a0, a1, a2, g1, g2, k_k, k_a, r_k,
               lnx_w, lnx_b, w_out, with_ctx):
    bsz = h_lat.shape[0]

    def prepare(h, shifted):
        xx = shifted - h
        xr, xw, xk, xv, xa, xg = [h + xx * mu[i] for i in range(6)]
        r = xr @ w_rkv[0]
        k = xk @ w_rkv[1]
        v = xv @ w_rkv[2]
        gate = jax.nn.sigmoid(xg @ g1) @ g2
        kk = to_heads(k * k_k, RWKV_HEADS).astype(jnp.float32)
        kk = kk * lax.rsqrt(jnp.maximum(jnp.sum(jnp.square(kk), axis=-1, keepdims=True), 1e-24))
        r_h, v_h = to_heads(r, RWKV_HEADS), to_heads(v, RWKV_HEADS)
        dirs = []
        for d in range(2):
            w_log = -jax.nn.softplus(-(w0[d] + jnp.tanh(xw @ w1[d]) @ w2[d]).astype(jnp.float32)) - 0.5
            decay = jnp.exp(-jnp.exp(w_log))
            a = jax.nn.sigmoid((a0[d] + (xa @ a1[d]) @ a2[d]).astype(jnp.float32))
            k_d = k.astype(jnp.float32) * (1 + (a - 1) * k_a.astype(jnp.float32))
            a_h = to_heads(a, RWKV_HEADS)
            dirs.append((r_h, to_heads(decay, RWKV_HEADS), to_heads(k_d, RWKV_HEADS), v_h, -kk, kk * a_h))
        k_bonus = 0.5 * (dirs[0][2] + dirs[1][2])
        return dirs, gate, k_bonus, r_h, v_h

    dirs_c, gate_c, kb_c, r_c, v_c = prepare(h_ctx, shift_seq(h_ctx))
    dirs_l, gate_l, kb_l, r_l, v_l = prepare(h_lat, qshift_grid(h_lat))
    init = jnp.zeros((bsz, RWKV_HEADS, RWKV_N, RWKV_N), jnp.float32)
    y_c, y_l = run_bidir(rwkv7_scan, rwkv7_scan, dirs_c[0], dirs_l[0], dirs_c[1], dirs_l[1], init)
    r_k_h = r_k.reshape(RWKV_HEADS, 1, RWKV_N).astype(jnp.float32)

    def finish(y, r, v, kb, gate):
        yf = y.astype(jnp.float32)
        mean = jnp.mean(yf, axis=-1, keepdims=True)
        var = jnp.mean(jnp.square(yf - mean), axis=-1, keepdims=True)
        yn = from_heads((yf - mean) * lax.rsqrt(var + LNX_EPS)) * lnx_w + lnx_b
        bonus = jnp.sum(r.astype(jnp.float32) * kb * r_k_h, axis=-1, keepdims=True) * v.astype(jnp.float32)
        out = (yn + from_heads(bonus)) * gate.astype(jnp.float32)
        return out.astype(gate.dtype) @ w_out

    out_ctx = finish(y_c, r_c, v_c, kb_c, gate_c) if with_ctx else None
    return out_ctx, finish(y_l, r_l, v_l, kb_l, gate_l)


def squared_relu_mlp(u, w1, w2):
    return jnp.square(jax.nn.relu(u @ w1)) @ w2


def setup_inputs(seed: int = 0) -> dict:
    key = jax.random.key(seed)
    keys = iter(jax.random.split(key, 48))
    D = D_MODEL

    def nrm(shape, scale):
        return jax.random.normal(next(keys), shape, jnp.float32) * scale

    def unif(shape, lo, hi):
        return jax.random.uniform(next(keys), shape, jnp.float32, minval=lo, maxval=hi)

    fbias = jnp.linspace(3.0, 6.0, MLSTM_HEADS, dtype=jnp.float32)
    ev_b_gate = jnp.concatenate([nrm((N_EVEN, MLSTM_HEADS), 0.1), fbias + nrm((N_EVEN, MLSTM_HEADS), 0.1),
                                 nrm((N_EVEN, MLSTM_HEADS), 0.1), fbias + nrm((N_EVEN, MLSTM_HEADS), 0.1)], axis=-1)
    return {
        'x': nrm((BATCH, SEQ, D), 1.0),
        'c': nrm((BATCH, D), 1.0),
        'ctx': nrm((BATCH, CTX_LEN, D), 1.0),
        'c_ctx': nrm((D,), 1.0),
        'ada_w': nrm((DEPTH, D, 6 * D), D ** -0.5),
        'ada_b': nrm((DEPTH, 6 * D), 0.01),
        'norm_g': 1.0 + nrm((DEPTH, 4, D), 0.1),
        'mlp_w_in': nrm((DEPTH, D, D_FF), D ** -0.5),
        'mlp_w_out': nrm((DEPTH, D_FF, D), D_FF ** -0.5),
        'ev_w_in': nrm((N_EVEN, D, IN_COLS), D ** -0.5),
        'ev_b_gate': ev_b_gate,
        'ev_conv_qk': nrm((N_EVEN, CONV_K, CONV_K, 2 * B_QK), 1.0 / CONV_K),
        'ev_gn_ret': 1.0 + nrm((N_EVEN, A_V), 0.1),
        'ev_gn_mlstm': 1.0 + nrm((N_EVEN, B_V), 0.1),
        'ev_w_out': nrm((N_EVEN, MIX_W, D), MIX_W ** -0.5),
        'od_mu': unif((N_ODD, 6, D), 0.0, 1.0),
        'od_w_rkv': nrm((N_ODD, 3, D, D), D ** -0.5),
        'od_w0': unif((N_ODD, 2, D), -6.0, -1.0),
        'od_w1': nrm((N_ODD, 2, D, DECAY_LORA), D ** -0.5),
        'od_w2': nrm((N_ODD, 2, DECAY_LORA, D), 0.1 * DECAY_LORA ** -0.5),
        'od_a0': nrm((N_ODD, 2, D), 0.1),
        'od_a1': nrm((N_ODD, 2, D, AAA_LORA), D ** -0.5),
        'od_a2': nrm((N_ODD, 2, AAA_LORA, D), AAA_LORA ** -0.5),
        'od_g1': nrm((N_ODD, D, GATE_LORA), D ** -0.5),
        'od_g2': nrm((N_ODD, GATE_LORA, D), GATE_LORA ** -0.5),
        'od_k_k': 0.85 + nrm((N_ODD, D), 0.05),
        'od_k_a': 1.0 + nrm((N_ODD, D), 0.05),
        'od_r_k': nrm((N_ODD, D), 0.1),
        'od_lnx_w': 1.0 + nrm((N_ODD, D), 0.1),
        'od_lnx_b': nrm((N_ODD, D), 0.01),
        'od_w_out': nrm((N_ODD, D, D), D ** -0.5),
    }


def reference(x, c, ctx, c_ctx, ada_w, ada_b, norm_g, mlp_w_in, mlp_w_out, ev_w_in, ev_b_gate, ev_conv_qk,
              ev_gn_ret, ev_gn_mlstm, ev_w_out, od_mu, od_w_rkv, od_w0, od_w1, od_w2, od_a0, od_a1, od_a2,
              od_g1, od_g2, od_k_k, od_k_a, od_r_k, od_lnx_w, od_lnx_b, od_w_out):
    cond_lat = jax.nn.silu(c)[:, None, :]
    cond_ctx = jax.nn.silu(c_ctx)[None, None, :]
    h_lat, h_ctx = x, ctx
    for layer in range(DEPTH):
        with_ctx = layer < DEPTH - 1
        ml = jnp.split(cond_lat @ ada_w[layer] + ada_b[layer], 6, axis=-1)
        mc = jnp.split(cond_ctx @ ada_w[layer] + ada_b[layer], 6, axis=-1)
        g_pre_mix, g_post_mix, g_pre_ff, g_post_ff = norm_g[layer, 0], norm_g[layer, 1], norm_g[layer, 2], norm_g[layer, 3]
        u_lat = modulate(h_lat, g_pre_mix, ml[0], ml[1])
        u_ctx = modulate(h_ctx, g_pre_mix, mc[0], mc[1])
        if layer % 2 == 0:
            e = layer // 2
            o_ctx, o_lat = even_mixer(u_ctx, u_lat, ev_w_in[e], ev_b_gate[e], ev_conv_qk[e], ev_gn_ret[e],
                                      ev_gn_mlstm[e], ev_w_out[e], with_ctx)
        else:
            o = layer // 2
            o_ctx, o_lat = rwkv_mixer(u_ctx, u_lat, od_mu[o], od_w_rkv[o], od_w0[o], od_w1[o], od_w2[o],
                                      od_a0[o], od_a1[o], od_a2[o], od_g1[o], od_g2[o], od_k_k[o], od_k_a[o],
                                      od_r_k[o], od_lnx_w[o], od_lnx_b[o], od_w_out[o], with_ctx)
        h_lat = h_lat + ml[2] * rms_norm(o_lat, g_post_mix)
        f_lat = squared_relu_mlp(modulate(h_lat, g_pre_ff, ml[3], ml[4]), mlp_w_in[layer], mlp_w_out[layer])
        h_lat = h_lat + ml[5] * rms_norm(f_lat, g_post_ff)
        if with_ctx:
            h_ctx = h_ctx + mc[2] * rms_norm(o_ctx, g_post_mix)
            f_ctx = squared_relu_mlp(modulate(h_ctx, g_pre_ff, mc[3], mc[4]), mlp_w_in[layer], mlp_w_out[layer])
            h_ctx = h_ctx + mc[5] * rms_norm(f_ctx, g_post_ff)
    return h_lat
```

```python
import numpy as np
from contextlib import ExitStack
import concourse.bass as bass
import concourse.mybir as mybir
from concourse.bass_utils import run_bass_kernel_spmd

F32 = mybir.dt.float32
BF16 = mybir.dt.bfloat16
ALU = mybir.AluOpType
AF = mybir.ActivationFunctionType
AX = mybir.AxisListType


class T:
    def __init__(self, name, h, space):
        self.name = name
        self.h = h
        self.space = space
        self.lastw = None
        self.readers = {}
        self.dsem = None
        self.dcount = 0
        self.dbase = 0
        self.pend_w = {}
        self.pend_r = {}
        self.untracked = False

    def __getitem__(self, idx):
        return self.h[idx]


class Prog:
    SAME_ENGINE_SYNC = ('act', 'dve', 'pool')

    def __init__(self):
        self.nc = bass.Bass("TRN2", target_bir_lowering=False)
        self.es = ExitStack()
        nc = self.nc
        self.engs = {'pe': nc.tensor, 'act': nc.scalar, 'dve': nc.vector, 'pool': nc.gpsimd, 'sp': nc.sync}
        self.sem = {e: self.es.enter_context(nc.semaphore("s_" + e)) for e in self.engs}
        self.cnt = {e: 0 for e in self.engs}
        self.seen = {e: {e2: 0 for e2 in self.engs} for e in self.engs}
        self.dseen = {e: {} for e in self.engs}
        self.n_ops = 0
        self.out_tiles = []
        self.tiles = []
        self.drams = []
        self.stage_es = None
        self.stage_tiles = []
        self.sem_pool = []
        self.uid = 0

    def dram(self, name, shape, dtype=F32, kind="ExternalInput"):
        h = self.nc.dram_tensor(name, list(shape), dtype, kind=kind)
        t = T(name, h.ap(), 'dram')
        t.kind = kind
        t.untracked = (kind != "Internal")
        if kind == "ExternalOutput":
            self.out_tiles.append(t)
        else:
            self.drams.append(t)
        return t

    def _es(self):
        return self.stage_es if self.stage_es is not None else self.es

    def sbuf(self, name, shape, dtype=F32):
        self.uid += 1
        name = f"{name}_{self.uid}"
        h = self._es().enter_context(self.nc.sbuf_tensor(name, list(shape), dtype))
        t = T(name, h, 'sbuf')
        self.tiles.append(t)
        if self.stage_es is not None:
            self.stage_tiles.append(t)
        return t

    def psum(self, name, shape, dtype=F32):
        self.uid += 1
        name = f"{name}_{self.uid}"
        h = self._es().enter_context(self.nc.psum_tensor(name, list(shape), dtype))
        t = T(name, h, 'psum')
        self.tiles.append(t)
        if self.stage_es is not None:
            self.stage_tiles.append(t)
        return t

    def begin_stage(self):
        assert self.stage_es is None
        self.stage_es = ExitStack()
        self.stage_tiles = []

    def barrier(self):
        alls = self.tiles + self.out_tiles + self.drams
        for e in self.engs:
            E = self.engs[e]
            for e2 in self.engs:
                if e2 != e and self.seen[e][e2] < self.cnt[e2]:
                    E.wait_ge(self.sem[e2], self.cnt[e2])
                    self.seen[e][e2] = self.cnt[e2]
            for t in alls:
                if t.dsem is not None and t.dcount > t.dbase:
                    E.wait_ge(t.dsem, t.dcount)
        for t in alls:
            t.lastw = None
            t.readers = {}
            t.pend_w = {}
            t.pend_r = {}
            if t.dsem is not None:
                self.sem_pool.append((t.dsem, t.dcount))
                t.dsem = None
        for e in self.engs:
            self.dseen[e] = {}

    def end_stage(self):
        self.barrier()
        st = set(self.stage_tiles)
        self.tiles = [t for t in self.tiles if t not in st]
        self.stage_es.close()
        self.stage_es = None
        self.stage_tiles = []

    def _get_dsem(self, owner):
        if owner.dsem is None:
            if self.sem_pool:
                owner.dsem, owner.dcount = self.sem_pool.pop()
            else:
                self.uid += 1
                owner.dsem = self.es.enter_context(self.nc.semaphore(f"d{self.uid}"))
                owner.dcount = 0
            owner.dbase = owner.dcount

    def _need(self, eng, reads, writes):
        cw = {}
        dw = {}

        def addc(ei):
            if ei is None:
                return
            e2, i2 = ei
            if i2 > cw.get(e2, 0):
                cw[e2] = i2

        def addd(d):
            for o, c in d.items():
                if c > dw.get(o, 0):
                    dw[o] = c
        reads = [t for t in reads if not t.untracked]
        writes = [t for t in writes if not t.untracked]
        for t in reads:
            addc(t.lastw)
            addd(t.pend_w)
        for t in writes:
            addc(t.lastw)
            for e2, i2 in t.readers.items():
                addc((e2, i2))
            addd(t.pend_w)
            addd(t.pend_r)
        return cw, dw

    def _emit_waits(self, eng, cw, dw):
        E = self.engs[eng]
        for e2, i2 in cw.items():
            if e2 == eng and eng not in self.SAME_ENGINE_SYNC:
                continue
            if self.seen[eng][e2] >= i2:
                continue
            E.wait_ge(self.sem[e2], i2)
            self.seen[eng][e2] = i2
        for o, c in dw.items():
            if self.dseen[eng].get(o, 0) >= c:
                continue
            E.wait_ge(o.dsem, c)
            self.dseen[eng][o] = c

    def op(self, eng, fn, reads=(), writes=()):
        cw, dw = self._need(eng, reads, writes)
        self._emit_waits(eng, cw, dw)
        ins = fn(self.engs[eng])
        self.cnt[eng] += 1
        idx = self.cnt[eng]
        ins.then_inc(self.sem[eng], 1)
        self.seen[eng][eng] = max(self.seen[eng][eng], 0)
        for t in writes:
            t.lastw = (eng, idx)
            t.readers = {}
            t.pend_w = {}
            t.pend_r = {}
        for t in reads:
            if t in writes:
                continue
            t.readers[eng] = idx
        self.n_ops += 1
        return ins

    def dma(self, q, out_t, out_ap, in_t, in_ap, **kw):
        return self.dma_group([(q, out_t, out_ap, in_t, in_ap)], **kw)

    def dma_group(self, items, **kw):
        needs = []
        for (q, out_t, out_ap, in_t, in_ap) in items:
            cw, dw = self._need(q, [in_t], [out_t])
            needs.append((cw, dw))
        recs = []
        for (q, out_t, out_ap, in_t, in_ap), (cw, dw) in zip(items, needs):
            self._emit_waits(q, cw, dw)
            if out_t.space != 'dram' or getattr(out_t, 'kind', '') == 'Internal':
                owner = out_t
            else:
                owner = in_t
            self._get_dsem(owner)
            ins = self.engs[q].dma_start(out=out_ap, in_=in_ap, **kw)
            owner.dcount += 16
            ins.then_inc(owner.dsem, 16)
            recs.append((out_t, in_t, owner))
            self.n_ops += 1
        wrote = set()
        for out_t, in_t, owner in recs:
            if out_t not in wrote and not out_t.untracked:
                out_t.lastw = None
                out_t.readers = {}
                out_t.pend_w = {}
                out_t.pend_r = {}
                wrote.add(out_t)
        for out_t, in_t, owner in recs:
            if not out_t.untracked:
                out_t.pend_w[owner] = owner.dcount
            if not in_t.untracked:
                in_t.pend_r[owner] = max(in_t.pend_r.get(owner, 0), owner.dcount)

    def finish(self):
        dw = {}
        for t in self.tiles + self.out_tiles + self.drams:
            if t.dsem is not None:
                dw[t] = t.dcount
        self._emit_waits('sp', {}, dw)

    def close(self):
        self.es.close()


D = 4096
KC = 32
DFF = 16384
EPS = 1e-6
NTK = 2304
NCTX = 256
GMAX = 512
GROUPS0 = [(0, 256, 1)] + [(256 + i * 512, 512, 0) for i in range(4)]
GROUPS1 = [(256 + i * 512, 512, 0) for i in range(4)]


class FCtx:
    def __init__(self, p, gmax=GMAX, n_wb=4):
        self.p = p
        self.gmax = gmax
        self.ones = p.sbuf("ones", [128, 128], F32)
        p.op('dve', lambda e: e.memset(self.ones[:, :], 1.0), writes=[self.ones])
        self.epsb = p.sbuf("epsb", [128, 1], F32)
        p.op('dve', lambda e: e.memset(self.epsb[:, :], EPS), writes=[self.epsb])
        self.wb = [p.sbuf(f"wb{i}", [128, 8192], BF16) for i in range(n_wb)]
        self.wi = 0
        self.ps = [p.psum(f"ps{i}", [128, 512], F32) for i in range(4)]
        self.pi = 0
        self.ps_r = p.psum("ps_r", [128, 512], F32)
        self.sq = [p.sbuf(f"sq{i}", [128, gmax], F32) for i in range(2)]
        self.sqi = 0
        self.rbc = p.sbuf("rbc", [128, gmax], F32)
        self.tmp = [p.sbuf(f"tmp{i}", [128, gmax], F32) for i in range(2)]
        self.tmi = 0
        self.stg = [p.sbuf(f"stg{i}", [128, 512], F32) for i in range(3)]
        self.sti = 0
        self.G = gmax

    def nxt(self, name):
        lst = getattr(self, name)
        k = name + "_i"
        i = getattr(self, k, 0)
        setattr(self, k, i + 1)
        return lst[i % len(lst)]


def f_rstd(c, src, kcs=KC):
    p, G = c.p, c.G
    for kc in range(kcs):
        sq = c.nxt("sq")
        p.op('act', lambda e: e.activation(out=sq[:, :G], in_=src[:, kc, :G], func=AF.Square), reads=[src], writes=[sq])
        p.op('pe', lambda e: e.matmul(c.ps_r[:, :G], c.ones[:, :], sq[:, :G], start=(kc == 0), stop=(kc == kcs - 1)),
             reads=[sq, c.ones], writes=[c.ps_r])
    p.op('act', lambda e: e.activation(out=c.rbc[:, :G], in_=c.ps_r[:, :G], func=AF.Sqrt, bias=c.epsb[:, 0:1], scale=1.0 / (kcs * 128)),
         reads=[c.ps_r, c.epsb], writes=[c.rbc])
    p.op('dve', lambda e: e.reciprocal(out=c.rbc[:, :G], in_=c.rbc[:, :G]), reads=[c.rbc], writes=[c.rbc])


def f_modulate(c, src, dst_fn, dst_t, gp, sh):
    p, G = c.p, c.G
    for kc in range(KC):
        tmp = c.nxt("tmp")
        p.op('dve', lambda e: e.scalar_tensor_tensor(out=tmp[:, :G], in0=src[:, kc, :G], scalar=gp[1](kc), in1=c.rbc[:, :G], op0=ALU.mult, op1=ALU.mult),
             reads=[src, gp[0], c.rbc], writes=[tmp])
        p.op('act', lambda e: e.activation(out=dst_fn(kc), in_=tmp[:, :G], func=AF.Identity, bias=sh[1](kc), scale=1.0),
             reads=[tmp, sh[0]], writes=[dst_t])


def f_resid(c, buf, a, hsrc, tok0):
    p, G = c.p, c.G
    for kc in range(KC):
        hb = c.nxt("stg")
        p.dma('sp', hb, hb[:, :G], hsrc, hsrc.h[kc * 128:(kc + 1) * 128, tok0:tok0 + G])
        tmp = c.nxt("tmp")
        p.op('dve', lambda e: e.scalar_tensor_tensor(out=tmp[:, :G], in0=buf[:, kc, :G], scalar=a[1](kc), in1=c.rbc[:, :G], op0=ALU.mult, op1=ALU.mult),
             reads=[buf, a[0], c.rbc], writes=[tmp])
        p.op('pool', lambda e: e.tensor_tensor(out=buf[:, kc, :G], in0=tmp[:, :G], in1=hb[:, :G], op=ALU.add), reads=[tmp, hb], writes=[buf])


def f_load_w(c, w_ap, nrk, ncol):
    p = c.p
    assert nrk * ncol <= 8192
    wb = c.nxt("wb")
    view = wb.h[:, 0:nrk * ncol].rearrange("p (k n) -> p k n", k=nrk)
    p.dma('pool', wb, view, w_ap[0], w_ap[1].rearrange("(k p) n -> p k n", p=128))
    return wb, view


def f_gemm_fm(c, xT, w, kcs, n0, n1, out_cb):
    p, G = c.p, c.G
    WC = min(8192 // kcs, 512)
    for c0 in range(n0, n1, WC):
        ncol = min(WC, n1 - c0)
        wb, view = f_load_w(c, (w, w.h[0:kcs * 128, c0:c0 + ncol]), kcs, ncol)
        for m0 in range(0, ncol, 128):
            msz = min(128, ncol - m0)
            ps = c.nxt("ps")
            for kc in range(kcs):
                p.op('pe', lambda e: e.matmul(ps[:msz, :G], view[:, kc, m0:m0 + msz], xT[:, kc, :G], start=(kc == 0), stop=(kc == kcs - 1)),
                     reads=[wb, xT], writes=[ps])
            out_cb(ps, (c0 + m0) // 128, msz)


def f_gemm_tm(c, xT, w, kcs, n0, n1, out_dram, row0, col0=0, act=None):
    p, G = c.p, c.G
    WC = min(8192 // kcs, 512)
    for c0 in range(n0, n1, WC):
        ncol = min(WC, n1 - c0)
        wb, view = f_load_w(c, (w, w.h[0:kcs * 128, c0:c0 + ncol]), kcs, ncol)
        for t0 in range(0, G, 128):
            tsz = min(128, G - t0)
            ps = c.nxt("ps")
            for kc in range(kcs):
                p.op('pe', lambda e: e.matmul(ps[:tsz, :ncol], xT[:, kc, t0:t0 + tsz], view[:, kc, :ncol], start=(kc == 0), stop=(kc == kcs - 1)),
                     reads=[wb, xT], writes=[ps])
            st = c.nxt("stg")
            if act is None:
                p.op('act', lambda e: e.copy(st[:tsz, :ncol], ps[:tsz, :ncol]), reads=[ps], writes=[st])
            else:
                p.op('act', lambda e: e.activation(out=st[:tsz, :ncol], in_=ps[:tsz, :ncol], func=act), reads=[ps], writes=[st])
            p.dma('sp', out_dram, out_dram.h[row0 + t0:row0 + t0 + tsz, col0 + c0 - n0:col0 + c0 - n0 + ncol], st, st[:tsz, :ncol])


def f_transpose_in(c, src_dram, row0, ncols, U, ident, tin):
    p, G = c.p, c.G
    nk = ncols // 128
    for ti, t0 in enumerate(range(0, G, 128)):
        tsz = min(128, G - t0)
        X = tin[ti % len(tin)]
        p.dma('sp', X, X[:tsz, :ncols], src_dram, src_dram.h[row0 + t0:row0 + t0 + tsz, 0:ncols])
        for k4 in range(0, nk, 4):
            ps = c.nxt("ps")
            for j in range(4):
                kc = k4 + j
                p.op('pe', lambda e: e.transpose(ps[:, j * 128:j * 128 + tsz], X[:tsz, kc * 128:(kc + 1) * 128], ident[:tsz, :tsz]),
                     reads=[X, ident], writes=[ps])
            eng = 'act' if (k4 // 4) % 2 == 0 else 'dve'
            src = ps.h[:, :].rearrange("p (j t) -> p j t", j=4)[:, :, :tsz]
            if eng == 'act':
                p.op('act', lambda e: e.copy(U[:, k4:k4 + 4, t0:t0 + tsz], src), reads=[ps], writes=[U])
            else:
                p.op('dve', lambda e: e.tensor_copy(U[:, k4:k4 + 4, t0:t0 + tsz], src), reads=[ps], writes=[U])


def f_mlp(c, U, acc, w1, w2, dff, hT, rl):
    p, G = c.p, c.G
    HG = 256
    for gi, h0 in enumerate(range(0, dff, HG)):
        wb1, v1 = f_load_w(c, (w1, w1.h[:, h0:h0 + HG]), KC, HG)
        ht = hT[gi % 2]
        for s in range(2):
            ps = c.nxt("ps")
            for kc in range(KC):
                p.op('pe', lambda e: e.matmul(ps[:, :G], v1[:, kc, s * 128:(s + 1) * 128], U[:, kc, :G], start=(kc == 0), stop=(kc == KC - 1)),
                     reads=[wb1, U], writes=[ps])
            r = rl[s]
            p.op('act', lambda e: e.activation(out=r[:, :G], in_=ps[:, :G], func=AF.Relu), reads=[ps], writes=[r])
            p.op('pool', lambda e: e.tensor_tensor(out=ht[:, s, :G], in0=r[:, :G], in1=r[:, :G], op=ALU.mult), reads=[r], writes=[ht])
        for half in range(2):
            wb2 = c.nxt("wb")
            v2 = wb2.h[:, 0:4096].rearrange("p (k n) -> p k n", k=2)
            cbase = half * 2048
            p.dma('pool', wb2, v2, w2, w2.h[h0:h0 + HG, cbase:cbase + 2048].rearrange("(k p) n -> p k n", p=128))
            for dci in range(16):
                dc = half * 16 + dci
                ps = c.nxt("ps")
                for s in range(2):
                    p.op('pe', lambda e: e.matmul(ps[:, :G], v2[:, s, dci * 128:(dci + 1) * 128], ht[:, s, :G], start=(s == 0), stop=(s == 1)),
                         reads=[wb2, ht], writes=[ps])
                if gi == 0:
                    p.op('dve', lambda e: e.tensor_copy(acc[:, dc, :G], ps[:, :G]), reads=[ps], writes=[acc])
                else:
                    p.op('dve', lambda e: e.tensor_tensor(out=acc[:, dc, :G], in0=acc[:, dc, :G], in1=ps[:, :G], op=ALU.add), reads=[ps, acc], writes=[acc])


def stage_ada(p, condT, adaw, adab, modv):
    p.begin_stage()
    cf = p.sbuf("cf", [128, KC, 2], F32)
    cb = p.sbuf("cb", [128, KC, 2], BF16)
    p.dma('sp', cf, cf[:, :, :], condT, condT.h[:, :, :])
    p.op('act', lambda e: e.activation(out=cb[:, :, :], in_=cf[:, :, :], func=AF.Silu), reads=[cf], writes=[cb])
    bs = p.sbuf("adab", [128, 2, 192], F32)
    p.dma('sp', bs, bs[:, :, :], adab, adab.h[:, :, :])
    wbs = [p.sbuf(f"awb{i}", [128, KC, 256], BF16) for i in range(3)]
    pss = [p.psum(f"aps{i}", [128, 2], F32) for i in range(4)]
    it = 0
    for l in range(2):
        for ci, c0 in enumerate(range(0, 6 * D, 256)):
            wb = wbs[ci % 3]
            p.dma('pool', wb, wb[:, :, :], adaw[l], adaw[l].h[:, c0:c0 + 256].rearrange("(k p) n -> p k n", p=128))
            for m in range(2):
                ps = pss[it % 4]
                it += 1
                for kc in range(KC):
                    p.op('pe', lambda e: e.matmul(ps[:, :], wb[:, kc, m * 128:(m + 1) * 128], cb[:, kc, :], start=(kc == 0), stop=(kc == KC - 1)),
                         reads=[wb, cb], writes=[ps])
                q = (c0 + m * 128) // 128
                i, kk = q // 32, q % 32
                p.op('dve', lambda e: e.tensor_scalar(out=modv[:, l, i, kk, :], in0=ps[:, :], scalar1=bs[:, l, q:q + 1], scalar2=0.0, op0=ALU.add, op1=ALU.add),
                     reads=[ps, bs], writes=[modv])
    p.end_stage()


def make_vecs(p, modv, ng, l, with_next=False):
    def mv(i, s):
        return (modv, lambda kc, i=i, s=s: modv[:, l, i, kc, s:s + 1])
    out = {}
    tl = {}

    def comb(name, gi, mi, plus1):
        res = []
        for s in range(2):
            t = p.sbuf(f"c_{name}{s}", [128, KC], F32)
            if plus1:
                p.op('dve', lambda e: e.scalar_tensor_tensor(out=t[:, :], in0=modv[:, l, mi, :, s], scalar=1.0, in1=ng[:, l, gi, :], op0=ALU.add, op1=ALU.mult),
                     reads=[modv, ng], writes=[t])
            else:
                p.op('dve', lambda e: e.tensor_tensor(out=t[:, :], in0=modv[:, l, mi, :, s], in1=ng[:, l, gi, :], op=ALU.mult), reads=[modv, ng], writes=[t])
            res.append((t, lambda kc, t=t: t[:, kc:kc + 1]))
        return res
    out["GP_mix"] = comb("GPm", 0, 1, True)
    out["sh_mix"] = [mv(0, s) for s in range(2)]
    out["A_mix"] = comb("Am", 1, 2, False)
    out["GP_ff"] = comb("GPf", 2, 4, True)
    out["sh_ff"] = [mv(3, s) for s in range(2)]
    out["A_ff"] = comb("Af", 3, 5, False)
    return out


def stage_inproj(p, xT, modv, ng, w_tm, w_bqk, w_gi, w_gf, S_P, S_BQK, S_GI, S_GF):
    p.begin_stage()
    c = FCtx(p)
    V = make_vecs(p, modv, ng, 0)
    B = p.sbuf("B", [128, KC, GMAX], F32)
    U = p.sbuf("U", [128, KC, GMAX], BF16)
    for (tok0, G, si) in GROUPS0:
        c.G = G
        p.dma('sp', B, B[:, :, :G], xT, xT.h[:, tok0:tok0 + G].rearrange("(k p) t -> p k t", p=128))
        f_rstd(c, B)
        f_modulate(c, B, lambda kc: U[:, kc, :G], U, V["GP_mix"][si], V["sh_mix"][si])
        f_gemm_tm(c, U, w_tm, KC, 0, 10240, S_P, tok0)

        def cb_to(dst):
            def cb(ps, fc, msz):
                st = c.nxt("stg")
                p.op('act', lambda e: e.copy(st[:msz, :G], ps[:msz, :G]), reads=[ps], writes=[st])
                p.dma('sp', dst, dst.h[fc * 128:fc * 128 + msz, tok0:tok0 + G], st, st[:msz, :G])
            return cb
        f_gemm_fm(c, U, w_bqk, KC, 0, 2048, cb_to(S_BQK))
        f_gemm_fm(c, U, w_gi, KC, 0, 16, cb_to(S_GI))
        f_gemm_fm(c, U, w_gf, KC, 0, 16, cb_to(S_GF))
    p.end_stage()


def _proc_src(dram_t, row0, nrows, rowlen, rev):
    import concourse.bass as bass
    tens = dram_t.h.tensor
    if not rev:
        return [(0, NTK, dram_t.h[row0:row0 + nrows, :])]
    a = bass.AP(tens, row0 * rowlen + (NCTX - 1), [[rowlen, nrows], [-1, NCTX]])
    b = bass.AP(tens, row0 * rowlen + (NTK - 1), [[rowlen, nrows], [-1, NTK - NCTX]])
    return [(0, NCTX, a), (NCTX, NTK - NCTX, b)]


def stage_c1(p, S_GI, S_GF, gbias, S_GD, S_GS, S_GE, S_BQK, wc_d, S_BQKtm, S_P, S_AQK, cos_d, sin_d, ident):
    p.begin_stage()
    T_ALL = NTK
    gi = p.sbuf("gi", [16, T_ALL]); gf = p.sbuf("gf", [16, T_ALL]); gb = p.sbuf("gb", [16, 2])
    p.dma('sp', gb, gb[:, :], gbias, gbias.h[:, :])
    for (tile_, src) in ((gi, S_GI), (gf, S_GF)):
        items = [('sp', tile_, tile_[0:8, :], src, src.h[0:8, :])]
        for (c0, n, ap) in _proc_src(src, 8, 8, T_ALL, True):
            items.append(('sp', tile_, tile_[8:16, c0:c0 + n], src, ap))
        p.dma_group(items, allow_slow_non_contiguous=True)
    g1 = p.sbuf("g1", [16, T_ALL]); g2 = p.sbuf("g2", [16, T_ALL]); gm = p.sbuf("gm", [16, T_ALL + 1]); g3 = p.sbuf("g3", [16, T_ALL])
    p.op('dve', lambda e: e.tensor_scalar(out=gi[:, :], in0=gi[:, :], scalar1=gb[:, 0:1], scalar2=0.0, op0=ALU.add, op1=ALU.add), reads=[gi, gb], writes=[gi])
    p.op('dve', lambda e: e.tensor_scalar(out=gf[:, :], in0=gf[:, :], scalar1=gb[:, 1:2], scalar2=0.0, op0=ALU.add, op1=ALU.add), reads=[gf, gb], writes=[gf])
    p.op('act', lambda e: e.activation(out=g1[:, :], in_=gf[:, :], func=AF.Abs), reads=[gf], writes=[g1])
    p.op('act', lambda e: e.activation(out=g1[:, :], in_=g1[:, :], func=AF.Exp, scale=-1.0), reads=[g1], writes=[g1])
    p.op('act', lambda e: e.activation(out=g1[:, :], in_=g1[:, :], func=AF.Ln, bias=1.0, scale=1.0), reads=[g1], writes=[g1])
    p.op('dve', lambda e: e.tensor_single_scalar(out=g2[:, :], in_=gf[:, :], scalar=0.0, op=ALU.min), reads=[gf], writes=[g2])
    p.op('dve', lambda e: e.tensor_tensor(out=gf[:, :], in0=g2[:, :], in1=g1[:, :], op=ALU.subtract), reads=[g1, g2], writes=[gf])
    p.op('dve', lambda e: e.memset(gm[:, 0:1], 0.0), writes=[gm])
    p.op('dve', lambda e: e.tensor_tensor_scan(out=gm[:, 1:T_ALL + 1], data0=gf[:, :], data1=gi[:, :], initial=0.0, op0=ALU.add, op1=ALU.max),
         reads=[gf, gi, gm], writes=[gm])
    p.op('dve', lambda e: e.tensor_tensor(out=g1[:, :], in0=gf[:, :], in1=gm[:, 0:T_ALL], op=ALU.add), reads=[gf, gm], writes=[g1])
    p.op('dve', lambda e: e.tensor_tensor(out=g1[:, :], in0=g1[:, :], in1=gm[:, 1:T_ALL + 1], op=ALU.subtract), reads=[g1, gm], writes=[g1])
    p.op('act', lambda e: e.activation(out=g1[:, :], in_=g1[:, :], func=AF.Exp), reads=[g1], writes=[g1])
    p.dma('sp', S_GD, S_GD.h[:, :], g1, g1[:, :])
    p.op('dve', lambda e: e.tensor_tensor(out=g2[:, :], in0=gi[:, :], in1=gm[:, 1:T_ALL + 1], op=ALU.subtract), reads=[gi, gm], writes=[g2])
    p.op('act', lambda e: e.activation(out=g2[:, :], in_=g2[:, :], func=AF.Exp), reads=[g2], writes=[g2])
    p.op('act', lambda e: e.mul(g2[:, :], g2[:, :], 128.0 ** -0.5), reads=[g2], writes=[g2])
    p.dma('sp', S_GS, S_GS.h[:, :], g2, g2[:, :])
    p.op('act', lambda e: e.activation(out=g3[:, :], in_=gm[:, 1:T_ALL + 1], func=AF.Exp, scale=-1.0), reads=[gm], writes=[g3])
    items = [('sp', S_GE, S_GE.h[0:8, :], g3, g3[0:8, :])]
    for (c0, n, ap) in _proc_src(S_GE, 8, 8, T_ALL, True):
        items.append(('sp', S_GE, ap, g3, g3[8:16, c0:c0 + n]))
    p.dma_group(items, allow_slow_non_contiguous=True)

    wc = p.sbuf("wcs", [128, 16, 9])
    p.dma('sp', wc, wc[:, :, :], wc_d, wc_d.h[:, :, :])
    idt = p.sbuf("idt", [128, 128])
    p.dma('sp', idt, idt[:, :], ident, ident.h[:, :])
    xs = [p.sbuf(f"cx{i}", [128, T_ALL]) for i in range(2)]
    ys = [p.sbuf(f"cy{i}", [128, T_ALL]) for i in range(2)]
    pst = [p.psum(f"tps{i}", [128, 512], F32) for i in range(2)]
    tst = [p.sbuf(f"tst{i}", [128, 512]) for i in range(2)]
    CTX = NCTX
    tcnt = 0
    for ck in range(16):
        X = xs[ck % 2]; Y = ys[ck % 2]
        p.dma('sp', X, X[:, :], S_BQK, S_BQK.h[ck * 128:(ck + 1) * 128, :])
        wt = lambda i, j: wc[:, ck, i * 3 + j:i * 3 + j + 1]
        p.op('dve', lambda e: e.tensor_scalar(out=Y[:, :], in0=X[:, :], scalar1=wt(1, 1), scalar2=0.0, op0=ALU.mult, op1=ALU.add), reads=[X, wc], writes=[Y])
        p.op('dve', lambda e: e.scalar_tensor_tensor(out=Y[:, 1:CTX], in0=X[:, 0:CTX - 1], scalar=wt(1, 0), in1=Y[:, 1:CTX], op0=ALU.mult, op1=ALU.add),
             reads=[X, wc, Y], writes=[Y])
        p.op('dve', lambda e: e.scalar_tensor_tensor(out=Y[:, 0:CTX - 1], in0=X[:, 1:CTX], scalar=wt(1, 2), in1=Y[:, 0:CTX - 1], op0=ALU.mult, op1=ALU.add),
             reads=[X, wc, Y], writes=[Y])
        Xg = X.h[:, CTX:T_ALL].rearrange("p (r c) -> p r c", c=64)
        Yg = Y.h[:, CTX:T_ALL].rearrange("p (r c) -> p r c", c=64)
        for i in range(3):
            for j in range(3):
                if i == 1 and j == 1:
                    continue
                dr, dc = i - 1, j - 1
                r0, r1 = max(0, -dr), 32 - max(0, dr)
                c0, c1 = max(0, -dc), 64 - max(0, dc)
                p.op('dve', lambda e: e.scalar_tensor_tensor(out=Yg[:, r0:r1, c0:c1], in0=Xg[:, r0 + dr:r1 + dr, c0 + dc:c1 + dc], scalar=wt(i, j),
                                                             in1=Yg[:, r0:r1, c0:c1], op0=ALU.mult, op1=ALU.add), reads=[X, wc, Y], writes=[Y])
        p.op('act', lambda e: e.activation(out=Y[:, :], in_=Y[:, :], func=AF.Silu), reads=[Y], writes=[Y])
        for t4 in range(0, 18, 4):
            nt = min(4, 18 - t4)
            ps = pst[tcnt % 2]; st = tst[tcnt % 2]; tcnt += 1
            for j in range(nt):
                tt = t4 + j
                p.op('pe', lambda e: e.transpose(ps[:, j * 128:(j + 1) * 128], Y[:, tt * 128:(tt + 1) * 128], idt[:, :]), reads=[Y, idt], writes=[ps])
            p.op('act', lambda e: e.copy(st[:, :nt * 128], ps[:, :nt * 128]), reads=[ps], writes=[st])
            p.dma('sp', S_BQKtm, S_BQKtm.h[t4 * 128:(t4 + nt) * 128, ck * 128:(ck + 1) * 128].rearrange("(j p) c -> p j c", p=128),
                  st, st.h[:, :nt * 128].rearrange("p (j c) -> p j c", c=128))

    p.dma('sp', S_AQK, S_AQK.h[0:NCTX, :], S_P, S_P.h[0:NCTX, 0:2048])
    rx = [p.sbuf(f"rx{i}", [128, 16, 128]) for i in range(2)]
    ro = [p.sbuf(f"ro{i}", [128, 16, 128]) for i in range(2)]
    rt = [p.sbuf(f"rt{i}", [128, 16, 128]) for i in range(2)]
    rc = [p.sbuf(f"rc{i}", [128, 128]) for i in range(2)]
    rs = [p.sbuf(f"rs{i}", [128, 128]) for i in range(2)]
    SHR = [128, 16, 32]
    for ti in range(16):
        X, O, Tm, C, S = rx[ti % 2], ro[ti % 2], rt[ti % 2], rc[ti % 2], rs[ti % 2]
        r0 = NCTX + ti * 128
        p.dma('sp', X, X.h[:, :, :].rearrange("p h d -> p (h d)"), S_P, S_P.h[r0:r0 + 128, 0:2048])
        p.dma('sp', C, C[:, :], cos_d, cos_d.h[ti]); p.dma('sp', S, S[:, :], sin_d, sin_d.h[ti])
        p.op('dve', lambda e: e.tensor_tensor(out=O[:, :, :], in0=X[:, :, :], in1=C[:, :].unsqueeze(1).to_broadcast([128, 16, 128]), op=ALU.mult),
             reads=[X, C], writes=[O])
        for (dst, src) in ((0, 32), (32, 0), (64, 96), (96, 64)):
            p.op('pool', lambda e: e.tensor_tensor(out=Tm[:, :, dst:dst + 32], in0=X[:, :, src:src + 32],
                                                   in1=S[:, dst:dst + 32].unsqueeze(1).to_broadcast(SHR), op=ALU.mult), reads=[X, S], writes=[Tm])
        p.op('dve', lambda e: e.tensor_tensor(out=O[:, :, :], in0=O[:, :, :], in1=Tm[:, :, :], op=ALU.add), reads=[O, Tm], writes=[O])
        p.dma('sp', S_AQK, S_AQK.h[r0:r0 + 128, :], O, O.h[:, :, :].rearrange("p h d -> p (h d)"))
    p.end_stage()


def _chunk_rows(ci, TC, rev):
    tau0 = ci * TC
    if not rev:
        return tau0, 1
    if tau0 < NCTX:
        return NCTX - 1 - tau0, -1
    return NTK - 1 - (tau0 - NCTX), -1


def _rows_ap(dram_t, rowlen, t_start, sgn, TC, col0, ncols, n_outer=None, outer_stride=None):
    import concourse.bass as bass
    dims = []
    if n_outer is not None:
        dims.append([outer_stride, n_outer])
    dims.append([sgn * rowlen, TC])
    dims.append([1, ncols])
    return bass.AP(dram_t.h.tensor, t_start * rowlen + col0, dims)


def stage_scan(p, T, TC, NS, VB, NVL, NK, delta, rev, load_in, load_v, store_y, Dsrc=None, Ssrc=None):
    p.begin_stage()
    NIN = 5 if delta else 2
    SH = [128, NVL, NK]
    Sb = [p.sbuf(f"S{i}", SH, F32) for i in range(2)]
    T3 = [p.sbuf(f"T3{i}", SH, F32) for i in range(2)]
    T4 = p.sbuf("T4", SH, F32)
    if delta:
        T1 = p.sbuf("T1", SH, F32)
        T2 = p.sbuf("T2", SH, F32)
        sa = p.sbuf("sa", [128, NVL], F32)
    INc = [p.sbuf(f"IN{i}", [128, TC, NIN, NK], F32) for i in range(2)]
    Vc = [p.sbuf(f"V{i}", [128, TC, NVL], F32) for i in range(2)]
    Yc = [p.sbuf(f"Y{i}", [128, TC, NVL], F32) for i in range(2)]
    if not delta:
        Dc = p.sbuf("Dall", [128, T], F32)
        Sc = p.sbuf("Sall", [128, T], F32)
        Dsrc(Dc)
        Ssrc(Sc)
    p.op('dve', lambda e: e.memset(Sb[0][:, :, :], 0.0), writes=[Sb[0]])
    nch = T // TC

    def load(ci):
        load_in(INc[ci % 2], ci)
        load_v(Vc[ci % 2], ci)
        if not delta:
            Vl = Vc[ci % 2]
            p.op('dve', lambda e: e.tensor_tensor(out=Vl[:, :, :], in0=Vl[:, :, :], in1=Sc[:, ci * TC:(ci + 1) * TC].unsqueeze(2).to_broadcast([128, TC, NVL]),
                                                  op=ALU.mult), reads=[Vl, Sc], writes=[Vl])
    load(0)
    KI = 2 if delta else 1
    T4b = [T4, p.sbuf("T4b", SH, F32)]
    pend = None

    def emit_t3(tt):
        if tt >= T:
            return
        c2, l2 = tt // TC, tt % TC
        IN2, V2 = INc[c2 % 2], Vc[c2 % 2]
        tgt = T3[tt % 2]
        p.op('pool', lambda e: e.tensor_tensor(out=tgt[:, :, :], in0=V2[:, l2, :].unsqueeze(2).to_broadcast(SH),
                                               in1=IN2[:, l2, KI, :].unsqueeze(1).to_broadcast(SH), op=ALU.mult), reads=[V2, IN2], writes=[tgt])

    def pool_prev_t4():
        if pend is not None and pend[4] is not None:
            pend[4]()
            pend[4] = None

    def dve_prev_y():
        nonlocal pend
        if pend is not None:
            t4p, Yp, tlp, st_ci, _ = pend
            p.op('dve', lambda e: e.tensor_reduce(out=Yp[:, tlp, :], in_=t4p[:, :, :], axis=AX.X, op=ALU.add), reads=[t4p], writes=[Yp])
            if st_ci is not None:
                store_y(Yp, st_ci)
            pend = None
    for ci in range(nch):
        pool_prev_t4()
        if ci + 1 < nch:
            load(ci + 1)
        IN, V, Y = INc[ci % 2], Vc[ci % 2], Yc[ci % 2]
        for tl in range(TC):
            t = ci * TC + tl
            So, Sn = Sb[t % 2], Sb[(t + 1) % 2]
            t3 = T3[t % 2]
            t4 = T4b[t % 2]
            bc = lambda j, IN=IN, tl=tl: IN[:, tl, j, :].unsqueeze(1).to_broadcast(SH)
            if delta:
                if t == 0:
                    emit_t3(0)
                p.op('dve', lambda e: e.tensor_tensor(out=T1[:, :, :], in0=So[:, :, :], in1=bc(3), op=ALU.mult), reads=[So, IN], writes=[T1])
                p.op('pool', lambda e: e.tensor_tensor(out=Sn[:, :, :], in0=So[:, :, :], in1=bc(1), op=ALU.mult), reads=[So, IN], writes=[Sn])
                pool_prev_t4()
                emit_t3(t + 1)
                p.op('dve', lambda e: e.tensor_reduce(out=sa[:, :], in_=T1[:, :, :], axis=AX.X, op=ALU.add), reads=[T1], writes=[sa])
                p.op('dve', lambda e: e.tensor_tensor(out=T2[:, :, :], in0=sa[:, :].unsqueeze(2).to_broadcast(SH), in1=bc(4), op=ALU.mult),
                     reads=[sa, IN], writes=[T2])
                p.op('dve', lambda e: e.tensor_tensor(out=Sn[:, :, :], in0=Sn[:, :, :], in1=T2[:, :, :], op=ALU.add), reads=[Sn, T2], writes=[Sn])
                p.op('dve', lambda e: e.tensor_tensor(out=Sn[:, :, :], in0=Sn[:, :, :], in1=t3[:, :, :], op=ALU.add), reads=[Sn, t3], writes=[Sn])
                dve_prev_y()

                def emit_t4(Sn=Sn, t4=t4, IN=IN, tl=tl):
                    p.op('pool', lambda e: e.tensor_tensor(out=t4[:, :, :], in0=Sn[:, :, :], in1=IN[:, tl, 0, :].unsqueeze(1).to_broadcast(SH), op=ALU.mult),
                         reads=[Sn, IN], writes=[t4])
                pend = [t4, Y, tl, ci if tl == TC - 1 else None, emit_t4]
            else:
                if t == 0:
                    emit_t3(0)
                emit_t3(t + 1)
                p.op('dve', lambda e: e.scalar_tensor_tensor(out=Sn[:, :, :], in0=So[:, :, :], scalar=Dc[:, t:t + 1], in1=t3[:, :, :],
                                                             op0=ALU.mult, op1=ALU.add), reads=[So, Dc, t3], writes=[Sn])
                p.op('dve', lambda e: e.tensor_tensor(out=t4[:, :, :], in0=Sn[:, :, :], in1=bc(0), op=ALU.mult), reads=[Sn, IN], writes=[t4])
                dve_prev_y()
                pend = [t4, Y, tl, ci if tl == TC - 1 else None, None]
    pool_prev_t4()
    dve_prev_y()
    p.end_stage()


def scans_l0(p, S_AQK, S_BQKtm, S_P, S_GD, S_GS, ret_ds, S_Y0):
    TC, NS, VB, NVL, NK = 32, 16, 8, 33, 128
    for d in range(2):
        rev = (d == 1)

        def load_in(IN, ci, rev=rev):
            t0, sg = _chunk_rows(ci, TC, rev)
            items = []
            for vb in range(VB):
                for j in range(2):
                    items.append(('sp', IN, IN[vb * NS:vb * NS + 8, :, j, :], S_AQK, _rows_ap(S_AQK, 2048, t0, sg, TC, j * 1024, 128, 8, 128)))
                    items.append(('sp', IN, IN[vb * NS + 8:vb * NS + 16, :, j, :], S_BQKtm, _rows_ap(S_BQKtm, 2048, t0, sg, TC, j * 1024, 128, 8, 128)))
            p.dma_group(items)

        def load_v(V, ci, rev=rev):
            t0, sg = _chunk_rows(ci, TC, rev)
            p.op('pool', lambda e: e.memset(V[:, :, 25:33], 0.0), writes=[V])
            p.op('pool', lambda e: e.memset(V[:, :, 25:26], 1.0), writes=[V])
            items = []
            for vb in range(VB):
                n = 33 if vb < 7 else 25
                items.append(('sp', V, V[vb * NS:vb * NS + 8, :, 0:n], S_P, _rows_ap(S_P, 10240, t0, sg, TC, 2048 + vb * 33, n, 8, 256)))
                items.append(('sp', V, V[vb * NS + 8:vb * NS + 16, :, 0:n], S_P, _rows_ap(S_P, 10240, t0, sg, TC, 6144 + vb * 33, n, 8, 256)))
            p.dma_group(items)

        def store_y(Y, ci, rev=rev, d=d):
            t0, sg = _chunk_rows(ci, TC, rev)
            items = []
            for vb in range(VB):
                items.append(('sp', S_Y0, _rows_ap(S_Y0, 16 * 264, d * NTK + t0, sg, TC, vb * 33, 33, 16, 264), Y, Y[vb * NS:(vb + 1) * NS, :, :]))
            p.dma_group(items)

        def Dsrc(Dc, d=d):
            items = []
            for vb in range(VB):
                items.append(('sp', Dc, Dc[vb * NS:vb * NS + 8, :], ret_ds, ret_ds.h[d, 0, :, :]))
                items.append(('sp', Dc, Dc[vb * NS + 8:vb * NS + 16, :], S_GD, S_GD.h[d * 8:(d + 1) * 8, :]))
            p.dma_group(items)

        def Ssrc(Sc, d=d):
            items = []
            for vb in range(VB):
                items.append(('sp', Sc, Sc[vb * NS:vb * NS + 8, :], ret_ds, ret_ds.h[d, 1, :, :]))
                items.append(('sp', Sc, Sc[vb * NS + 8:vb * NS + 16, :], S_GS, S_GS.h[d * 8:(d + 1) * 8, :]))
            p.dma_group(items)
        stage_scan(p, NTK, TC, NS, VB, NVL, NK, False, rev, load_in, load_v, store_y, Dsrc, Ssrc)


def scans_l1(p, S_R, S_W, S_KD, S_NKK, S_Bb, S_V, S_Y1):
    TC, NS, VB, NVL, NK = 32, 64, 2, 32, 64
    for d in range(2):
        rev = (d == 1)
        srcs = [S_R, S_W[d], S_KD[d], S_NKK, S_Bb[d]]

        def load_in(IN, ci, rev=rev, srcs=srcs):
            t0, sg = _chunk_rows(ci, TC, rev)
            items = []
            for vb in range(VB):
                for j in range(5):
                    items.append(('sp', IN, IN[vb * NS:(vb + 1) * NS, :, j, :], srcs[j], _rows_ap(srcs[j], D, t0, sg, TC, 0, 64, 64, 64)))
            p.dma_group(items)

        def load_v(V, ci, rev=rev):
            t0, sg = _chunk_rows(ci, TC, rev)
            p.dma_group([('sp', V, V[vb * NS:(vb + 1) * NS, :, :], S_V, _rows_ap(S_V, D, t0, sg, TC, vb * 32, 32, 64, 64)) for vb in range(VB)])

        def store_y(Y, ci, rev=rev, d=d):
            t0, sg = _chunk_rows(ci, TC, rev)
            p.dma_group([('sp', S_Y1, _rows_ap(S_Y1, D, d * NTK + t0, sg, TC, vb * 32, 32, 64, 64), Y, Y[vb * NS:(vb + 1) * NS, :, :]) for vb in range(VB)])
        stage_scan(p, NTK, TC, NS, VB, NVL, NK, True, rev, load_in, load_v, store_y)


def tm_head_rms(p, X, H, DH, sq, ss, eps_t):
    p.op('pool', lambda e: e.tensor_tensor(out=sq[:, :, :], in0=X[:, :, :], in1=X[:, :, :], op=ALU.mult), reads=[X], writes=[sq])
    p.op('dve', lambda e: e.tensor_reduce(out=ss[:, :], in_=sq[:, :, :], axis=AX.X, op=ALU.add), reads=[sq], writes=[ss])
    p.op('act', lambda e: e.activation(out=ss[:, :], in_=ss[:, :], func=AF.Sqrt, bias=eps_t[:, 0:1], scale=1.0 / DH), reads=[ss, eps_t], writes=[ss])
    p.op('dve', lambda e: e.reciprocal(out=ss[:, :], in_=ss[:, :]), reads=[ss], writes=[ss])
    p.op('dve', lambda e: e.tensor_tensor(out=X[:, :, :], in0=X[:, :, :], in1=ss[:, :].unsqueeze(2).to_broadcast([128, H, DH]), op=ALU.mult),
         reads=[X, ss], writes=[X])


def stage_c3(p, S_Y0, S_GE, S_P, gr_d, gm_d, S_MIX):
    import concourse.bass as bass
    p.begin_stage()
    gr = p.sbuf("gr", [128, 2048]); gm = p.sbuf("gmm", [128, 2048])
    p.dma('sp', gr, gr[:, :], gr_d, gr_d.h[:, :]); p.dma('sp', gm, gm[:, :], gm_d, gm_d.h[:, :])
    eps_t = p.sbuf("eps", [128, 1]); p.op('dve', lambda e: e.memset(eps_t[:, :], 1e-6), writes=[eps_t])
    Yt = [[p.sbuf(f"Yt{i}{d}", [128, 16, 264]) for d in range(2)] for i in range(2)]
    EM = [[p.sbuf(f"EM{i}{d}", [128, 8]) for d in range(2)] for i in range(2)]
    A = p.sbuf("A", [128, 8, 256]); Bt = p.sbuf("Bt", [128, 8, 256])
    Gt = [p.sbuf(f"G{i}", [128, 2048]) for i in range(2)]
    O = [p.sbuf(f"O{i}", [128, 4096]) for i in range(2)]
    sq = p.sbuf("sq", [128, 8, 256]); ss = p.sbuf("ss", [128, 8]); dn = p.sbuf("dn", [128, 8])
    af = A.h[:, :, :].rearrange("p h d -> p (h d)")
    for ti in range(18):
        r0 = ti * 128
        o, g = O[ti % 2], Gt[ti % 2]
        ys = Yt[ti % 2]
        for d in range(2):
            p.dma('sp', ys[d], ys[d].h[:, :, :].rearrange("p s c -> p (s c)"), S_Y0, S_Y0.h[d * NTK + r0:d * NTK + r0 + 128, :])
            em = EM[ti % 2][d]
            src = bass.AP(S_GE.h.tensor, d * 8 * NTK + r0, [[1, 128], [NTK, 8]])
            p.dma('sp', em, em[:, :], S_GE, src, allow_slow_non_contiguous=True)
        p.op('dve', lambda e: e.tensor_tensor(out=A[:, :, :], in0=ys[0][:, 0:8, 0:256], in1=ys[1][:, 0:8, 0:256], op=ALU.add), reads=[ys[0], ys[1]], writes=[A])
        tm_head_rms(p, A, 8, 256, sq, ss, eps_t)
        p.dma('sp', g, g[:, :], S_P, S_P.h[r0:r0 + 128, 4096:6144])
        p.op('act', lambda e: e.activation(out=g[:, :], in_=g[:, :], func=AF.Silu), reads=[g], writes=[g])
        p.op('dve', lambda e: e.tensor_tensor(out=af, in0=af, in1=gr[:, :], op=ALU.mult), reads=[A, gr], writes=[A])
        p.op('dve', lambda e: e.tensor_tensor(out=o[:, 0:2048], in0=af, in1=g[:, :], op=ALU.mult), reads=[A, g], writes=[o])
        for d in range(2):
            m, em = ys[d], EM[ti % 2][d]
            p.op('act', lambda e: e.activation(out=dn[:, :], in_=m[:, 8:16, 256], func=AF.Abs), reads=[m], writes=[dn])
            p.op('dve', lambda e: e.tensor_tensor(out=dn[:, :], in0=dn[:, :], in1=em[:, :], op=ALU.max), reads=[dn, em], writes=[dn])
            p.op('dve', lambda e: e.reciprocal(out=dn[:, :], in_=dn[:, :]), reads=[dn], writes=[dn])
            tgt = A if d == 0 else Bt
            p.op('dve', lambda e: e.tensor_tensor(out=tgt[:, :, :], in0=m[:, 8:16, 0:256], in1=dn[:, :].unsqueeze(2).to_broadcast([128, 8, 256]), op=ALU.mult),
                 reads=[m, dn], writes=[tgt])
        p.op('dve', lambda e: e.tensor_tensor(out=A[:, :, :], in0=A[:, :, :], in1=Bt[:, :, :], op=ALU.add), reads=[A, Bt], writes=[A])
        tm_head_rms(p, A, 8, 256, sq, ss, eps_t)
        g2 = Gt[(ti + 1) % 2]
        p.dma('sp', g2, g2[:, :], S_P, S_P.h[r0:r0 + 128, 8192:10240])
        p.op('act', lambda e: e.activation(out=g2[:, :], in_=g2[:, :], func=AF.Sigmoid), reads=[g2], writes=[g2])
        p.op('dve', lambda e: e.tensor_tensor(out=af, in0=af, in1=gm[:, :], op=ALU.mult), reads=[A, gm], writes=[A])
        p.op('dve', lambda e: e.tensor_tensor(out=o[:, 2048:4096], in0=af, in1=g2[:, :], op=ALU.mult), reads=[A, g2], writes=[o])
        p.dma('sp', S_MIX, S_MIX.h[r0:r0 + 128, :], o, o[:, :])
    p.end_stage()


def stage_post(p, l, groups, S_MIX, h_src, modv, ng, w_out, w1, w2, ident, hmid, h_out, h_out_tok0, u1_out=None):
    p.begin_stage()
    c = FCtx(p)
    V = make_vecs(p, modv, ng, l)
    if u1_out is not None:
        V1 = make_vecs(p, modv, ng, l + 1)
    idt = p.sbuf("idt", [128, 128])
    p.dma('sp', idt, idt[:, :], ident, ident.h[:, :])
    B = p.sbuf("B", [128, KC, GMAX], F32)
    U = p.sbuf("U", [128, KC, GMAX], BF16)
    hT = [p.sbuf(f"hT{i}", [128, 2, GMAX], BF16) for i in range(2)]
    rl = [p.sbuf(f"rl{i}", [128, GMAX], F32) for i in range(2)]
    for (tok0, G, si) in groups:
        c.G = G
        for ti, t0 in enumerate(range(0, G, 128)):
            tsz = min(128, G - t0)
            Xap = B.h[:, :, :].rearrange("p k g -> p (k g)")[:, ti * 4096:(ti + 1) * 4096]
            p.dma('sp', B, Xap[:tsz, :], S_MIX, S_MIX.h[tok0 + t0:tok0 + t0 + tsz, :])
            for k4 in range(0, KC, 4):
                ps = c.nxt("ps")
                for j in range(4):
                    kc = k4 + j
                    p.op('pe', lambda e: e.transpose(ps[:, j * 128:j * 128 + tsz], Xap[:tsz, kc * 128:(kc + 1) * 128], idt[:tsz, :tsz]),
                         reads=[B, idt], writes=[ps])
                src = ps.h[:, :].rearrange("p (j t) -> p j t", j=4)[:, :, :tsz]
                if (k4 // 4) % 2 == 0:
                    p.op('act', lambda e: e.copy(U[:, k4:k4 + 4, t0:t0 + tsz], src), reads=[ps], writes=[U])
                else:
                    p.op('dve', lambda e: e.tensor_copy(U[:, k4:k4 + 4, t0:t0 + tsz], src), reads=[ps], writes=[U])
        def cbB(ps, fc, msz):
            p.op('act', lambda e: e.copy(B[:, fc, :G], ps[:, :G]), reads=[ps], writes=[B])
        f_gemm_fm(c, U, w_out, KC, 0, D, cbB)
        f_rstd(c, B)
        f_resid(c, B, V["A_mix"][si], h_src, tok0)
        f_rstd(c, B)
        f_modulate(c, B, lambda kc: U[:, kc, :G], U, V["GP_ff"][si], V["sh_ff"][si])
        p.dma('sp', hmid, hmid.h[:, tok0:tok0 + G].rearrange("(k p) t -> p k t", p=128), B, B[:, :, :G])
        f_mlp(c, U, B, w1, w2, DFF, hT, rl)
        f_rstd(c, B)
        f_resid(c, B, V["A_ff"][si], hmid, tok0)
        oc = tok0 - h_out_tok0
        p.dma('sp', h_out, h_out.h[:, oc:oc + G].rearrange("(k p) t -> p k t", p=128), B, B[:, :, :G])
        if u1_out is not None:
            f_rstd(c, B)
            for kc in range(KC):
                st = c.nxt("stg")
                tmp = c.nxt("tmp")
                gp, sh = V1["GP_mix"][si], V1["sh_mix"][si]
                p.op('dve', lambda e: e.scalar_tensor_tensor(out=tmp[:, :G], in0=B[:, kc, :G], scalar=gp[1](kc), in1=c.rbc[:, :G], op0=ALU.mult, op1=ALU.mult),
                     reads=[B, gp[0], c.rbc], writes=[tmp])
                p.op('act', lambda e: e.activation(out=st[:, :G], in_=tmp[:, :G], func=AF.Identity, bias=sh[1](kc), scale=1.0), reads=[tmp, sh[0]], writes=[st])
                p.dma('sp', u1_out, u1_out.h[kc * 128:(kc + 1) * 128, tok0:tok0 + G], st, st[:, :G])
    p.end_stage()


def stage_e(p, S_U1, mu_d, w_rkv, w1, w2, a1, a2, g1, g2, OUT):
    p.begin_stage()
    G = 256
    c = FCtx(p, gmax=G)
    c.G = G
    mu = p.sbuf("mu_s", [128, 6, KC])
    p.dma('sp', mu, mu[:, :, :], mu_d, mu_d.h[:, :, :])
    B1 = p.sbuf("B1", [128, KC, G], F32)
    B2 = p.sbuf("B2", [128, KC, G], F32)
    U = p.sbuf("U", [128, KC, G], BF16)
    L = p.sbuf("L", [128, 4, G], BF16)
    for gi in range(9):
        tok0 = gi * G
        p.dma('sp', B1, B1[:, :, :], S_U1, S_U1.h[:, tok0:tok0 + G].rearrange("(k p) t -> p k t", p=128))
        if gi == 0:
            p.op('act', lambda e: e.copy(B2[:, 0:16, 1:G], B1[:, 0:16, 0:G - 1]), reads=[B1], writes=[B2])
            p.op('pool', lambda e: e.memset(B2[:, 0:16, 0:1], 0.0), writes=[B2])
            p.op('act', lambda e: e.copy(B2[:, 16:32, 0:G - 1], B1[:, 16:32, 1:G]), reads=[B1], writes=[B2])
            p.op('pool', lambda e: e.memset(B2[:, 16:32, G - 1:G], 0.0), writes=[B2])
        else:
            g = gi - 1
            v4 = lambda t_, k0, k1: t_.h[:, k0:k1, :].rearrange("p k (r c) -> p k r c", c=64)
            p.op('act', lambda e: e.copy(B2[:, 0:8, 1:G], B1[:, 0:8, 0:G - 1]), reads=[B1], writes=[B2])
            p.op('pool', lambda e: e.memset(v4(B2, 0, 8)[:, :, :, 0:1], 0.0), reads=[B2], writes=[B2])
            p.op('act', lambda e: e.copy(B2[:, 8:16, 0:G - 1], B1[:, 8:16, 1:G]), reads=[B1], writes=[B2])
            p.op('pool', lambda e: e.memset(v4(B2, 8, 16)[:, :, :, 63:64], 0.0), reads=[B2], writes=[B2])
            p.op('act', lambda e: e.copy(B2[:, 16:24, 64:G], B1[:, 16:24, 0:G - 64]), reads=[B1], writes=[B2])
            if g == 0:
                p.op('pool', lambda e: e.memset(B2[:, 16:24, 0:64], 0.0), reads=[B2], writes=[B2])
            else:
                p.dma('sp', B2, B2[:, 16:24, 0:64], S_U1, S_U1.h[16 * 128:24 * 128, tok0 - 64:tok0].rearrange("(k p) t -> p k t", p=128))
            p.op('act', lambda e: e.copy(B2[:, 24:32, 0:G - 64], B1[:, 24:32, 64:G]), reads=[B1], writes=[B2])
            if g == 7:
                p.op('pool', lambda e: e.memset(B2[:, 24:32, G - 64:G], 0.0), reads=[B2], writes=[B2])
            else:
                p.dma('sp', B2, B2[:, 24:32, G - 64:G], S_U1, S_U1.h[24 * 128:32 * 128, tok0 + G:tok0 + G + 64].rearrange("(k p) t -> p k t", p=128))
        p.op('dve', lambda e: e.tensor_tensor(out=B2[:, :, :], in0=B2[:, :, :], in1=B1[:, :, :], op=ALU.subtract), reads=[B1, B2], writes=[B2])

        def mix(i):
            for kc in range(KC):
                eng = 'dve'
                p.op(eng, lambda e: e.scalar_tensor_tensor(out=U[:, kc, :], in0=B2[:, kc, :], scalar=mu[:, i, kc:kc + 1], in1=B1[:, kc, :],
                                                           op0=ALU.mult, op1=ALU.add), reads=[B1, B2, mu], writes=[U])

        def lora_cb(func):
            def cb(ps, fc, msz):
                p.op('act', lambda e: e.activation(out=L[:msz, fc, :], in_=ps[:msz, :G], func=func), reads=[ps], writes=[L])
            return cb
        mix(0); f_gemm_tm(c, U, w_rkv[0], KC, 0, D, OUT["r"], tok0)
        mix(1)
        for d in range(2):
            f_gemm_fm(c, U, w1[d], KC, 0, 128, lora_cb(AF.Tanh))
            f_gemm_tm(c, L, w2[d], 1, 0, D, OUT[f"wl{d}"], tok0)
        mix(2); f_gemm_tm(c, U, w_rkv[1], KC, 0, D, OUT["k"], tok0)
        mix(3); f_gemm_tm(c, U, w_rkv[2], KC, 0, D, OUT["v"], tok0)
        mix(4)
        for d in range(2):
            f_gemm_fm(c, U, a1[d], KC, 0, 128, lora_cb(AF.Identity))
            f_gemm_tm(c, L, a2[d], 1, 0, D, OUT[f"al{d}"], tok0)
        mix(5)
        f_gemm_fm(c, U, g1, KC, 0, 512, lora_cb(AF.Sigmoid))
        f_gemm_tm(c, L, g2, 4, 0, D, OUT["gate"], tok0)
    p.end_stage()


def stage_e2(p, INS, vec_d, OUTS):
    p.begin_stage()
    names_in = ["r", "k", "v", "wl0", "wl1", "al0", "al1"]
    names_out = ["w0o", "w1o", "kd0", "kd1", "nkk", "b0", "b1", "bonus"]
    HW, NH = 1024, 16
    vt = [p.sbuf(f"vec{i}", [128, HW]) for i in range(7)]
    tin = {n: p.sbuf("i_" + n, [128, HW]) for n in names_in}
    tout = {n: p.sbuf("o_" + n, [128, HW]) for n in names_out}
    a_t = [p.sbuf(f"a{d}", [128, HW]) for d in range(2)]
    kk = p.sbuf("kk", [128, HW]); sq = p.sbuf("sqq", [128, HW]); ss = p.sbuf("ss", [128, NH]); tmp = p.sbuf("tmpp", [128, HW])
    C_DEC = -float(np.exp(-0.5))
    for part in range(4):
        c0 = part * HW
        for i in range(7):
            p.dma('sp', vt[i], vt[i][:, :], vec_d, vec_d.h[i, :, c0:c0 + HW])
        k_k, k_a, r_k, w0, a0 = vt[0], vt[1], vt[2], [vt[3], vt[4]], [vt[5], vt[6]]
        for ti in range(18):
            r0 = ti * 128
            for n in names_in:
                p.dma('sp', tin[n], tin[n][:, :], INS[n], INS[n].h[r0:r0 + 128, c0:c0 + HW])
            k = tin["k"]
            p.op('dve', lambda e: e.tensor_tensor(out=kk[:, :], in0=k[:, :], in1=k_k[:, :], op=ALU.mult), reads=[k, k_k], writes=[kk])
            p.op('pool', lambda e: e.tensor_tensor(out=sq[:, :], in0=kk[:, :], in1=kk[:, :], op=ALU.mult), reads=[kk], writes=[sq])
            p.op('dve', lambda e: e.tensor_reduce(out=ss[:, :], in_=sq.h[:, :].rearrange("p (h n) -> p h n", n=64), axis=AX.X, op=ALU.add), reads=[sq], writes=[ss])
            p.op('dve', lambda e: e.tensor_single_scalar(out=ss[:, :], in_=ss[:, :], scalar=1e-24, op=ALU.max), reads=[ss], writes=[ss])
            p.op('act', lambda e: e.activation(out=ss[:, :], in_=ss[:, :], func=AF.Sqrt), reads=[ss], writes=[ss])
            p.op('dve', lambda e: e.reciprocal(out=ss[:, :], in_=ss[:, :]), reads=[ss], writes=[ss])
            kk3 = kk.h[:, :].rearrange("p (h n) -> p h n", n=64)
            p.op('dve', lambda e: e.tensor_tensor(out=kk3, in0=kk3, in1=ss[:, :].unsqueeze(2).to_broadcast([128, NH, 64]), op=ALU.mult), reads=[kk, ss], writes=[kk])
            p.op('act', lambda e: e.mul(tout["nkk"][:, :], kk[:, :], -1.0), reads=[kk], writes=[tout["nkk"]])
            p.dma('sp', OUTS["nkk"], OUTS["nkk"].h[r0:r0 + 128, c0:c0 + HW], tout["nkk"], tout["nkk"][:, :])
            for d in range(2):
                wl, al = tin[f"wl{d}"], tin[f"al{d}"]
                wo, kd, bo = tout[f"w{d}o"], tout[f"kd{d}"], tout[f"b{d}"]
                p.op('dve', lambda e: e.tensor_tensor(out=wl[:, :], in0=wl[:, :], in1=w0[d][:, :], op=ALU.add), reads=[wl, w0[d]], writes=[wl])
                p.op('act', lambda e: e.activation(out=wl[:, :], in_=wl[:, :], func=AF.Sigmoid), reads=[wl], writes=[wl])
                p.op('act', lambda e: e.activation(out=wo[:, :], in_=wl[:, :], func=AF.Exp, scale=C_DEC), reads=[wl], writes=[wo])
                p.dma('sp', OUTS[f"w{d}o"], OUTS[f"w{d}o"].h[r0:r0 + 128, c0:c0 + HW], wo, wo[:, :])
                p.op('dve', lambda e: e.tensor_tensor(out=al[:, :], in0=al[:, :], in1=a0[d][:, :], op=ALU.add), reads=[al, a0[d]], writes=[al])
                p.op('act', lambda e: e.activation(out=a_t[d][:, :], in_=al[:, :], func=AF.Sigmoid), reads=[al], writes=[a_t[d]])
                p.op('dve', lambda e: e.scalar_tensor_tensor(out=tmp[:, :], in0=a_t[d][:, :], scalar=-1.0, in1=k_a[:, :], op0=ALU.add, op1=ALU.mult),
                     reads=[a_t[d], k_a], writes=[tmp])
                p.op('dve', lambda e: e.scalar_tensor_tensor(out=kd[:, :], in0=tmp[:, :], scalar=1.0, in1=k[:, :], op0=ALU.add, op1=ALU.mult),
                     reads=[tmp, k], writes=[kd])
                p.dma('sp', OUTS[f"kd{d}"], OUTS[f"kd{d}"].h[r0:r0 + 128, c0:c0 + HW], kd, kd[:, :])
                p.op('pool', lambda e: e.tensor_tensor(out=bo[:, :], in0=kk[:, :], in1=a_t[d][:, :], op=ALU.mult), reads=[kk, a_t[d]], writes=[bo])
                p.dma('sp', OUTS[f"b{d}"], OUTS[f"b{d}"].h[r0:r0 + 128, c0:c0 + HW], bo, bo[:, :])
            p.op('dve', lambda e: e.tensor_tensor(out=tmp[:, :], in0=tout["kd0"][:, :], in1=tout["kd1"][:, :], op=ALU.add), reads=[tout["kd0"], tout["kd1"]], writes=[tmp])
            p.op('dve', lambda e: e.scalar_tensor_tensor(out=tmp[:, :], in0=tmp[:, :], scalar=0.5, in1=r_k[:, :], op0=ALU.mult, op1=ALU.mult),
                 reads=[tmp, r_k], writes=[tmp])
            p.op('dve', lambda e: e.tensor_tensor(out=tmp[:, :], in0=tmp[:, :], in1=tin["r"][:, :], op=ALU.mult), reads=[tmp, tin["r"]], writes=[tmp])
            p.op('dve', lambda e: e.tensor_reduce(out=ss[:, :], in_=tmp.h[:, :].rearrange("p (h n) -> p h n", n=64), axis=AX.X, op=ALU.add), reads=[tmp], writes=[ss])
            bn = tout["bonus"]
            p.op('dve', lambda e: e.tensor_tensor(out=bn.h[:, :].rearrange("p (h n) -> p h n", n=64), in0=tin["v"].h[:, :].rearrange("p (h n) -> p h n", n=64),
                                                  in1=ss[:, :].unsqueeze(2).to_broadcast([128, NH, 64]), op=ALU.mult), reads=[tin["v"], ss], writes=[bn])
            p.dma('sp', OUTS["bonus"], OUTS["bonus"].h[r0:r0 + 128, c0:c0 + HW], bn, bn[:, :])
    p.end_stage()


def stage_g0(p, S_Y1, S_BONUS, S_GATE, vec_d, S_MIX):
    p.begin_stage()
    HW, NH = 2048, 32
    lw = p.sbuf("lw", [128, HW]); lb = p.sbuf("lb", [128, HW])
    names = ["yf", "yb", "bonus", "gate"]
    t = {n: [p.sbuf(f"g_{n}{i}", [128, HW]) for i in range(2)] for n in names}
    sq = p.sbuf("sq", [128, HW]); mean = p.sbuf("mean", [128, NH]); var = p.sbuf("var", [128, NH])
    epsl = p.sbuf("epsl", [128, 1]); p.op('dve', lambda e: e.memset(epsl[:, :], 64e-5), writes=[epsl])
    it = 0
    SH3 = [128, NH, 64]
    v3 = lambda tt: tt.h[:, :].rearrange("p (h n) -> p h n", n=64)
    for half in range(2):
        c0 = half * HW
        p.dma('sp', lw, lw[:, :], vec_d, vec_d.h[0, :, c0:c0 + HW]); p.dma('sp', lb, lb[:, :], vec_d, vec_d.h[1, :, c0:c0 + HW])
        for ti in range(16):
            r0 = NCTX + ti * 128
            cur = {n: t[n][it % 2] for n in t}; it += 1
            p.dma('sp', cur["yf"], cur["yf"][:, :], S_Y1, S_Y1.h[r0:r0 + 128, c0:c0 + HW])
            p.dma('sp', cur["yb"], cur["yb"][:, :], S_Y1, S_Y1.h[NTK + r0:NTK + r0 + 128, c0:c0 + HW])
            p.dma('sp', cur["bonus"], cur["bonus"][:, :], S_BONUS, S_BONUS.h[r0:r0 + 128, c0:c0 + HW])
            p.dma('sp', cur["gate"], cur["gate"][:, :], S_GATE, S_GATE.h[r0:r0 + 128, c0:c0 + HW])
            y = cur["yf"]
            p.op('dve', lambda e: e.tensor_tensor(out=y[:, :], in0=y[:, :], in1=cur["yb"][:, :], op=ALU.add), reads=[y, cur["yb"]], writes=[y])
            p.op('dve', lambda e: e.tensor_reduce(out=mean[:, :], in_=v3(y), axis=AX.X, op=ALU.add), reads=[y], writes=[mean])
            p.op('act', lambda e: e.mul(mean[:, :], mean[:, :], 1.0 / 64), reads=[mean], writes=[mean])
            p.op('dve', lambda e: e.tensor_tensor(out=v3(y), in0=v3(y), in1=mean[:, :].unsqueeze(2).to_broadcast(SH3), op=ALU.subtract), reads=[y, mean], writes=[y])
            p.op('pool', lambda e: e.tensor_tensor(out=sq[:, :], in0=y[:, :], in1=y[:, :], op=ALU.mult), reads=[y], writes=[sq])
            p.op('dve', lambda e: e.tensor_reduce(out=var[:, :], in_=v3(sq), axis=AX.X, op=ALU.add), reads=[sq], writes=[var])
            p.op('act', lambda e: e.activation(out=var[:, :], in_=var[:, :], func=AF.Sqrt, bias=epsl[:, 0:1], scale=1.0 / 64), reads=[var, epsl], writes=[var])
            p.op('dve', lambda e: e.reciprocal(out=var[:, :], in_=var[:, :]), reads=[var], writes=[var])
            p.op('dve', lambda e: e.tensor_tensor(out=v3(y), in0=v3(y), in1=var[:, :].unsqueeze(2).to_broadcast(SH3), op=ALU.mult), reads=[y, var], writes=[y])
            p.op('dve', lambda e: e.tensor_tensor(out=y[:, :], in0=y[:, :], in1=lw[:, :], op=ALU.mult), reads=[y, lw], writes=[y])
            p.op('pool', lambda e: e.tensor_tensor(out=y[:, :], in0=y[:, :], in1=lb[:, :], op=ALU.add), reads=[y, lb], writes=[y])
            p.op('dve', lambda e: e.tensor_tensor(out=y[:, :], in0=y[:, :], in1=cur["bonus"][:, :], op=ALU.add), reads=[y, cur["bonus"]], writes=[y])
            p.op('dve', lambda e: e.tensor_tensor(out=y[:, :], in0=y[:, :], in1=cur["gate"][:, :], op=ALU.mult), reads=[y, cur["gate"]], writes=[y])
            p.dma('sp', S_MIX, S_MIX.h[r0:r0 + 128, c0:c0 + HW], y, y[:, :])
    p.end_stage()


CHUNKED_L0 = True


def build_fused(upto=99):
    p = Prog()
    I = lambda n, s: p.dram(n, s)
    S = lambda n, s: p.dram(n, s, kind="Internal")
    xT = I("xT", [D, NTK])
    condT = I("condT", [128, KC, 2])
    adaw = [I("adaw0", [D, 6 * D]), I("adaw1", [D, 6 * D])]
    adab = I("adab", [128, 2, 192])
    ng_d = I("ng", [128, 2, 4, KC])
    w_tm = I("w_tm", [D, 10240]); w_bqk = I("w_bqk", [D, 2048]); w_gi = I("w_gi", [D, 16]); w_gf = I("w_gf", [D, 16])
    gbias = I("gbias", [16, 2]); wc_d = I("wc", [128, 16, 9])
    cos_d = I("cos", [16, 128, 128]); sin_d = I("sins", [16, 128, 128]); ident = I("ident", [128, 128])
    ret_ds = I("ret_ds", [2, 2, 8, NTK])
    retlf_d = I("retlf", [16, NTK]); masks_d = I("masks", [2, 128, 128])
    gr_d = I("gn_ret", [128, 2048]); gm_d = I("gn_ml", [128, 2048])
    w_out0 = I("w_out0", [D, D]); w1_0 = I("w1_0", [D, DFF]); w2_0 = I("w2_0", [DFF, D])
    mu_d = I("mu", [128, 6, KC])
    w_rkv = [I(f"w_rkv{i}", [D, D]) for i in range(3)]
    lw1 = [I(f"lw1_{d}", [D, 128]) for d in range(2)]; lw2 = [I(f"lw2_{d}", [128, D]) for d in range(2)]
    la1 = [I(f"la1_{d}", [D, 128]) for d in range(2)]; la2 = [I(f"la2_{d}", [128, D]) for d in range(2)]
    g1p = I("g1p", [D, 512]); g2p = I("g2p", [512, D])
    vecs_e2 = I("vecs_e2", [7, 128, D]); vecs_g0 = I("vecs_g0", [2, 128, D])
    w_out1 = I("w_out1", [D, D]); w1_1 = I("w1_1", [D, DFF]); w2_1 = I("w2_1", [DFF, D])
    outT = p.dram("outT", [D, 2048], kind="ExternalOutput")

    S_P = S("S_P", [NTK, 10240]); S_BQK = S("S_BQK", [2048, NTK]); S_GI = S("S_GI", [16, NTK]); S_GF = S("S_GF", [16, NTK])
    S_GD = S("S_GD", [16, NTK]); S_GS = S("S_GS", [16, NTK]); S_GE = S("S_GE", [16, NTK])
    SU = S("SU", [48, NTK]); SW = S("SW", [48, NTK]); SA = S("SA", [48, NTK]); SK = S("SK", [48, NTK]); SD = S("SD", [48, NCH])
    SCOL = S("SCOL", [NTK, 144])
    S_BQKtm = S("S_BQKtm", [NTK, 2048]); S_AQK = S("S_AQK", [NTK, 2048])
    S_Y0 = S("S_Y0", [2 * NTK, 16 * 264]); S_MIX = S("S_MIX", [NTK, D])
    hmid = S("hmid", [D, NTK]); S_H1 = S("S_H1", [D, NTK]); S_U1 = S("S_U1", [D, NTK])
    en = ["r", "k", "v", "wl0", "wl1", "al0", "al1", "gate"]
    EO = {n: S("S_e_" + n, [NTK, D]) for n in en}
    e2n = ["w0o", "w1o", "kd0", "kd1", "nkk", "b0", "b1", "bonus"]
    E2O = {n: S("S_e2_" + n, [NTK, D]) for n in e2n}
    S_Y1 = S("S_Y1", [2 * NTK, D])

    modv = p.sbuf("modv", [128, 2, 6, KC, 2], F32)
    ng = p.sbuf("ng_s", [128, 2, 4, KC], F32)
    p.dma('sp', ng, ng[:, :, :, :], ng_d, ng_d.h[:, :, :, :])

    dbg = {}
    stage_ada(p, condT, adaw, adab, modv)
    if upto >= 1:
        stage_inproj(p, xT, modv, ng, w_tm, w_bqk, w_gi, w_gf, S_P, S_BQK, S_GI, S_GF)
    if upto >= 2:
        stage_c1(p, S_GI, S_GF, gbias, S_GD, S_GS, S_GE, S_BQK, wc_d, S_BQKtm, S_P, S_AQK, cos_d, sin_d, ident)
    if upto >= 3:
        if CHUNKED_L0:
            stage_gates2(p, S_GI, S_GF, gbias, retlf_d, S_GE, SU, SW, SA, SK, SD, SCOL, ident)
            scans_l0_chunked(p, S_AQK, S_BQKtm, S_P, SU, SCOL, SD, masks_d, ident, S_Y0, dirs=(0,))
            scans_l0_chunked(p, S_AQK, S_BQKtm, S_P, SU, SCOL, SD, masks_d, ident, S_Y0, dirs=(1,))
        else:
            scans_l0(p, S_AQK, S_BQKtm, S_P, S_GD, S_GS, ret_ds, S_Y0)
    if upto >= 4:
        stage_c3(p, S_Y0, S_GE, S_P, gr_d, gm_d, S_MIX)
    if upto >= 5:
        stage_post(p, 0, GROUPS0, S_MIX, xT, modv, ng, w_out0, w1_0, w2_0, ident, hmid, S_H1, 0, u1_out=S_U1)
    if upto >= 6:
        stage_e(p, S_U1, mu_d, w_rkv, lw1, lw2, la1, la2, g1p, g2p, EO)
    if upto >= 7:
        stage_e2(p, EO, vecs_e2, E2O)
    if upto >= 8:
        scans_l1(p, EO["r"], [E2O["w0o"], E2O["w1o"]], [E2O["kd0"], E2O["kd1"]], E2O["nkk"], [E2O["b0"], E2O["b1"]], EO["v"], S_Y1)
    if upto >= 9:
        stage_g0(p, S_Y1, E2O["bonus"], EO["gate"], vecs_g0, S_MIX)
    if upto >= 10:
        stage_post(p, 1, GROUPS1, S_MIX, S_H1, modv, ng, w_out1, w1_1, w2_1, ident, hmid, outT, NCTX)
    if upto < 10:
        src = {0: None, 1: S_BQK, 2: S_BQK, 3: S_BQK, 4: S_MIX, 5: S_H1, 6: S_U1, 7: S_U1, 8: S_U1, 9: S_MIX}[upto]
        p.begin_stage()
        if src is not None:
            tt = p.sbuf("dbgt", [128, 2048])
            fm = (src.h.shape[0] != NTK)
            for kc in range(KC if fm else 16):
                if fm:
                    rows = min(128, src.h.shape[0] - kc * 128)
                    if rows <= 0:
                        break
                    p.dma('sp', tt, tt[:rows, :], src, src.h[kc * 128:kc * 128 + rows, 256:2304])
                else:
                    p.dma('sp', tt, tt[:, :], src, src.h[256 + kc * 128:256 + (kc + 1) * 128, 0:2048])
                p.dma('sp', outT, outT.h[kc * 128:(kc + 1) * 128, :], tt, tt[:, :])
        p.end_stage()
    p.finish()
    return p


NCH = NTK // 128


def stage_gates2(p, S_GI, S_GF, gbias, retlf_d, S_GE, SU, SW, SA, SK, SD, SCOL, ident):
    p.begin_stage()
    T_ALL = NTK
    R = 48
    gi = p.sbuf("gi", [R, T_ALL]); gf = p.sbuf("gf", [R, T_ALL]); gb = p.sbuf("gb", [16, 2])
    p.op('dve', lambda e: e.memset(gi[:, :], 0.0), writes=[gi])
    p.op('dve', lambda e: e.memset(gf[:, :], 0.0), writes=[gf])
    p.dma('sp', gb, gb[:, :], gbias, gbias.h[:, :])
    for (tile_, src) in ((gi, S_GI), (gf, S_GF)):
        items = [('sp', tile_, tile_[0:8, :], src, src.h[0:8, :])]
        for (c0, n, ap) in _proc_src(src, 8, 8, T_ALL, True):
            items.append(('sp', tile_, tile_[8:16, c0:c0 + n], src, ap))
        p.dma_group(items, allow_slow_non_contiguous=True)
    g1 = p.sbuf("g1", [R, T_ALL]); g2 = p.sbuf("g2", [R, T_ALL]); gm = p.sbuf("gm", [R, T_ALL + 1]); Fx = p.sbuf("Fx", [R, T_ALL + 1])
    bb = p.sbuf("bb", [R, T_ALL]); g3 = p.sbuf("g3", [R, T_ALL]); dl = p.sbuf("dl", [R, NCH]); zz = p.sbuf("zz", [R, T_ALL])
    p.op('dve', lambda e: e.memset(zz[:, :], 0.0), writes=[zz])
    p.op('dve', lambda e: e.tensor_scalar(out=gi[0:16, :], in0=gi[0:16, :], scalar1=gb[:, 0:1], scalar2=0.0, op0=ALU.add, op1=ALU.add), reads=[gi, gb], writes=[gi])
    p.op('dve', lambda e: e.tensor_scalar(out=gf[0:16, :], in0=gf[0:16, :], scalar1=gb[:, 1:2], scalar2=0.0, op0=ALU.add, op1=ALU.add), reads=[gf, gb], writes=[gf])
    p.op('act', lambda e: e.activation(out=g1[0:16, :], in_=gf[0:16, :], func=AF.Abs), reads=[gf], writes=[g1])
    p.op('act', lambda e: e.activation(out=g1[0:16, :], in_=g1[0:16, :], func=AF.Exp, scale=-1.0), reads=[g1], writes=[g1])
    p.op('act', lambda e: e.activation(out=g1[0:16, :], in_=g1[0:16, :], func=AF.Ln, bias=1.0, scale=1.0), reads=[g1], writes=[g1])
    p.op('dve', lambda e: e.tensor_single_scalar(out=g2[0:16, :], in_=gf[0:16, :], scalar=0.0, op=ALU.min), reads=[gf], writes=[g2])
    p.op('dve', lambda e: e.tensor_tensor(out=gf[0:16, :], in0=g2[0:16, :], in1=g1[0:16, :], op=ALU.subtract), reads=[g1, g2], writes=[gf])
    p.dma('sp', gf, gf[32:48, :], retlf_d, retlf_d.h[:, :])
    p.op('dve', lambda e: e.memset(gm[:, 0:1], 0.0), writes=[gm])
    p.op('dve', lambda e: e.tensor_tensor_scan(out=gm[:, 1:T_ALL + 1], data0=gf[:, :], data1=gi[:, :], initial=0.0, op0=ALU.add, op1=ALU.max),
         reads=[gf, gi, gm], writes=[gm])
    p.op('dve', lambda e: e.memset(Fx[:, 0:1], 0.0), writes=[Fx])
    p.op('dve', lambda e: e.tensor_tensor_scan(out=Fx[:, 1:T_ALL + 1], data0=gf[:, :], data1=zz[:, :], initial=0.0, op0=ALU.add, op1=ALU.add),
         reads=[gf, zz, Fx], writes=[Fx])
    v3 = lambda ap: ap.rearrange("p (c l) -> p c l", l=128)
    SH3 = [R, NCH, 128]
    p.op('dve', lambda e: e.tensor_tensor(out=v3(bb[:, :]), in0=v3(Fx[:, 1:T_ALL + 1]), in1=v3(Fx[:, 0:T_ALL])[:, :, 0:1].to_broadcast(SH3), op=ALU.subtract),
         reads=[Fx], writes=[bb])

    def store_nat(dst, tile_, rows48=True):
        items = [('sp', dst, dst.h[0:8, :], tile_, tile_[0:8, :])]
        for (c0, n, ap) in _proc_src(dst, 8, 8, T_ALL, True):
            items.append(('sp', dst, ap, tile_, tile_[8:16, c0:c0 + n]))
        if rows48:
            items.append(('sp', dst, dst.h[32:40, :], tile_, tile_[32:40, :]))
            for (c0, n, ap) in _proc_src(dst, 40, 8, T_ALL, True):
                items.append(('sp', dst, ap, tile_, tile_[40:48, c0:c0 + n]))
        p.dma_group(items, allow_slow_non_contiguous=True)
    m = gm[:, 1:T_ALL + 1]
    me3 = v3(gm[:, 0:T_ALL])[:, :, 0:1].to_broadcast(SH3)
    p.op('dve', lambda e: e.tensor_tensor(out=g1[:, :], in0=bb[:, :], in1=m, op=ALU.subtract), reads=[bb, gm], writes=[g1])
    store_nat(SU, g1)
    p.op('dve', lambda e: e.tensor_tensor(out=v3(g2[:, :]), in0=v3(g1[:, :]), in1=me3, op=ALU.add), reads=[g1, gm], writes=[g2])
    p.op('act', lambda e: e.activation(out=g2[:, :], in_=g2[:, :], func=AF.Exp), reads=[g2], writes=[g2])
    store_nat(SA, g2)
    p.op('dve', lambda e: e.tensor_tensor(out=g3[:, :], in0=gi[:, :], in1=bb[:, :], op=ALU.subtract), reads=[gi, bb], writes=[g3])
    store_nat(SW, g3)
    blm = p.sbuf("blm", [R, NCH])
    p.op('dve', lambda e: e.tensor_tensor(out=blm[:, :], in0=v3(bb[:, :])[:, :, 127], in1=v3(gm[:, 1:T_ALL + 1])[:, :, 127], op=ALU.subtract),
         reads=[bb, gm], writes=[blm])
    p.op('dve', lambda e: e.tensor_tensor(out=v3(zz[:, :]), in0=v3(g3[:, :]), in1=blm[:, :].unsqueeze(2).to_broadcast(SH3), op=ALU.add), reads=[g3, blm], writes=[zz])
    p.op('act', lambda e: e.activation(out=zz[:, :], in_=zz[:, :], func=AF.Exp), reads=[zz], writes=[zz])
    store_nat(SK, zz)
    p.op('dve', lambda e: e.tensor_tensor(out=dl[:, :], in0=blm[:, :], in1=v3(gm[:, 0:T_ALL])[:, :, 0], op=ALU.add), reads=[blm, gm], writes=[dl])
    p.op('act', lambda e: e.activation(out=dl[:, :], in_=dl[:, :], func=AF.Exp), reads=[dl], writes=[dl])
    p.dma('sp', SD, SD.h[:, :], dl, dl[:, :])
    em = p.sbuf("em", [R, T_ALL])
    p.op('act', lambda e: e.activation(out=em[:, :], in_=gm[:, 1:T_ALL + 1], func=AF.Exp, scale=-1.0), reads=[gm], writes=[em])
    store_nat(S_GE, em, rows48=False)
    idt = p.sbuf("idt48", [128, 128]); p.dma('sp', idt, idt[:, :], ident, ident.h[:, :])
    nat = [p.sbuf(f"nat{q}", [R, T_ALL]) for q in range(3)]
    for q, srcd in enumerate((SW, SA, SK)):
        p.dma('sp', nat[q], nat[q][:, :], srcd, srcd.h[:, :])
    pst = [p.psum(f"gps{i}", [128, 3, 48]) for i in range(2)]
    ct = [p.sbuf(f"ct{i}", [128, 3, 48]) for i in range(2)]
    for c in range(NCH):
        ps, cc = pst[c % 2], ct[c % 2]
        for q in range(3):
            p.op('pe', lambda e: e.transpose(ps[:, q, :], nat[q][:, c * 128:(c + 1) * 128], idt[:R, :R]), reads=[nat[q], idt], writes=[ps])
        p.op('act', lambda e: e.copy(cc[:, :, :], ps[:, :, :]), reads=[ps], writes=[cc])
        p.dma('sp', SCOL, SCOL.h[c * 128:(c + 1) * 128, :], cc, cc.h[:, :, :].rearrange("p q g -> p (q g)"))
    p.end_stage()


def scans_l0_chunked(p, S_AQK, S_BQKtm, S_P, SU, SCOL, SD, masks_d, ident, S_Y0, dirs=(0, 1)):
    import concourse.bass as bass
    p.begin_stage()
    idt = p.sbuf("idt", [128, 128]); p.dma('sp', idt, idt[:, :], ident, ident.h[:, :])
    mk_ = [p.sbuf(f"mask{d}", [128, 128]) for d in range(2)]
    for d in range(2):
        p.dma('sp', mk_[d], mk_[d][:, :], masks_d, masks_d.h[d])
    ones_r = p.sbuf("ones_r", [1, 128]); p.op('dve', lambda e: e.memset(ones_r[:, :], 1.0), writes=[ones_r])
    QKr = [p.sbuf(f"QKr{i}", [128, 16, 128]) for i in range(2)]
    QKm = [p.sbuf(f"QKm{i}", [128, 16, 128]) for i in range(2)]
    Vx = [p.sbuf(f"Vx{i}", [128, 16, 257]) for i in range(2)]
    Vb = [p.sbuf(f"Vb{i}", [128, 16, 257], BF16) for i in range(2)]
    for i in range(2):
        p.op('dve', lambda e: e.memset(Vx[i][:, :, 256:257], 1.0), writes=[Vx[i]])
    Yo = [p.sbuf(f"Yo{i}", [128, 16, 264]) for i in range(2)]
    for i in range(2):
        p.op('pool', lambda e: e.memset(Yo[i][:, :, :], 0.0), writes=[Yo[i]])
    Cols = [p.sbuf(f"Cols{i}", [128, 3, 48]) for i in range(2)]
    Dall = p.sbuf("Dall", [128, 48 * NCH])
    p.dma('sp', Dall, Dall[:, :], SD, bass.AP(SD.h.tensor, 0, [[0, 128], [1, 48 * NCH]]))
    Ur = [p.sbuf(f"Ur{i}", [1, 16, 128]) for i in range(2)]
    C = [p.sbuf(f"C{s}", [128, 257]) for s in range(16)]
    Cb = [p.sbuf(f"Cb{s}", [128, 257], BF16) for s in range(16)]
    QT = [p.sbuf(f"QT{i}", [128, 128], BF16) for i in range(2)]; KT = [p.sbuf(f"KT{i}", [128, 128], BF16) for i in range(2)]
    Wt = [p.sbuf(f"Wt{i}", [128, 128]) for i in range(2)]; Pm = [p.sbuf(f"Pm{i}", [128, 128], BF16) for i in range(2)]
    Bs = [p.sbuf(f"Bs{i}", [128, 257]) for i in range(2)]; Kk = [p.sbuf(f"Kk{i}", [128, 128], BF16) for i in range(2)]
    ps_t = [p.psum(f"pst{i}", [128, 512]) for i in range(2)]
    ps_su = [p.psum(f"pssu{i}", [128, 2, 128]) for i in range(2)]
    ps_a = p.psum("psa", [128, 257]); ps_b = p.psum("psb", [128, 257]); ps_c = p.psum("psc", [128, 257])
    SCL = 128.0 ** -0.5
    it = 0
    tens = lambda t_: t_.h.tensor
    import os
    LVL = int(os.environ.get("CHK_LEVEL", "9")); NCHL = int(os.environ.get("CHK_NCH", str(NCH)))
    for d in dirs:
        for s in range(16):
            p.op('dve', lambda e: e.memset(C[s][:, :], 0.0), writes=[C[s]])
            p.op('pool', lambda e: e.memset(Cb[s][:, :], 0.0), writes=[Cb[s]])
        for c in range(NCHL):
            cn = c if d == 0 else ((1 - c) if c < 2 else (19 - c))
            r0 = cn * 128
            b_ = (d * NCH + c) % 2
            qr, qm, vx, vb, yo = QKr[b_], QKm[b_], Vx[b_], Vb[b_], Yo[b_]
            cols, ur = Cols[b_], Ur[b_]
            p.dma('sp', qr, qr.h[:, :, :].rearrange("p h e -> p (h e)"), S_AQK, S_AQK.h[r0:r0 + 128, :])
            p.dma('sp', qm, qm.h[:, :, :].rearrange("p h e -> p (h e)"), S_BQKtm, S_BQKtm.h[r0:r0 + 128, :])
            p.dma_group([('sp', vx, vx[:, 0:8, 0:256], S_P, S_P.h[r0:r0 + 128, 2048:4096].rearrange("p (h e) -> p h e", e=256)),
                         ('sp', vx, vx[:, 8:16, 0:256], S_P, S_P.h[r0:r0 + 128, 6144:8192].rearrange("p (h e) -> p h e", e=256))])
            p.op('act', lambda e: e.copy(vb[:, :, :], vx[:, :, :]), reads=[vx], writes=[vb])
            p.dma('sp', cols, cols.h[:, :, :].rearrange("p q g -> p (q g)"), SCOL, SCOL.h[r0:r0 + 128, :])
            p.dma_group([('sp', ur, ur[0:1, 0:8, :], SU, bass.AP(tens(SU), (32 + d * 8) * NTK + r0, [[0, 1], [NTK, 8], [1, 128]])),
                         ('sp', ur, ur[0:1, 8:16, :], SU, bass.AP(tens(SU), (d * 8) * NTK + r0, [[0, 1], [NTK, 8], [1, 128]]))])
            for s in range(16 if LVL >= 1 else 0):
                h = s % 8
                qk = qr if s < 8 else qm
                g = (32 if s < 8 else 0) + d * 8 + h
                wcol, acol, kcol = cols[:, 0, g:g + 1], cols[:, 1, g:g + 1], cols[:, 2, g:g + 1]
                dcol = Dall[:, g * NCH + c:g * NCH + c + 1]
                i2 = it % 2
                it += 1
                pt, psu = ps_t[i2], ps_su[i2]
                qT, kT, wt, pm, bs, kk = QT[i2], KT[i2], Wt[i2], Pm[i2], Bs[i2], Kk[i2]
                p.op('pe', lambda e: e.transpose(pt[:, 0:128], qk[:, h, :], idt[:, :]), reads=[qk, idt], writes=[pt])
                p.op('pe', lambda e: e.transpose(pt[:, 128:256], qk[:, 8 + h, :], idt[:, :]), reads=[qk, idt], writes=[pt])
                p.op('act', lambda e: e.activation(out=qT[:, :], in_=pt[:, 0:128], func=AF.Identity, scale=SCL), reads=[pt], writes=[qT])
                p.op('dve', lambda e: e.tensor_copy(kT[:, :], pt[:, 128:256]), reads=[pt], writes=[kT])
                p.op('pe', lambda e: e.matmul(psu[:, 0, :], kT[:, :], qT[:, :], start=True, stop=True), reads=[kT, qT], writes=[psu])
                if LVL < 2:
                    continue
                p.op('pe', lambda e: e.matmul(psu[:, 1, :], ones_r[0:1, :], ur[0:1, s, :], start=True, stop=False), reads=[ones_r, ur], writes=[psu])
                p.op('pe', lambda e: e.matmul(psu[:, 1, :], idt[:, :], mk_[d][:, :], start=False, stop=True), reads=[idt, mk_[d]], writes=[psu])
                p.op('act', lambda e: e.activation(out=wt[:, :], in_=psu[:, 1, :], func=AF.Exp, bias=wcol, scale=1.0), reads=[psu, cols], writes=[wt])
                p.op('dve', lambda e: e.tensor_tensor(out=pm[:, :], in0=psu[:, 0, :], in1=wt[:, :], op=ALU.mult), reads=[psu, wt], writes=[pm])
                if LVL < 3:
                    continue
                p.op('pe', lambda e: e.matmul(ps_a[:, :], pm[:, :], vb[:, s, :], start=True, stop=True), reads=[pm, vb], writes=[ps_a])
                p.op('pe', lambda e: e.matmul(ps_b[:, :], qT[:, :], Cb[s][:, :], start=True, stop=True), reads=[qT, Cb[s]], writes=[ps_b])
                p.op('act', lambda e: e.activation(out=bs[:, :], in_=ps_b[:, :], func=AF.Identity, scale=acol), reads=[ps_b, cols], writes=[bs])
                p.op('dve', lambda e: e.tensor_tensor(out=yo[:, s, 0:257], in0=ps_a[:, :], in1=bs[:, :], op=ALU.add), reads=[ps_a, bs], writes=[yo])
                p.op('act', lambda e: e.activation(out=kk[:, :], in_=qk[:, 8 + h, :], func=AF.Identity, scale=kcol), reads=[qk, cols], writes=[kk])
                p.op('pe', lambda e: e.matmul(ps_c[:, :], kk[:, :], vb[:, s, :], start=True, stop=True), reads=[kk, vb], writes=[ps_c])
                p.op('dve', lambda e: e.scalar_tensor_tensor(out=C[s][:, :], in0=C[s][:, :], scalar=dcol, in1=ps_c[:, :], op0=ALU.mult, op1=ALU.add),
                     reads=[C[s], Dall, ps_c], writes=[C[s]])
                p.op('act', lambda e: e.copy(Cb[s][:, :], C[s][:, :]), reads=[C[s]], writes=[Cb[s]])
            p.dma('sp', S_Y0, S_Y0.h[d * NTK + r0:d * NTK + r0 + 128, :], yo, yo.h[:, :, :].rearrange("p s c -> p (s c)"))
    p.end_stage()

import time as _time

N_CORES = 8


def _vl(v):
    return np.ascontiguousarray(np.asarray(v, np.float32).reshape(KC, 128).T)


def fused_shared_inputs(ada_w, ada_b, norm_g, mlp_w_in, mlp_w_out, ev_w_in, ev_b_gate, ev_conv_qk,
                        ev_gn_ret, ev_gn_mlstm, ev_w_out, od_mu, od_w_rkv, od_w0, od_w1, od_w2, od_a0, od_a1, od_a2,
                        od_g1, od_g2, od_k_k, od_k_a, od_r_k, od_lnx_w, od_lnx_b, od_w_out):
    f32 = np.float32
    A = lambda v: np.ascontiguousarray(np.asarray(v, dtype=f32))
    sh = {}
    sh["adaw0"] = A(ada_w[0]); sh["adaw1"] = A(ada_w[1])
    sh["adab"] = A(np.stack([A(ada_b[l]).reshape(192, 128).T for l in range(2)], axis=1))
    sh["ng"] = A(np.stack([np.stack([_vl(norm_g[l, i]) for i in range(4)], axis=1) for l in range(2)], axis=1))
    w_in = A(ev_w_in[0])
    cols = np.concatenate([np.arange(0, 6144), np.arange(8192, 12288)])
    sh["w_tm"] = A(w_in[:, cols]); sh["w_bqk"] = A(w_in[:, 6144:8192])
    gi_cols = 12288 + np.concatenate([np.arange(0, 8), np.arange(16, 24)])
    gf_cols = 12288 + np.concatenate([np.arange(8, 16), np.arange(24, 32)])
    sh["w_gi"] = A(w_in[:, gi_cols]); sh["w_gf"] = A(w_in[:, gf_cols])
    bg = A(ev_b_gate[0])
    sh["gbias"] = A(np.stack([bg[gi_cols - 12288], bg[gf_cols - 12288]], axis=1))
    conv = A(ev_conv_qk[0]).reshape(9, 2048)
    sh["wc"] = A(conv.T.reshape(16, 128, 9).transpose(1, 0, 2))
    inv = np.power(f32(10000.0), -np.arange(32, dtype=f32) / f32(32))
    pos = np.arange(2048)
    rows = (pos // 64).astype(f32); colsp = (pos % 64).astype(f32)
    ar = rows[:, None] * inv[None, :]; ac = colsp[:, None] * inv[None, :]
    sh["cos"] = A(np.concatenate([np.cos(ar), np.cos(ar), np.cos(ac), np.cos(ac)], axis=1).reshape(16, 128, 128))
    sh["sins"] = A(np.concatenate([-np.sin(ar), np.sin(ar), -np.sin(ac), np.sin(ac)], axis=1).reshape(16, 128, 128))
    sh["ident"] = np.eye(128, dtype=f32)
    hh_ = np.arange(8, dtype=f32) / f32(7)
    lg = np.log1p(-np.exp2(-(f32(5.0) + f32(7.0) * hh_))).astype(f32)
    gam = [np.exp(lg).astype(f32), np.exp(lg[::-1]).astype(f32)]
    rds = np.empty((2, 2, 8, NTK), f32)
    for d in range(2):
        rds[d, 0] = gam[d][:, None]
        rds[d, 1] = f32(128.0 ** -0.5)
    sh["ret_ds"] = rds
    rlf = np.empty((16, NTK), f32)
    for d in range(2):
        lgd = lg if d == 0 else lg[::-1]
        rlf[d * 8:(d + 1) * 8] = lgd[:, None]
    sh["retlf"] = rlf
    jj = np.arange(128)[:, None]; ii = np.arange(128)[None, :]
    NEG = f32(-1e30)
    sh["masks"] = np.stack([np.where(jj <= ii, f32(0), NEG), np.where(jj >= ii, f32(0), NEG)]).astype(f32)
    sh["gn_ret"] = A(np.tile(A(ev_gn_ret[0]), (128, 1))); sh["gn_ml"] = A(np.tile(A(ev_gn_mlstm[0]), (128, 1)))
    sh["w_out0"] = A(ev_w_out[0]); sh["w1_0"] = A(mlp_w_in[0]); sh["w2_0"] = A(mlp_w_out[0])
    sh["mu"] = A(np.stack([_vl(od_mu[0, i]) for i in range(6)], axis=1))
    for i in range(3):
        sh[f"w_rkv{i}"] = A(od_w_rkv[0, i])
    for d in range(2):
        sh[f"lw1_{d}"] = A(od_w1[0, d]); sh[f"lw2_{d}"] = A(od_w2[0, d]); sh[f"la1_{d}"] = A(od_a1[0, d]); sh[f"la2_{d}"] = A(od_a2[0, d])
    g1p = np.zeros((D, 512), f32); g1p[:, :480] = A(od_g1[0]); g2p = np.zeros((512, D), f32); g2p[:480] = A(od_g2[0])
    sh["g1p"] = g1p; sh["g2p"] = g2p
    sh["vecs_e2"] = A(np.stack([np.tile(A(v), (128, 1)) for v in [od_k_k[0], od_k_a[0], od_r_k[0], od_w0[0, 0], od_w0[0, 1], od_a0[0, 0], od_a0[0, 1]]]))
    sh["vecs_g0"] = A(np.stack([np.tile(A(od_lnx_w[0]), (128, 1)), np.tile(A(od_lnx_b[0]), (128, 1))]))
    sh["w_out1"] = A(od_w_out[0]); sh["w1_1"] = A(mlp_w_in[1]); sh["w2_1"] = A(mlp_w_out[1])
    return sh


def fused_core_inputs(b, x, c, ctx, c_ctx):
    f32 = np.float32
    xT = np.ascontiguousarray(np.concatenate([np.asarray(ctx[b], f32), np.asarray(x[b], f32)], axis=0).T)
    c2 = np.stack([np.asarray(c[b], f32), np.asarray(c_ctx, f32)], axis=0)
    condT = np.ascontiguousarray(c2.T.reshape(KC, 128, 2).transpose(1, 0, 2))
    return dict(xT=xT, condT=condT)


def kernel(x, c, ctx, c_ctx, **w):
    t0 = _time.time()
    sh = fused_shared_inputs(**w)
    p = build_fused(10)
    print(f"[kernel] built fused program: {p.n_ops} ops in {_time.time() - t0:.1f}s", flush=True)
    in_maps = []
    for j in range(N_CORES):
        m = dict(sh)
        m.update(fused_core_inputs(j % 4, x, c, ctx, c_ctx))
        in_maps.append(m)
    res = run_bass_kernel_spmd(p.nc, in_maps, core_ids=list(range(N_CORES)))
    print(f"[kernel] fused launch done {_time.time() - t0:.1f}s", flush=True)
    out = np.empty((4, 2048, D), np.float32)
    for b in range(4):
        out[b] = res.results[b]["outT"].T
    return out
```

```python
import numpy as np
from contextlib import ExitStack
import concourse.bass as bass
import concourse.mybir as mybir
from concourse.bass_utils import run_bass_kernel_spmd

F32 = mybir.dt.float32
BF16 = mybir.dt.bfloat16
ALU = mybir.AluOpType
AF = mybir.ActivationFunctionType
AX = mybir.AxisListType


class T:
    def __init__(self, name, h, space):
        self.name = name
        self.h = h
        self.space = space
        self.lastw = None
        self.readers = {}
        self.dsem = None
        self.dcount = 0
        self.dbase = 0
        self.pend_w = {}
        self.pend_r = {}
        self.untracked = False

    def __getitem__(self, idx):
        return self.h[idx]


class Prog:
    SAME_ENGINE_SYNC = ('act', 'dve', 'pool')

    def __init__(self):
        self.nc = bass.Bass("TRN2", target_bir_lowering=False)
        self.es = ExitStack()
        nc = self.nc
        self.engs = {'pe': nc.tensor, 'act': nc.scalar, 'dve': nc.vector, 'pool': nc.gpsimd, 'sp': nc.sync}
        self.sem = {e: self.es.enter_context(nc.semaphore("s_" + e)) for e in self.engs}
        self.cnt = {e: 0 for e in self.engs}
        self.seen = {e: {e2: 0 for e2 in self.engs} for e in self.engs}
        self.dseen = {e: {} for e in self.engs}
        self.n_ops = 0
        self.out_tiles = []
        self.tiles = []
        self.drams = []
        self.stage_es = None
        self.stage_tiles = []
        self.sem_pool = []
        self.uid = 0

    def dram(self, name, shape, dtype=F32, kind="ExternalInput"):
        h = self.nc.dram_tensor(name, list(shape), dtype, kind=kind)
        t = T(name, h.ap(), 'dram')
        t.kind = kind
        t.untracked = (kind != "Internal")
        if kind == "ExternalOutput":
            self.out_tiles.append(t)
        else:
            self.drams.append(t)
        return t

    def _es(self):
        return self.stage_es if self.stage_es is not None else self.es

    def sbuf(self, name, shape, dtype=F32):
        self.uid += 1
        name = f"{name}_{self.uid}"
        h = self._es().enter_context(self.nc.sbuf_tensor(name, list(shape), dtype))
        t = T(name, h, 'sbuf')
        self.tiles.append(t)
        if self.stage_es is not None:
            self.stage_tiles.append(t)
        return t

    def psum(self, name, shape, dtype=F32):
        self.uid += 1
        name = f"{name}_{self.uid}"
        h = self._es().enter_context(self.nc.psum_tensor(name, list(shape), dtype))
        t = T(name, h, 'psum')
        self.tiles.append(t)
        if self.stage_es is not None:
            self.stage_tiles.append(t)
        return t

    def begin_stage(self):
        assert self.stage_es is None
        self.stage_es = ExitStack()
        self.stage_tiles = []

    def barrier(self):
        alls = self.tiles + self.out_tiles + self.drams
        for e in self.engs:
            E = self.engs[e]
            for e2 in self.engs:
                if e2 != e and self.seen[e][e2] < self.cnt[e2]:
                    E.wait_ge(self.sem[e2], self.cnt[e2])
                    self.seen[e][e2] = self.cnt[e2]
            for t in alls:
                if t.dsem is not None and t.dcount > t.dbase:
                    E.wait_ge(t.dsem, t.dcount)
        for t in alls:
            t.lastw = None
            t.readers = {}
            t.pend_w = {}
            t.pend_r = {}
            if t.dsem is not None:
                self.sem_pool.append((t.dsem, t.dcount))
                t.dsem = None
        for e in self.engs:
            self.dseen[e] = {}

    def end_stage(self):
        self.barrier()
        st = set(self.stage_tiles)
        self.tiles = [t for t in self.tiles if t not in st]
        self.stage_es.close()
        self.stage_es = None
        self.stage_tiles = []

    def _get_dsem(self, owner):
        if owner.dsem is None:
            if self.sem_pool:
                owner.dsem, owner.dcount = self.sem_pool.pop()
            else:
                self.uid += 1
                owner.dsem = self.es.enter_context(self.nc.semaphore(f"d{self.uid}"))
                owner.dcount = 0
            owner.dbase = owner.dcount

    def _need(self, eng, reads, writes):
        cw = {}
        dw = {}

        def addc(ei):
            if ei is None:
                return
            e2, i2 = ei
            if i2 > cw.get(e2, 0):
                cw[e2] = i2

        def addd(d):
            for o, c in d.items():
                if c > dw.get(o, 0):
                    dw[o] = c
        reads = [t for t in reads if not t.untracked]
        writes = [t for t in writes if not t.untracked]
        for t in reads:
            addc(t.lastw)
            addd(t.pend_w)
        for t in writes:
            addc(t.lastw)
            for e2, i2 in t.readers.items():
                addc((e2, i2))
            addd(t.pend_w)
            addd(t.pend_r)
        return cw, dw

    def _emit_waits(self, eng, cw, dw):
        E = self.engs[eng]
        for e2, i2 in cw.items():
            if e2 == eng and eng not in self.SAME_ENGINE_SYNC:
                continue
            if self.seen[eng][e2] >= i2:
                continue
            E.wait_ge(self.sem[e2], i2)
            self.seen[eng][e2] = i2
        for o, c in dw.items():
            if self.dseen[eng].get(o, 0) >= c:
                continue
            E.wait_ge(o.dsem, c)
            self.dseen[eng][o] = c

    def op(self, eng, fn, reads=(), writes=()):
        cw, dw = self._need(eng, reads, writes)
        self._emit_waits(eng, cw, dw)
        ins = fn(self.engs[eng])
        self.cnt[eng] += 1
        idx = self.cnt[eng]
        ins.then_inc(self.sem[eng], 1)
        self.seen[eng][eng] = max(self.seen[eng][eng], 0)
        for t in writes:
            t.lastw = (eng, idx)
            t.readers = {}
            t.pend_w = {}
            t.pend_r = {}
        for t in reads:
            if t in writes:
                continue
            t.readers[eng] = idx
        self.n_ops += 1
        return ins

    def dma(self, q, out_t, out_ap, in_t, in_ap, **kw):
        return self.dma_group([(q, out_t, out_ap, in_t, in_ap)], **kw)

    def dma_group(self, items, **kw):
        needs = []
        for (q, out_t, out_ap, in_t, in_ap) in items:
            cw, dw = self._need(q, [in_t], [out_t])
            needs.append((cw, dw))
        recs = []
        for (q, out_t, out_ap, in_t, in_ap), (cw, dw) in zip(items, needs):
            self._emit_waits(q, cw, dw)
            if out_t.space != 'dram' or getattr(out_t, 'kind', '') == 'Internal':
                owner = out_t
            else:
                owner = in_t
            self._get_dsem(owner)
            ins = self.engs[q].dma_start(out=out_ap, in_=in_ap, **kw)
            owner.dcount += 16
            ins.then_inc(owner.dsem, 16)
            recs.append((out_t, in_t, owner))
            self.n_ops += 1
        wrote = set()
        for out_t, in_t, owner in recs:
            if out_t not in wrote and not out_t.untracked:
                out_t.lastw = None
                out_t.readers = {}
                out_t.pend_w = {}
                out_t.pend_r = {}
                wrote.add(out_t)
        for out_t, in_t, owner in recs:
            if not out_t.untracked:
                out_t.pend_w[owner] = owner.dcount
            if not in_t.untracked:
                in_t.pend_r[owner] = max(in_t.pend_r.get(owner, 0), owner.dcount)

    def finish(self):
        dw = {}
        for t in self.tiles + self.out_tiles + self.drams:
            if t.dsem is not None:
                dw[t] = t.dcount
        self._emit_waits('sp', {}, dw)

    def close(self):
        self.es.close()


D = 4096
KC = 32
DFF = 16384
EPS = 1e-6
NTK = 2304
NCTX = 256
GMAX = 512
GROUPS0 = [(0, 256, 1)] + [(256 + i * 512, 512, 0) for i in range(4)]
GROUPS1 = [(256 + i * 512, 512, 0) for i in range(4)]


class FCtx:
    def __init__(self, p, gmax=GMAX, n_wb=4):
        self.p = p
        self.gmax = gmax
        self.ones = p.sbuf("ones", [128, 128], F32)
        p.op('dve', lambda e: e.memset(self.ones[:, :], 1.0), writes=[self.ones])
        self.epsb = p.sbuf("epsb", [128, 1], F32)
        p.op('dve', lambda e: e.memset(self.epsb[:, :], EPS), writes=[self.epsb])
        self.wb = [p.sbuf(f"wb{i}", [128, 8192], BF16) for i in range(n_wb)]
        self.wi = 0
        self.ps = [p.psum(f"ps{i}", [128, 512], F32) for i in range(4)]
        self.pi = 0
        self.ps_r = p.psum("ps_r", [128, 512], F32)
        self.sq = [p.sbuf(f"sq{i}", [128, gmax], F32) for i in range(2)]
        self.sqi = 0
        self.rbc = p.sbuf("rbc", [128, gmax], F32)
        self.tmp = [p.sbuf(f"tmp{i}", [128, gmax], F32) for i in range(2)]
        self.tmi = 0
        self.stg = [p.sbuf(f"stg{i}", [128, 512], F32) for i in range(3)]
        self.sti = 0
        self.G = gmax

    def nxt(self, name):
        lst = getattr(self, name)
        k = name + "_i"
        i = getattr(self, k, 0)
        setattr(self, k, i + 1)
        return lst[i % len(lst)]


def f_rstd(c, src, kcs=KC):
    p, G = c.p, c.G
    for kc in range(kcs):
        sq = c.nxt("sq")
        p.op('act', lambda e: e.activation(out=sq[:, :G], in_=src[:, kc, :G], func=AF.Square), reads=[src], writes=[sq])
        p.op('pe', lambda e: e.matmul(c.ps_r[:, :G], c.ones[:, :], sq[:, :G], start=(kc == 0), stop=(kc == kcs - 1)),
             reads=[sq, c.ones], writes=[c.ps_r])
    p.op('act', lambda e: e.activation(out=c.rbc[:, :G], in_=c.ps_r[:, :G], func=AF.Sqrt, bias=c.epsb[:, 0:1], scale=1.0 / (kcs * 128)),
         reads=[c.ps_r, c.epsb], writes=[c.rbc])
    p.op('dve', lambda e: e.reciprocal(out=c.rbc[:, :G], in_=c.rbc[:, :G]), reads=[c.rbc], writes=[c.rbc])


def f_modulate(c, src, dst_fn, dst_t, gp, sh):
    p, G = c.p, c.G
    for kc in range(KC):
        tmp = c.nxt("tmp")
        p.op('dve', lambda e: e.scalar_tensor_tensor(out=tmp[:, :G], in0=src[:, kc, :G], scalar=gp[1](kc), in1=c.rbc[:, :G], op0=ALU.mult, op1=ALU.mult),
             reads=[src, gp[0], c.rbc], writes=[tmp])
        p.op('act', lambda e: e.activation(out=dst_fn(kc), in_=tmp[:, :G], func=AF.Identity, bias=sh[1](kc), scale=1.0),
             reads=[tmp, sh[0]], writes=[dst_t])


def f_resid(c, buf, a, hsrc, tok0):
    p, G = c.p, c.G
    for kc in range(KC):
        hb = c.nxt("stg")
        p.dma('sp', hb, hb[:, :G], hsrc, hsrc.h[kc * 128:(kc + 1) * 128, tok0:tok0 + G])
        tmp = c.nxt("tmp")
        p.op('dve', lambda e: e.scalar_tensor_tensor(out=tmp[:, :G], in0=buf[:, kc, :G], scalar=a[1](kc), in1=c.rbc[:, :G], op0=ALU.mult, op1=ALU.mult),
             reads=[buf, a[0], c.rbc], writes=[tmp])
        p.op('dve', lambda e: e.tensor_tensor(out=buf[:, kc, :G], in0=tmp[:, :G], in1=hb[:, :G], op=ALU.add), reads=[tmp, hb], writes=[buf])


def f_load_w(c, w_ap, nrk, ncol):
    p = c.p
    assert nrk * ncol <= 8192
    wb = c.nxt("wb")
    view = wb.h[:, 0:nrk * ncol].rearrange("p (k n) -> p k n", k=nrk)
    p.dma('pool', wb, view, w_ap[0], w_ap[1].rearrange("(k p) n -> p k n", p=128))
    return wb, view


def f_gemm_fm(c, xT, w, kcs, n0, n1, out_cb):
    p, G = c.p, c.G
    WC = min(8192 // kcs, 512)
    for c0 in range(n0, n1, WC):
        ncol = min(WC, n1 - c0)
        wb, view = f_load_w(c, (w, w.h[0:kcs * 128, c0:c0 + ncol]), kcs, ncol)
        for m0 in range(0, ncol, 128):
            msz = min(128, ncol - m0)
            ps = c.nxt("ps")
            for kc in range(kcs):
                p.op('pe', lambda e: e.matmul(ps[:msz, :G], view[:, kc, m0:m0 + msz], xT[:, kc, :G], start=(kc == 0), stop=(kc == kcs - 1)),
                     reads=[wb, xT], writes=[ps])
            out_cb(ps, (c0 + m0) // 128, msz)


def f_gemm_tm(c, xT, w, kcs, n0, n1, out_dram, row0, col0=0, act=None):
    p, G = c.p, c.G
    WC = min(8192 // kcs, 512)
    for c0 in range(n0, n1, WC):
        ncol = min(WC, n1 - c0)
        wb, view = f_load_w(c, (w, w.h[0:kcs * 128, c0:c0 + ncol]), kcs, ncol)
        for t0 in range(0, G, 128):
            tsz = min(128, G - t0)
            ps = c.nxt("ps")
            for kc in range(kcs):
                p.op('pe', lambda e: e.matmul(ps[:tsz, :ncol], xT[:, kc, t0:t0 + tsz], view[:, kc, :ncol], start=(kc == 0), stop=(kc == kcs - 1)),
                     reads=[wb, xT], writes=[ps])
            st = c.nxt("stg")
            if act is None:
                p.op('act', lambda e: e.copy(st[:tsz, :ncol], ps[:tsz, :ncol]), reads=[ps], writes=[st])
            else:
                p.op('act', lambda e: e.activation(out=st[:tsz, :ncol], in_=ps[:tsz, :ncol], func=act), reads=[ps], writes=[st])
            p.dma('sp', out_dram, out_dram.h[row0 + t0:row0 + t0 + tsz, col0 + c0 - n0:col0 + c0 - n0 + ncol], st, st[:tsz, :ncol])


def f_transpose_in(c, src_dram, row0, ncols, U, ident, tin):
    p, G = c.p, c.G
    nk = ncols // 128
    for ti, t0 in enumerate(range(0, G, 128)):
        tsz = min(128, G - t0)
        X = tin[ti % len(tin)]
        p.dma('sp', X, X[:tsz, :ncols], src_dram, src_dram.h[row0 + t0:row0 + t0 + tsz, 0:ncols])
        for k4 in range(0, nk, 4):
            ps = c.nxt("ps")
            for j in range(4):
                kc = k4 + j
                p.op('pe', lambda e: e.transpose(ps[:, j * 128:j * 128 + tsz], X[:tsz, kc * 128:(kc + 1) * 128], ident[:tsz, :tsz]),
                     reads=[X, ident], writes=[ps])
            eng = 'act' if (k4 // 4) % 2 == 0 else 'dve'
            src = ps.h[:, :].rearrange("p (j t) -> p j t", j=4)[:, :, :tsz]
            if eng == 'act':
                p.op('act', lambda e: e.copy(U[:, k4:k4 + 4, t0:t0 + tsz], src), reads=[ps], writes=[U])
            else:
                p.op('dve', lambda e: e.tensor_copy(U[:, k4:k4 + 4, t0:t0 + tsz], src), reads=[ps], writes=[U])


def f_mlp(c, U, acc, w1, w2, dff, hT, rl):
    p, G = c.p, c.G
    HG = 256
    for gi, h0 in enumerate(range(0, dff, HG)):
        wb1, v1 = f_load_w(c, (w1, w1.h[:, h0:h0 + HG]), KC, HG)
        ht = hT[gi % 2]
        for s in range(2):
            ps = c.nxt("ps")
            for kc in range(KC):
                p.op('pe', lambda e: e.matmul(ps[:, :G], v1[:, kc, s * 128:(s + 1) * 128], U[:, kc, :G], start=(kc == 0), stop=(kc == KC - 1)),
                     reads=[wb1, U], writes=[ps])
            r = rl[s]
            p.op('act', lambda e: e.activation(out=r[:, :G], in_=ps[:, :G], func=AF.Relu), reads=[ps], writes=[r])
            p.op('act', lambda e: e.activation(out=ht[:, s, :G], in_=r[:, :G], func=AF.Square), reads=[r], writes=[ht])
        for half in range(2):
            wb2 = c.nxt("wb")
            v2 = wb2.h[:, 0:4096].rearrange("p (k n) -> p k n", k=2)
            cbase = half * 2048
            p.dma('pool', wb2, v2, w2, w2.h[h0:h0 + HG, cbase:cbase + 2048].rearrange("(k p) n -> p k n", p=128))
            for dci in range(16):
                dc = half * 16 + dci
                ps = c.nxt("ps")
                for s in range(2):
                    p.op('pe', lambda e: e.matmul(ps[:, :G], v2[:, s, dci * 128:(dci + 1) * 128], ht[:, s, :G], start=(s == 0), stop=(s == 1)),
                         reads=[wb2, ht], writes=[ps])
                if gi == 0:
                    p.op('dve', lambda e: e.tensor_copy(acc[:, dc, :G], ps[:, :G]), reads=[ps], writes=[acc])
                else:
                    p.op('dve', lambda e: e.tensor_tensor(out=acc[:, dc, :G], in0=acc[:, dc, :G], in1=ps[:, :G], op=ALU.add), reads=[ps, acc], writes=[acc])


def stage_ada(p, condT, adaw, adab, modv):
    p.begin_stage()
    cf = p.sbuf("cf", [128, KC, 2], F32)
    cb = p.sbuf("cb", [128, KC, 2], BF16)
    p.dma('sp', cf, cf[:, :, :], condT, condT.h[:, :, :])
    p.op('act', lambda e: e.activation(out=cb[:, :, :], in_=cf[:, :, :], func=AF.Silu), reads=[cf], writes=[cb])
    bs = p.sbuf("adab", [128, 2, 192], F32)
    p.dma('sp', bs, bs[:, :, :], adab, adab.h[:, :, :])
    wbs = [p.sbuf(f"awb{i}", [128, KC, 256], BF16) for i in range(3)]
    pss = [p.psum(f"aps{i}", [128, 2], F32) for i in range(4)]
    it = 0
    for l in range(2):
        for ci, c0 in enumerate(range(0, 6 * D, 256)):
            wb = wbs[ci % 3]
            p.dma('pool', wb, wb[:, :, :], adaw[l], adaw[l].h[:, c0:c0 + 256].rearrange("(k p) n -> p k n", p=128))
            for m in range(2):
                ps = pss[it % 4]
                it += 1
                for kc in range(KC):
                    p.op('pe', lambda e: e.matmul(ps[:, :], wb[:, kc, m * 128:(m + 1) * 128], cb[:, kc, :], start=(kc == 0), stop=(kc == KC - 1)),
                         reads=[wb, cb], writes=[ps])
                q = (c0 + m * 128) // 128
                i, kk = q // 32, q % 32
                p.op('dve', lambda e: e.tensor_scalar(out=modv[:, l, i, kk, :], in0=ps[:, :], scalar1=bs[:, l, q:q + 1], scalar2=0.0, op0=ALU.add, op1=ALU.add),
                     reads=[ps, bs], writes=[modv])
    p.end_stage()


def make_vecs(p, modv, ng, l, with_next=False):
    def mv(i, s):
        return (modv, lambda kc, i=i, s=s: modv[:, l, i, kc, s:s + 1])
    out = {}
    tl = {}

    def comb(name, gi, mi, plus1):
        res = []
        for s in range(2):
            t = p.sbuf(f"c_{name}{s}", [128, KC], F32)
            if plus1:
                p.op('dve', lambda e: e.scalar_tensor_tensor(out=t[:, :], in0=modv[:, l, mi, :, s], scalar=1.0, in1=ng[:, l, gi, :], op0=ALU.add, op1=ALU.mult),
                     reads=[modv, ng], writes=[t])
            else:
                p.op('dve', lambda e: e.tensor_tensor(out=t[:, :], in0=modv[:, l, mi, :, s], in1=ng[:, l, gi, :], op=ALU.mult), reads=[modv, ng], writes=[t])
            res.append((t, lambda kc, t=t: t[:, kc:kc + 1]))
        return res
    out["GP_mix"] = comb("GPm", 0, 1, True)
    out["sh_mix"] = [mv(0, s) for s in range(2)]
    out["A_mix"] = comb("Am", 1, 2, False)
    out["GP_ff"] = comb("GPf", 2, 4, True)
    out["sh_ff"] = [mv(3, s) for s in range(2)]
    out["A_ff"] = comb("Af", 3, 5, False)
    return out


def stage_inproj(p, xT, modv, ng, w_tm, w_bqk, w_gi, w_gf, S_P, S_BQK, S_GI, S_GF):
    p.begin_stage()
    c = FCtx(p)
    V = make_vecs(p, modv, ng, 0)
    B = p.sbuf("B", [128, KC, GMAX], F32)
    U = p.sbuf("U", [128, KC, GMAX], BF16)
    for (tok0, G, si) in GROUPS0:
        c.G = G
        p.dma('sp', B, B[:, :, :G], xT, xT.h[:, tok0:tok0 + G].rearrange("(k p) t -> p k t", p=128))
        f_rstd(c, B)
        f_modulate(c, B, lambda kc: U[:, kc, :G], U, V["GP_mix"][si], V["sh_mix"][si])
        f_gemm_tm(c, U, w_tm, KC, 0, 10240, S_P, tok0)

        def cb_to(dst):
            def cb(ps, fc, msz):
                st = c.nxt("stg")
                p.op('act', lambda e: e.copy(st[:msz, :G], ps[:msz, :G]), reads=[ps], writes=[st])
                p.dma('sp', dst, dst.h[fc * 128:fc * 128 + msz, tok0:tok0 + G], st, st[:msz, :G])
            return cb
        f_gemm_fm(c, U, w_bqk, KC, 0, 2048, cb_to(S_BQK))
        f_gemm_fm(c, U, w_gi, KC, 0, 16, cb_to(S_GI))
        f_gemm_fm(c, U, w_gf, KC, 0, 16, cb_to(S_GF))
    p.end_stage()


def _proc_src(dram_t, row0, nrows, rowlen, rev):
    import concourse.bass as bass
    tens = dram_t.h.tensor
    if not rev:
        return [(0, NTK, dram_t.h[row0:row0 + nrows, :])]
    a = bass.AP(tens, row0 * rowlen + (NCTX - 1), [[rowlen, nrows], [-1, NCTX]])
    b = bass.AP(tens, row0 * rowlen + (NTK - 1), [[rowlen, nrows], [-1, NTK - NCTX]])
    return [(0, NCTX, a), (NCTX, NTK - NCTX, b)]


def stage_c1(p, S_GI, S_GF, gbias, S_GD, S_GS, S_GE, S_BQK, wc_d, S_BQKtm, S_P, S_AQK, cos_d, sin_d, ident):
    p.begin_stage()
    T_ALL = NTK
    gi = p.sbuf("gi", [16, T_ALL]); gf = p.sbuf("gf", [16, T_ALL]); gb = p.sbuf("gb", [16, 2])
    p.dma('sp', gb, gb[:, :], gbias, gbias.h[:, :])
    for (tile_, src) in ((gi, S_GI), (gf, S_GF)):
        items = [('sp', tile_, tile_[0:8, :], src, src.h[0:8, :])]
        for (c0, n, ap) in _proc_src(src, 8, 8, T_ALL, True):
            items.append(('sp', tile_, tile_[8:16, c0:c0 + n], src, ap))
        p.dma_group(items, allow_slow_non_contiguous=True)
    g1 = p.sbuf("g1", [16, T_ALL]); g2 = p.sbuf("g2", [16, T_ALL]); gm = p.sbuf("gm", [16, T_ALL + 1]); g3 = p.sbuf("g3", [16, T_ALL])
    p.op('dve', lambda e: e.tensor_scalar(out=gi[:, :], in0=gi[:, :], scalar1=gb[:, 0:1], scalar2=0.0, op0=ALU.add, op1=ALU.add), reads=[gi, gb], writes=[gi])
    p.op('dve', lambda e: e.tensor_scalar(out=gf[:, :], in0=gf[:, :], scalar1=gb[:, 1:2], scalar2=0.0, op0=ALU.add, op1=ALU.add), reads=[gf, gb], writes=[gf])
    p.op('act', lambda e: e.activation(out=g1[:, :], in_=gf[:, :], func=AF.Abs), reads=[gf], writes=[g1])
    p.op('act', lambda e: e.activation(out=g1[:, :], in_=g1[:, :], func=AF.Exp, scale=-1.0), reads=[g1], writes=[g1])
    p.op('act', lambda e: e.activation(out=g1[:, :], in_=g1[:, :], func=AF.Ln, bias=1.0, scale=1.0), reads=[g1], writes=[g1])
    p.op('dve', lambda e: e.tensor_single_scalar(out=g2[:, :], in_=gf[:, :], scalar=0.0, op=ALU.min), reads=[gf], writes=[g2])
    p.op('dve', lambda e: e.tensor_tensor(out=gf[:, :], in0=g2[:, :], in1=g1[:, :], op=ALU.subtract), reads=[g1, g2], writes=[gf])
    p.op('dve', lambda e: e.memset(gm[:, 0:1], 0.0), writes=[gm])
    p.op('dve', lambda e: e.tensor_tensor_scan(out=gm[:, 1:T_ALL + 1], data0=gf[:, :], data1=gi[:, :], initial=0.0, op0=ALU.add, op1=ALU.max),
         reads=[gf, gi, gm], writes=[gm])
    p.op('dve', lambda e: e.tensor_tensor(out=g1[:, :], in0=gf[:, :], in1=gm[:, 0:T_ALL], op=ALU.add), reads=[gf, gm], writes=[g1])
    p.op('dve', lambda e: e.tensor_tensor(out=g1[:, :], in0=g1[:, :], in1=gm[:, 1:T_ALL + 1], op=ALU.subtract), reads=[g1, gm], writes=[g1])
    p.op('act', lambda e: e.activation(out=g1[:, :], in_=g1[:, :], func=AF.Exp), reads=[g1], writes=[g1])
    p.dma('sp', S_GD, S_GD.h[:, :], g1, g1[:, :])
    p.op('dve', lambda e: e.tensor_tensor(out=g2[:, :], in0=gi[:, :], in1=gm[:, 1:T_ALL + 1], op=ALU.subtract), reads=[gi, gm], writes=[g2])
    p.op('act', lambda e: e.activation(out=g2[:, :], in_=g2[:, :], func=AF.Exp), reads=[g2], writes=[g2])
    p.op('act', lambda e: e.mul(g2[:, :], g2[:, :], 128.0 ** -0.5), reads=[g2], writes=[g2])
    p.dma('sp', S_GS, S_GS.h[:, :], g2, g2[:, :])
    p.op('act', lambda e: e.activation(out=g3[:, :], in_=gm[:, 1:T_ALL + 1], func=AF.Exp, scale=-1.0), reads=[gm], writes=[g3])
    items = [('sp', S_GE, S_GE.h[0:8, :], g3, g3[0:8, :])]
    for (c0, n, ap) in _proc_src(S_GE, 8, 8, T_ALL, True):
        items.append(('sp', S_GE, ap, g3, g3[8:16, c0:c0 + n]))
    p.dma_group(items, allow_slow_non_contiguous=True)

    wc = p.sbuf("wcs", [128, 16, 9])
    p.dma('sp', wc, wc[:, :, :], wc_d, wc_d.h[:, :, :])
    idt = p.sbuf("idt", [128, 128])
    p.dma('sp', idt, idt[:, :], ident, ident.h[:, :])
    xs = [p.sbuf(f"cx{i}", [128, T_ALL]) for i in range(2)]
    ys = [p.sbuf(f"cy{i}", [128, T_ALL]) for i in range(2)]
    pst = [p.psum(f"tps{i}", [128, 512], F32) for i in range(2)]
    tst = [p.sbuf(f"tst{i}", [128, 512]) for i in range(2)]
    CTX = NCTX
    tcnt = 0
    for ck in range(16):
        X = xs[ck % 2]; Y = ys[ck % 2]
        p.dma('sp', X, X[:, :], S_BQK, S_BQK.h[ck * 128:(ck + 1) * 128, :])
        wt = lambda i, j: wc[:, ck, i * 3 + j:i * 3 + j + 1]
        p.op('dve', lambda e: e.tensor_scalar(out=Y[:, :], in0=X[:, :], scalar1=wt(1, 1), scalar2=0.0, op0=ALU.mult, op1=ALU.add), reads=[X, wc], writes=[Y])
        p.op('dve', lambda e: e.scalar_tensor_tensor(out=Y[:, 1:CTX], in0=X[:, 0:CTX - 1], scalar=wt(1, 0), in1=Y[:, 1:CTX], op0=ALU.mult, op1=ALU.add),
             reads=[X, wc, Y], writes=[Y])
        p.op('dve', lambda e: e.scalar_tensor_tensor(out=Y[:, 0:CTX - 1], in0=X[:, 1:CTX], scalar=wt(1, 2), in1=Y[:, 0:CTX - 1], op0=ALU.mult, op1=ALU.add),
             reads=[X, wc, Y], writes=[Y])
        Xg = X.h[:, CTX:T_ALL].rearrange("p (r c) -> p r c", c=64)
        Yg = Y.h[:, CTX:T_ALL].rearrange("p (r c) -> p r c", c=64)
        for i in range(3):
            for j in range(3):
                if i == 1 and j == 1:
                    continue
                dr, dc = i - 1, j - 1
                r0, r1 = max(0, -dr), 32 - max(0, dr)
                c0, c1 = max(0, -dc), 64 - max(0, dc)
                p.op('dve', lambda e: e.scalar_tensor_tensor(out=Yg[:, r0:r1, c0:c1], in0=Xg[:, r0 + dr:r1 + dr, c0 + dc:c1 + dc], scalar=wt(i, j),
                                                             in1=Yg[:, r0:r1, c0:c1], op0=ALU.mult, op1=ALU.add), reads=[X, wc, Y], writes=[Y])
        p.op('act', lambda e: e.activation(out=Y[:, :], in_=Y[:, :], func=AF.Silu), reads=[Y], writes=[Y])
        for t4 in range(0, 18, 4):
            nt = min(4, 18 - t4)
            ps = pst[tcnt % 2]; st = tst[tcnt % 2]; tcnt += 1
            for j in range(nt):
                tt = t4 + j
                p.op('pe', lambda e: e.transpose(ps[:, j * 128:(j + 1) * 128], Y[:, tt * 128:(tt + 1) * 128], idt[:, :]), reads=[Y, idt], writes=[ps])
            p.op('act', lambda e: e.copy(st[:, :nt * 128], ps[:, :nt * 128]), reads=[ps], writes=[st])
            p.dma('sp', S_BQKtm, S_BQKtm.h[t4 * 128:(t4 + nt) * 128, ck * 128:(ck + 1) * 128].rearrange("(j p) c -> p j c", p=128),
                  st, st.h[:, :nt * 128].rearrange("p (j c) -> p j c", c=128))

    p.dma('sp', S_AQK, S_AQK.h[0:NCTX, :], S_P, S_P.h[0:NCTX, 0:2048])
    rx = [p.sbuf(f"rx{i}", [128, 16, 128]) for i in range(2)]
    ro = [p.sbuf(f"ro{i}", [128, 16, 128]) for i in range(2)]
    rt = [p.sbuf(f"rt{i}", [128, 16, 128]) for i in range(2)]
    rc = [p.sbuf(f"rc{i}", [128, 128]) for i in range(2)]
    rs = [p.sbuf(f"rs{i}", [128, 128]) for i in range(2)]
    SHR = [128, 16, 32]
    for ti in range(16):
        X, O, Tm, C, S = rx[ti % 2], ro[ti % 2], rt[ti % 2], rc[ti % 2], rs[ti % 2]
        r0 = NCTX + ti * 128
        p.dma('sp', X, X.h[:, :, :].rearrange("p h d -> p (h d)"), S_P, S_P.h[r0:r0 + 128, 0:2048])
        p.dma('sp', C, C[:, :], cos_d, cos_d.h[ti]); p.dma('sp', S, S[:, :], sin_d, sin_d.h[ti])
        p.op('dve', lambda e: e.tensor_tensor(out=O[:, :, :], in0=X[:, :, :], in1=C[:, :].unsqueeze(1).to_broadcast([128, 16, 128]), op=ALU.mult),
             reads=[X, C], writes=[O])
        for (dst, src) in ((0, 32), (32, 0), (64, 96), (96, 64)):
            p.op('pool', lambda e: e.tensor_tensor(out=Tm[:, :, dst:dst + 32], in0=X[:, :, src:src + 32],
                                                   in1=S[:, dst:dst + 32].unsqueeze(1).to_broadcast(SHR), op=ALU.mult), reads=[X, S], writes=[Tm])
        p.op('dve', lambda e: e.tensor_tensor(out=O[:, :, :], in0=O[:, :, :], in1=Tm[:, :, :], op=ALU.add), reads=[O, Tm], writes=[O])
        p.dma('sp', S_AQK, S_AQK.h[r0:r0 + 128, :], O, O.h[:, :, :].rearrange("p h d -> p (h d)"))
    p.end_stage()


def _chunk_rows(ci, TC, rev):
    tau0 = ci * TC
    if not rev:
        return tau0, 1
    if tau0 < NCTX:
        return NCTX - 1 - tau0, -1
    return NTK - 1 - (tau0 - NCTX), -1


def _rows_ap(dram_t, rowlen, t_start, sgn, TC, col0, ncols, n_outer=None, outer_stride=None):
    import concourse.bass as bass
    dims = []
    if n_outer is not None:
        dims.append([outer_stride, n_outer])
    dims.append([sgn * rowlen, TC])
    dims.append([1, ncols])
    return bass.AP(dram_t.h.tensor, t_start * rowlen + col0, dims)


def stage_scan(p, T, TC, NS, VB, NVL, NK, delta, rev, load_in, load_v, store_y, Dsrc=None, Ssrc=None):
    p.begin_stage()
    NIN = 5 if delta else 2
    SH = [128, NVL, NK]
    Sb = [p.sbuf(f"S{i}", SH, F32) for i in range(2)]
    T3 = [p.sbuf(f"T3{i}", SH, F32) for i in range(2)]
    T4 = p.sbuf("T4", SH, F32)
    if delta:
        T1 = p.sbuf("T1", SH, F32)
        T2 = p.sbuf("T2", SH, F32)
        sa = p.sbuf("sa", [128, NVL], F32)
    INc = [p.sbuf(f"IN{i}", [128, TC, NIN, NK], F32) for i in range(2)]
    Vc = [p.sbuf(f"V{i}", [128, TC, NVL], F32) for i in range(2)]
    Yc = [p.sbuf(f"Y{i}", [128, TC, NVL], F32) for i in range(2)]
    if not delta:
        Dc = p.sbuf("Dall", [128, T], F32)
        Sc = p.sbuf("Sall", [128, T], F32)
        Dsrc(Dc)
        Ssrc(Sc)
    p.op('dve', lambda e: e.memset(Sb[0][:, :, :], 0.0), writes=[Sb[0]])
    nch = T // TC

    def load(ci):
        load_in(INc[ci % 2], ci)
        load_v(Vc[ci % 2], ci)
        if not delta:
            Vl = Vc[ci % 2]
            p.op('dve', lambda e: e.tensor_tensor(out=Vl[:, :, :], in0=Vl[:, :, :], in1=Sc[:, ci * TC:(ci + 1) * TC].unsqueeze(2).to_broadcast([128, TC, NVL]),
                                                  op=ALU.mult), reads=[Vl, Sc], writes=[Vl])
    load(0)
    KI = 2 if delta else 1
    T4b = [T4, p.sbuf("T4b", SH, F32)]
    pend = None

    def emit_t3(tt):
        if tt >= T:
            return
        c2, l2 = tt // TC, tt % TC
        IN2, V2 = INc[c2 % 2], Vc[c2 % 2]
        tgt = T3[tt % 2]
        p.op('pool', lambda e: e.tensor_tensor(out=tgt[:, :, :], in0=V2[:, l2, :].unsqueeze(2).to_broadcast(SH),
                                               in1=IN2[:, l2, KI, :].unsqueeze(1).to_broadcast(SH), op=ALU.mult), reads=[V2, IN2], writes=[tgt])

    def pool_prev_t4():
        if pend is not None and pend[4] is not None:
            pend[4]()
            pend[4] = None

    def dve_prev_y():
        nonlocal pend
        if pend is not None:
            t4p, Yp, tlp, st_ci, _ = pend
            p.op('dve', lambda e: e.tensor_reduce(out=Yp[:, tlp, :], in_=t4p[:, :, :], axis=AX.X, op=ALU.add), reads=[t4p], writes=[Yp])
            if st_ci is not None:
                store_y(Yp, st_ci)
            pend = None
    for ci in range(nch):
        pool_prev_t4()
        if ci + 1 < nch:
            load(ci + 1)
        IN, V, Y = INc[ci % 2], Vc[ci % 2], Yc[ci % 2]
        for tl in range(TC):
            t = ci * TC + tl
            So, Sn = Sb[t % 2], Sb[(t + 1) % 2]
            t3 = T3[t % 2]
            t4 = T4b[t % 2]
            bc = lambda j, IN=IN, tl=tl: IN[:, tl, j, :].unsqueeze(1).to_broadcast(SH)
            if delta:
                if t == 0:
                    emit_t3(0)
                p.op('dve', lambda e: e.tensor_tensor(out=T1[:, :, :], in0=So[:, :, :], in1=bc(3), op=ALU.mult), reads=[So, IN], writes=[T1])
                p.op('pool', lambda e: e.tensor_tensor(out=Sn[:, :, :], in0=So[:, :, :], in1=bc(1), op=ALU.mult), reads=[So, IN], writes=[Sn])
                pool_prev_t4()
                emit_t3(t + 1)
                p.op('dve', lambda e: e.tensor_reduce(out=sa[:, :], in_=T1[:, :, :], axis=AX.X, op=ALU.add), reads=[T1], writes=[sa])
                p.op('dve', lambda e: e.tensor_tensor(out=T2[:, :, :], in0=sa[:, :].unsqueeze(2).to_broadcast(SH), in1=bc(4), op=ALU.mult),
                     reads=[sa, IN], writes=[T2])
                p.op('dve', lambda e: e.tensor_tensor(out=Sn[:, :, :], in0=Sn[:, :, :], in1=T2[:, :, :], op=ALU.add), reads=[Sn, T2], writes=[Sn])
                p.op('dve', lambda e: e.tensor_tensor(out=Sn[:, :, :], in0=Sn[:, :, :], in1=t3[:, :, :], op=ALU.add), reads=[Sn, t3], writes=[Sn])
                dve_prev_y()

                def emit_t4(Sn=Sn, t4=t4, IN=IN, tl=tl):
                    p.op('pool', lambda e: e.tensor_tensor(out=t4[:, :, :], in0=Sn[:, :, :], in1=IN[:, tl, 0, :].unsqueeze(1).to_broadcast(SH), op=ALU.mult),
                         reads=[Sn, IN], writes=[t4])
                pend = [t4, Y, tl, ci if tl == TC - 1 else None, emit_t4]
            else:
                if t == 0:
                    emit_t3(0)
                emit_t3(t + 1)
                p.op('dve', lambda e: e.scalar_tensor_tensor(out=Sn[:, :, :], in0=So[:, :, :], scalar=Dc[:, t:t + 1], in1=t3[:, :, :],
                                                             op0=ALU.mult, op1=ALU.add), reads=[So, Dc, t3], writes=[Sn])
                p.op('dve', lambda e: e.tensor_tensor(out=t4[:, :, :], in0=Sn[:, :, :], in1=bc(0), op=ALU.mult), reads=[Sn, IN], writes=[t4])
                dve_prev_y()
                pend = [t4, Y, tl, ci if tl == TC - 1 else None, None]
    pool_prev_t4()
    dve_prev_y()
    p.end_stage()


def scans_l0(p, S_AQK, S_BQKtm, S_P, S_GD, S_GS, ret_ds, S_Y0):
    TC, NS, VB, NVL, NK = 32, 16, 8, 33, 128
    for d in range(2):
        rev = (d == 1)

        def load_in(IN, ci, rev=rev):
            t0, sg = _chunk_rows(ci, TC, rev)
            items = []
            for vb in range(VB):
                for j in range(2):
                    items.append(('sp', IN, IN[vb * NS:vb * NS + 8, :, j, :], S_AQK, _rows_ap(S_AQK, 2048, t0, sg, TC, j * 1024, 128, 8, 128)))
                    items.append(('sp', IN, IN[vb * NS + 8:vb * NS + 16, :, j, :], S_BQKtm, _rows_ap(S_BQKtm, 2048, t0, sg, TC, j * 1024, 128, 8, 128)))
            p.dma_group(items)

        def load_v(V, ci, rev=rev):
            t0, sg = _chunk_rows(ci, TC, rev)
            p.op('pool', lambda e: e.memset(V[:, :, 25:33], 0.0), writes=[V])
            p.op('pool', lambda e: e.memset(V[:, :, 25:26], 1.0), writes=[V])
            items = []
            for vb in range(VB):
                n = 33 if vb < 7 else 25
                items.append(('sp', V, V[vb * NS:vb * NS + 8, :, 0:n], S_P, _rows_ap(S_P, 10240, t0, sg, TC, 2048 + vb * 33, n, 8, 256)))
                items.append(('sp', V, V[vb * NS + 8:vb * NS + 16, :, 0:n], S_P, _rows_ap(S_P, 10240, t0, sg, TC, 6144 + vb * 33, n, 8, 256)))
            p.dma_group(items)

        def store_y(Y, ci, rev=rev, d=d):
            t0, sg = _chunk_rows(ci, TC, rev)
            items = []
            for vb in range(VB):
                items.append(('sp', S_Y0, _rows_ap(S_Y0, 16 * 264, d * NTK + t0, sg, TC, vb * 33, 33, 16, 264), Y, Y[vb * NS:(vb + 1) * NS, :, :]))
            p.dma_group(items)

        def Dsrc(Dc, d=d):
            items = []
            for vb in range(VB):
                items.append(('sp', Dc, Dc[vb * NS:vb * NS + 8, :], ret_ds, ret_ds.h[d, 0, :, :]))
                items.append(('sp', Dc, Dc[vb * NS + 8:vb * NS + 16, :], S_GD, S_GD.h[d * 8:(d + 1) * 8, :]))
            p.dma_group(items)

        def Ssrc(Sc, d=d):
            items = []
            for vb in range(VB):
                items.append(('sp', Sc, Sc[vb * NS:vb * NS + 8, :], ret_ds, ret_ds.h[d, 1, :, :]))
                items.append(('sp', Sc, Sc[vb * NS + 8:vb * NS + 16, :], S_GS, S_GS.h[d * 8:(d + 1) * 8, :]))
            p.dma_group(items)
        stage_scan(p, NTK, TC, NS, VB, NVL, NK, False, rev, load_in, load_v, store_y, Dsrc, Ssrc)


def scans_l1(p, S_R, S_W, S_KD, S_NKK, S_Bb, S_V, S_Y1):
    TC, NS, VB, NVL, NK = 32, 64, 2, 32, 64
    for d in range(2):
        rev = (d == 1)
        srcs = [S_R, S_W[d], S_KD[d], S_NKK, S_Bb[d]]

        def load_in(IN, ci, rev=rev, srcs=srcs):
            t0, sg = _chunk_rows(ci, TC, rev)
            items = []
            for vb in range(VB):
                for j in range(5):
                    items.append(('sp', IN, IN[vb * NS:(vb + 1) * NS, :, j, :], srcs[j], _rows_ap(srcs[j], D, t0, sg, TC, 0, 64, 64, 64)))
            p.dma_group(items)

        def load_v(V, ci, rev=rev):
            t0, sg = _chunk_rows(ci, TC, rev)
            p.dma_group([('sp', V, V[vb * NS:(vb + 1) * NS, :, :], S_V, _rows_ap(S_V, D, t0, sg, TC, vb * 32, 32, 64, 64)) for vb in range(VB)])

        def store_y(Y, ci, rev=rev, d=d):
            t0, sg = _chunk_rows(ci, TC, rev)
            p.dma_group([('sp', S_Y1, _rows_ap(S_Y1, D, d * NTK + t0, sg, TC, vb * 32, 32, 64, 64), Y, Y[vb * NS:(vb + 1) * NS, :, :]) for vb in range(VB)])
        stage_scan(p, NTK, TC, NS, VB, NVL, NK, True, rev, load_in, load_v, store_y)


def tm_head_rms(p, X, H, DH, sq, ss, eps_t):
    p.op('pool', lambda e: e.tensor_tensor(out=sq[:, :, :], in0=X[:, :, :], in1=X[:, :, :], op=ALU.mult), reads=[X], writes=[sq])
    p.op('dve', lambda e: e.tensor_reduce(out=ss[:, :], in_=sq[:, :, :], axis=AX.X, op=ALU.add), reads=[sq], writes=[ss])
    p.op('act', lambda e: e.activation(out=ss[:, :], in_=ss[:, :], func=AF.Sqrt, bias=eps_t[:, 0:1], scale=1.0 / DH), reads=[ss, eps_t], writes=[ss])
    p.op('dve', lambda e: e.reciprocal(out=ss[:, :], in_=ss[:, :]), reads=[ss], writes=[ss])
    p.op('dve', lambda e: e.tensor_tensor(out=X[:, :, :], in0=X[:, :, :], in1=ss[:, :].unsqueeze(2).to_broadcast([128, H, DH]), op=ALU.mult),
         reads=[X, ss], writes=[X])


def stage_c3(p, S_Y0, S_GE, S_P, gr_d, gm_d, S_MIX):
    import concourse.bass as bass
    p.begin_stage()
    gr = p.sbuf("gr", [128, 2048]); gm = p.sbuf("gmm", [128, 2048])
    p.dma('sp', gr, gr[:, :], gr_d, gr_d.h[:, :]); p.dma('sp', gm, gm[:, :], gm_d, gm_d.h[:, :])
    eps_t = p.sbuf("eps", [128, 1]); p.op('dve', lambda e: e.memset(eps_t[:, :], 1e-6), writes=[eps_t])
    Yt = [[p.sbuf(f"Yt{i}{d}", [128, 16, 264]) for d in range(2)] for i in range(2)]
    EM = [[p.sbuf(f"EM{i}{d}", [128, 8]) for d in range(2)] for i in range(2)]
    A = p.sbuf("A", [128, 8, 256]); Bt = p.sbuf("Bt", [128, 8, 256])
    Gt = [p.sbuf(f"G{i}", [128, 2048]) for i in range(2)]
    O = [p.sbuf(f"O{i}", [128, 4096]) for i in range(2)]
    sq = p.sbuf("sq", [128, 8, 256]); ss = p.sbuf("ss", [128, 8]); dn = p.sbuf("dn", [128, 8])
    af = A.h[:, :, :].rearrange("p h d -> p (h d)")
    for ti in range(18):
        r0 = ti * 128
        o, g = O[ti % 2], Gt[ti % 2]
        ys = Yt[ti % 2]
        for d in range(2):
            p.dma('sp', ys[d], ys[d].h[:, :, :].rearrange("p s c -> p (s c)"), S_Y0, S_Y0.h[d * NTK + r0:d * NTK + r0 + 128, :])
            em = EM[ti % 2][d]
            src = bass.AP(S_GE.h.tensor, d * 8 * NTK + r0, [[1, 128], [NTK, 8]])
            p.dma('sp', em, em[:, :], S_GE, src, allow_slow_non_contiguous=True)
        p.op('dve', lambda e: e.tensor_tensor(out=A[:, :, :], in0=ys[0][:, 0:8, 0:256], in1=ys[1][:, 0:8, 0:256], op=ALU.add), reads=[ys[0], ys[1]], writes=[A])
        tm_head_rms(p, A, 8, 256, sq, ss, eps_t)
        p.dma('sp', g, g[:, :], S_P, S_P.h[r0:r0 + 128, 4096:6144])
        p.op('act', lambda e: e.activation(out=g[:, :], in_=g[:, :], func=AF.Silu), reads=[g], writes=[g])
        p.op('dve', lambda e: e.tensor_tensor(out=af, in0=af, in1=gr[:, :], op=ALU.mult), reads=[A, gr], writes=[A])
        p.op('dve', lambda e: e.tensor_tensor(out=o[:, 0:2048], in0=af, in1=g[:, :], op=ALU.mult), reads=[A, g], writes=[o])
        for d in range(2):
            m, em = ys[d], EM[ti % 2][d]
            p.op('act', lambda e: e.activation(out=dn[:, :], in_=m[:, 8:16, 256], func=AF.Abs), reads=[m], writes=[dn])
            p.op('dve', lambda e: e.tensor_tensor(out=dn[:, :], in0=dn[:, :], in1=em[:, :], op=ALU.max), reads=[dn, em], writes=[dn])
            p.op('dve', lambda e: e.reciprocal(out=dn[:, :], in_=dn[:, :]), reads=[dn], writes=[dn])
            tgt = A if d == 0 else Bt
            p.op('dve', lambda e: e.tensor_tensor(out=tgt[:, :, :], in0=m[:, 8:16, 0:256], in1=dn[:, :].unsqueeze(2).to_broadcast([128, 8, 256]), op=ALU.mult),
                 reads=[m, dn], writes=[tgt])
        p.op('dve', lambda e: e.tensor_tensor(out=A[:, :, :], in0=A[:, :, :], in1=Bt[:, :, :], op=ALU.add), reads=[A, Bt], writes=[A])
        tm_head_rms(p, A, 8, 256, sq, ss, eps_t)
        g2 = Gt[(ti + 1) % 2]
        p.dma('sp', g2, g2[:, :], S_P, S_P.h[r0:r0 + 128, 8192:10240])
        p.op('act', lambda e: e.activation(out=g2[:, :], in_=g2[:, :], func=AF.Sigmoid), reads=[g2], writes=[g2])
        p.op('dve', lambda e: e.tensor_tensor(out=af, in0=af, in1=gm[:, :], op=ALU.mult), reads=[A, gm], writes=[A])
        p.op('dve', lambda e: e.tensor_tensor(out=o[:, 2048:4096], in0=af, in1=g2[:, :], op=ALU.mult), reads=[A, g2], writes=[o])
        p.dma('sp', S_MIX, S_MIX.h[r0:r0 + 128, :], o, o[:, :])
    p.end_stage()


def stage_post(p, l, groups, S_MIX, h_src, modv, ng, w_out, w1, w2, ident, hmid, h_out, h_out_tok0, u1_out=None):
    p.begin_stage()
    c = FCtx(p)
    V = make_vecs(p, modv, ng, l)
    if u1_out is not None:
        V1 = make_vecs(p, modv, ng, l + 1)
    idt = p.sbuf("idt", [128, 128])
    p.dma('sp', idt, idt[:, :], ident, ident.h[:, :])
    B = p.sbuf("B", [128, KC, GMAX], F32)
    U = p.sbuf("U", [128, KC, GMAX], BF16)
    hT = [p.sbuf(f"hT{i}", [128, 2, GMAX], BF16) for i in range(2)]
    rl = [p.sbuf(f"rl{i}", [128, GMAX], F32) for i in range(2)]
    for (tok0, G, si) in groups:
        c.G = G
        for ti, t0 in enumerate(range(0, G, 128)):
            tsz = min(128, G - t0)
            Xap = B.h[:, :, :].rearrange("p k g -> p (k g)")[:, ti * 4096:(ti + 1) * 4096]
            p.dma('sp', B, Xap[:tsz, :], S_MIX, S_MIX.h[tok0 + t0:tok0 + t0 + tsz, :])
            for k4 in range(0, KC, 4):
                ps = c.nxt("ps")
                for j in range(4):
                    kc = k4 + j
                    p.op('pe', lambda e: e.transpose(ps[:, j * 128:j * 128 + tsz], Xap[:tsz, kc * 128:(kc + 1) * 128], idt[:tsz, :tsz]),
                         reads=[B, idt], writes=[ps])
                src = ps.h[:, :].rearrange("p (j t) -> p j t", j=4)[:, :, :tsz]
                if (k4 // 4) % 2 == 0:
                    p.op('act', lambda e: e.copy(U[:, k4:k4 + 4, t0:t0 + tsz], src), reads=[ps], writes=[U])
                else:
                    p.op('dve', lambda e: e.tensor_copy(U[:, k4:k4 + 4, t0:t0 + tsz], src), reads=[ps], writes=[U])
        def cbB(ps, fc, msz):
            p.op('act', lambda e: e.copy(B[:, fc, :G], ps[:, :G]), reads=[ps], writes=[B])
        f_gemm_fm(c, U, w_out, KC, 0, D, cbB)
        f_rstd(c, B)
        f_resid(c, B, V["A_mix"][si], h_src, tok0)
        f_rstd(c, B)
        f_modulate(c, B, lambda kc: U[:, kc, :G], U, V["GP_ff"][si], V["sh_ff"][si])
        p.dma('sp', hmid, hmid.h[:, tok0:tok0 + G].rearrange("(k p) t -> p k t", p=128), B, B[:, :, :G])
        f_mlp(c, U, B, w1, w2, DFF, hT, rl)
        f_rstd(c, B)
        f_resid(c, B, V["A_ff"][si], hmid, tok0)
        oc = tok0 - h_out_tok0
        p.dma('sp', h_out, h_out.h[:, oc:oc + G].rearrange("(k p) t -> p k t", p=128), B, B[:, :, :G])
        if u1_out is not None:
            f_rstd(c, B)
            for kc in range(KC):
                st = c.nxt("stg")
                tmp = c.nxt("tmp")
                gp, sh = V1["GP_mix"][si], V1["sh_mix"][si]
                p.op('dve', lambda e: e.scalar_tensor_tensor(out=tmp[:, :G], in0=B[:, kc, :G], scalar=gp[1](kc), in1=c.rbc[:, :G], op0=ALU.mult, op1=ALU.mult),
                     reads=[B, gp[0], c.rbc], writes=[tmp])
                p.op('act', lambda e: e.activation(out=st[:, :G], in_=tmp[:, :G], func=AF.Identity, bias=sh[1](kc), scale=1.0), reads=[tmp, sh[0]], writes=[st])
                p.dma('sp', u1_out, u1_out.h[kc * 128:(kc + 1) * 128, tok0:tok0 + G], st, st[:, :G])
    p.end_stage()


def stage_e(p, S_U1, mu_d, w_rkv, w1, w2, a1, a2, g1, g2, OUT):
    p.begin_stage()
    G = 256
    c = FCtx(p, gmax=G)
    c.G = G
    mu = p.sbuf("mu_s", [128, 6, KC])
    p.dma('sp', mu, mu[:, :, :], mu_d, mu_d.h[:, :, :])
    B1 = p.sbuf("B1", [128, KC, G], F32)
    B2 = p.sbuf("B2", [128, KC, G], F32)
    U = p.sbuf("U", [128, KC, G], BF16)
    L = p.sbuf("L", [128, 4, G], BF16)
    for gi in range(9):
        tok0 = gi * G
        p.dma('sp', B1, B1[:, :, :], S_U1, S_U1.h[:, tok0:tok0 + G].rearrange("(k p) t -> p k t", p=128))
        if gi == 0:
            p.op('act', lambda e: e.copy(B2[:, 0:16, 1:G], B1[:, 0:16, 0:G - 1]), reads=[B1], writes=[B2])
            p.op('pool', lambda e: e.memset(B2[:, 0:16, 0:1], 0.0), writes=[B2])
            p.op('act', lambda e: e.copy(B2[:, 16:32, 0:G - 1], B1[:, 16:32, 1:G]), reads=[B1], writes=[B2])
            p.op('pool', lambda e: e.memset(B2[:, 16:32, G - 1:G], 0.0), writes=[B2])
        else:
            g = gi - 1
            v4 = lambda t_, k0, k1: t_.h[:, k0:k1, :].rearrange("p k (r c) -> p k r c", c=64)
            p.op('act', lambda e: e.copy(B2[:, 0:8, 1:G], B1[:, 0:8, 0:G - 1]), reads=[B1], writes=[B2])
            p.op('pool', lambda e: e.memset(v4(B2, 0, 8)[:, :, :, 0:1], 0.0), reads=[B2], writes=[B2])
            p.op('act', lambda e: e.copy(B2[:, 8:16, 0:G - 1], B1[:, 8:16, 1:G]), reads=[B1], writes=[B2])
            p.op('pool', lambda e: e.memset(v4(B2, 8, 16)[:, :, :, 63:64], 0.0), reads=[B2], writes=[B2])
            p.op('act', lambda e: e.copy(B2[:, 16:24, 64:G], B1[:, 16:24, 0:G - 64]), reads=[B1], writes=[B2])
            if g == 0:
                p.op('pool', lambda e: e.memset(B2[:, 16:24, 0:64], 0.0), reads=[B2], writes=[B2])
            else:
                p.dma('sp', B2, B2[:, 16:24, 0:64], S_U1, S_U1.h[16 * 128:24 * 128, tok0 - 64:tok0].rearrange("(k p) t -> p k t", p=128))
            p.op('act', lambda e: e.copy(B2[:, 24:32, 0:G - 64], B1[:, 24:32, 64:G]), reads=[B1], writes=[B2])
            if g == 7:
                p.op('pool', lambda e: e.memset(B2[:, 24:32, G - 64:G], 0.0), reads=[B2], writes=[B2])
            else:
                p.dma('sp', B2, B2[:, 24:32, G - 64:G], S_U1, S_U1.h[24 * 128:32 * 128, tok0 + G:tok0 + G + 64].rearrange("(k p) t -> p k t", p=128))
        p.op('dve', lambda e: e.tensor_tensor(out=B2[:, :, :], in0=B2[:, :, :], in1=B1[:, :, :], op=ALU.subtract), reads=[B1, B2], writes=[B2])

        def mix(i):
            for kc in range(KC):
                eng = 'dve'
                p.op(eng, lambda e: e.scalar_tensor_tensor(out=U[:, kc, :], in0=B2[:, kc, :], scalar=mu[:, i, kc:kc + 1], in1=B1[:, kc, :],
                                                           op0=ALU.mult, op1=ALU.add), reads=[B1, B2, mu], writes=[U])

        def lora_cb(func):
            def cb(ps, fc, msz):
                p.op('act', lambda e: e.activation(out=L[:msz, fc, :], in_=ps[:msz, :G], func=func), reads=[ps], writes=[L])
            return cb
        mix(0); f_gemm_tm(c, U, w_rkv[0], KC, 0, D, OUT["r"], tok0)
        mix(1)
        for d in range(2):
            f_gemm_fm(c, U, w1[d], KC, 0, 128, lora_cb(AF.Tanh))
            f_gemm_tm(c, L, w2[d], 1, 0, D, OUT[f"wl{d}"], tok0)
        mix(2); f_gemm_tm(c, U, w_rkv[1], KC, 0, D, OUT["k"], tok0)
        mix(3); f_gemm_tm(c, U, w_rkv[2], KC, 0, D, OUT["v"], tok0)
        mix(4)
        for d in range(2):
            f_gemm_fm(c, U, a1[d], KC, 0, 128, lora_cb(AF.Identity))
            f_gemm_tm(c, L, a2[d], 1, 0, D, OUT[f"al{d}"], tok0)
        mix(5)
        f_gemm_fm(c, U, g1, KC, 0, 512, lora_cb(AF.Sigmoid))
        f_gemm_tm(c, L, g2, 4, 0, D, OUT["gate"], tok0)
    p.end_stage()


def stage_e2(p, INS, vec_d, OUTS):
    p.begin_stage()
    names_in = ["r", "k", "v", "wl0", "wl1", "al0", "al1"]
    names_out = ["w0o", "w1o", "kd0", "kd1", "nkk", "b0", "b1", "bonus"]
    HW, NH = 1024, 16
    vt = [p.sbuf(f"vec{i}", [128, HW]) for i in range(7)]
    tin = {n: p.sbuf("i_" + n, [128, HW]) for n in names_in}
    tout = {n: p.sbuf("o_" + n, [128, HW]) for n in names_out}
    a_t = [p.sbuf(f"a{d}", [128, HW]) for d in range(2)]
    kk = p.sbuf("kk", [128, HW]); sq = p.sbuf("sqq", [128, HW]); ss = p.sbuf("ss", [128, NH]); tmp = p.sbuf("tmpp", [128, HW])
    C_DEC = -float(np.exp(-0.5))
    for part in range(4):
        c0 = part * HW
        for i in range(7):
            p.dma('sp', vt[i], vt[i][:, :], vec_d, vec_d.h[i, :, c0:c0 + HW])
        k_k, k_a, r_k, w0, a0 = vt[0], vt[1], vt[2], [vt[3], vt[4]], [vt[5], vt[6]]
        for ti in range(18):
            r0 = ti * 128
            for n in names_in:
                p.dma('sp', tin[n], tin[n][:, :], INS[n], INS[n].h[r0:r0 + 128, c0:c0 + HW])
            k = tin["k"]
            p.op('dve', lambda e: e.tensor_tensor(out=kk[:, :], in0=k[:, :], in1=k_k[:, :], op=ALU.mult), reads=[k, k_k], writes=[kk])
            p.op('pool', lambda e: e.tensor_tensor(out=sq[:, :], in0=kk[:, :], in1=kk[:, :], op=ALU.mult), reads=[kk], writes=[sq])
            p.op('dve', lambda e: e.tensor_reduce(out=ss[:, :], in_=sq.h[:, :].rearrange("p (h n) -> p h n", n=64), axis=AX.X, op=ALU.add), reads=[sq], writes=[ss])
            p.op('dve', lambda e: e.tensor_single_scalar(out=ss[:, :], in_=ss[:, :], scalar=1e-24, op=ALU.max), reads=[ss], writes=[ss])
            p.op('act', lambda e: e.activation(out=ss[:, :], in_=ss[:, :], func=AF.Sqrt), reads=[ss], writes=[ss])
            p.op('dve', lambda e: e.reciprocal(out=ss[:, :], in_=ss[:, :]), reads=[ss], writes=[ss])
            kk3 = kk.h[:, :].rearrange("p (h n) -> p h n", n=64)
            p.op('dve', lambda e: e.tensor_tensor(out=kk3, in0=kk3, in1=ss[:, :].unsqueeze(2).to_broadcast([128, NH, 64]), op=ALU.mult), reads=[kk, ss], writes=[kk])
            p.op('act', lambda e: e.mul(tout["nkk"][:, :], kk[:, :], -1.0), reads=[kk], writes=[tout["nkk"]])
            p.dma('sp', OUTS["nkk"], OUTS["nkk"].h[r0:r0 + 128, c0:c0 + HW], tout["nkk"], tout["nkk"][:, :])
            for d in range(2):
                wl, al = tin[f"wl{d}"], tin[f"al{d}"]
                wo, kd, bo = tout[f"w{d}o"], tout[f"kd{d}"], tout[f"b{d}"]
                p.op('dve', lambda e: e.tensor_tensor(out=wl[:, :], in0=wl[:, :], in1=w0[d][:, :], op=ALU.add), reads=[wl, w0[d]], writes=[wl])
                p.op('act', lambda e: e.activation(out=wl[:, :], in_=wl[:, :], func=AF.Sigmoid), reads=[wl], writes=[wl])
                p.op('act', lambda e: e.activation(out=wo[:, :], in_=wl[:, :], func=AF.Exp, scale=C_DEC), reads=[wl], writes=[wo])
                p.dma('sp', OUTS[f"w{d}o"], OUTS[f"w{d}o"].h[r0:r0 + 128, c0:c0 + HW], wo, wo[:, :])
                p.op('dve', lambda e: e.tensor_tensor(out=al[:, :], in0=al[:, :], in1=a0[d][:, :], op=ALU.add), reads=[al, a0[d]], writes=[al])
                p.op('act', lambda e: e.activation(out=a_t[d][:, :], in_=al[:, :], func=AF.Sigmoid), reads=[al], writes=[a_t[d]])
                p.op('dve', lambda e: e.scalar_tensor_tensor(out=tmp[:, :], in0=a_t[d][:, :], scalar=-1.0, in1=k_a[:, :], op0=ALU.add, op1=ALU.mult),
                     reads=[a_t[d], k_a], writes=[tmp])
                p.op('dve', lambda e: e.scalar_tensor_tensor(out=kd[:, :], in0=tmp[:, :], scalar=1.0, in1=k[:, :], op0=ALU.add, op1=ALU.mult),
                     reads=[tmp, k], writes=[kd])
                p.dma('sp', OUTS[f"kd{d}"], OUTS[f"kd{d}"].h[r0:r0 + 128, c0:c0 + HW], kd, kd[:, :])
                p.op('pool', lambda e: e.tensor_tensor(out=bo[:, :], in0=kk[:, :], in1=a_t[d][:, :], op=ALU.mult), reads=[kk, a_t[d]], writes=[bo])
                p.dma('sp', OUTS[f"b{d}"], OUTS[f"b{d}"].h[r0:r0 + 128, c0:c0 + HW], bo, bo[:, :])
            p.op('dve', lambda e: e.tensor_tensor(out=tmp[:, :], in0=tout["kd0"][:, :], in1=tout["kd1"][:, :], op=ALU.add), reads=[tout["kd0"], tout["kd1"]], writes=[tmp])
            p.op('dve', lambda e: e.scalar_tensor_tensor(out=tmp[:, :], in0=tmp[:, :], scalar=0.5, in1=r_k[:, :], op0=ALU.mult, op1=ALU.mult),
                 reads=[tmp, r_k], writes=[tmp])
            p.op('dve', lambda e: e.tensor_tensor(out=tmp[:, :], in0=tmp[:, :], in1=tin["r"][:, :], op=ALU.mult), reads=[tmp, tin["r"]], writes=[tmp])
            p.op('dve', lambda e: e.tensor_reduce(out=ss[:, :], in_=tmp.h[:, :].rearrange("p (h n) -> p h n", n=64), axis=AX.X, op=ALU.add), reads=[tmp], writes=[ss])
            bn = tout["bonus"]
            p.op('dve', lambda e: e.tensor_tensor(out=bn.h[:, :].rearrange("p (h n) -> p h n", n=64), in0=tin["v"].h[:, :].rearrange("p (h n) -> p h n", n=64),
                                                  in1=ss[:, :].unsqueeze(2).to_broadcast([128, NH, 64]), op=ALU.mult), reads=[tin["v"], ss], writes=[bn])
            p.dma('sp', OUTS["bonus"], OUTS["bonus"].h[r0:r0 + 128, c0:c0 + HW], bn, bn[:, :])
    p.end_stage()


def stage_g0(p, S_Y1, S_BONUS, S_GATE, vec_d, S_MIX):
    p.begin_stage()
    HW, NH = 2048, 32
    lw = p.sbuf("lw", [128, HW]); lb = p.sbuf("lb", [128, HW])
    names = ["yf", "yb", "bonus", "gate"]
    t = {n: [p.sbuf(f"g_{n}{i}", [128, HW]) for i in range(2)] for n in names}
    sq = p.sbuf("sq", [128, HW]); mean = p.sbuf("mean", [128, NH]); var = p.sbuf("var", [128, NH])
    epsl = p.sbuf("epsl", [128, 1]); p.op('dve', lambda e: e.memset(epsl[:, :], 64e-5), writes=[epsl])
    it = 0
    SH3 = [128, NH, 64]
    v3 = lambda tt: tt.h[:, :].rearrange("p (h n) -> p h n", n=64)
    for half in range(2):
        c0 = half * HW
        p.dma('sp', lw, lw[:, :], vec_d, vec_d.h[0, :, c0:c0 + HW]); p.dma('sp', lb, lb[:, :], vec_d, vec_d.h[1, :, c0:c0 + HW])
        for ti in range(16):
            r0 = NCTX + ti * 128
            cur = {n: t[n][it % 2] for n in t}; it += 1
            p.dma('sp', cur["yf"], cur["yf"][:, :], S_Y1, S_Y1.h[r0:r0 + 128, c0:c0 + HW])
            p.dma('sp', cur["yb"], cur["yb"][:, :], S_Y1, S_Y1.h[NTK + r0:NTK + r0 + 128, c0:c0 + HW])
            p.dma('sp', cur["bonus"], cur["bonus"][:, :], S_BONUS, S_BONUS.h[r0:r0 + 128, c0:c0 + HW])
            p.dma('sp', cur["gate"], cur["gate"][:, :], S_GATE, S_GATE.h[r0:r0 + 128, c0:c0 + HW])
            y = cur["yf"]
            p.op('dve', lambda e: e.tensor_tensor(out=y[:, :], in0=y[:, :], in1=cur["yb"][:, :], op=ALU.add), reads=[y, cur["yb"]], writes=[y])
            p.op('dve', lambda e: e.tensor_reduce(out=mean[:, :], in_=v3(y), axis=AX.X, op=ALU.add), reads=[y], writes=[mean])
            p.op('act', lambda e: e.mul(mean[:, :], mean[:, :], 1.0 / 64), reads=[mean], writes=[mean])
            p.op('dve', lambda e: e.tensor_tensor(out=v3(y), in0=v3(y), in1=mean[:, :].unsqueeze(2).to_broadcast(SH3), op=ALU.subtract), reads=[y, mean], writes=[y])
            p.op('pool', lambda e: e.tensor_tensor(out=sq[:, :], in0=y[:, :], in1=y[:, :], op=ALU.mult), reads=[y], writes=[sq])
            p.op('dve', lambda e: e.tensor_reduce(out=var[:, :], in_=v3(sq), axis=AX.X, op=ALU.add), reads=[sq], writes=[var])
            p.op('act', lambda e: e.activation(out=var[:, :], in_=var[:, :], func=AF.Sqrt, bias=epsl[:, 0:1], scale=1.0 / 64), reads=[var, epsl], writes=[var])
            p.op('dve', lambda e: e.reciprocal(out=var[:, :], in_=var[:, :]), reads=[var], writes=[var])
            p.op('dve', lambda e: e.tensor_tensor(out=v3(y), in0=v3(y), in1=var[:, :].unsqueeze(2).to_broadcast(SH3), op=ALU.mult), reads=[y, var], writes=[y])
            p.op('dve', lambda e: e.tensor_tensor(out=y[:, :], in0=y[:, :], in1=lw[:, :], op=ALU.mult), reads=[y, lw], writes=[y])
            p.op('pool', lambda e: e.tensor_tensor(out=y[:, :], in0=y[:, :], in1=lb[:, :], op=ALU.add), reads=[y, lb], writes=[y])
            p.op('dve', lambda e: e.tensor_tensor(out=y[:, :], in0=y[:, :], in1=cur["bonus"][:, :], op=ALU.add), reads=[y, cur["bonus"]], writes=[y])
            p.op('dve', lambda e: e.tensor_tensor(out=y[:, :], in0=y[:, :], in1=cur["gate"][:, :], op=ALU.mult), reads=[y, cur["gate"]], writes=[y])
            p.dma('sp', S_MIX, S_MIX.h[r0:r0 + 128, c0:c0 + HW], y, y[:, :])
    p.end_stage()


CHUNKED_L0 = True


def build_fused(upto=99):
    p = Prog()
    I = lambda n, s: p.dram(n, s)
    S = lambda n, s: p.dram(n, s, kind="Internal")
    xT = I("xT", [D, NTK])
    condT = I("condT", [128, KC, 2])
    adaw = [I("adaw0", [D, 6 * D]), I("adaw1", [D, 6 * D])]
    adab = I("adab", [128, 2, 192])
    ng_d = I("ng", [128, 2, 4, KC])
    w_tm = I("w_tm", [D, 10240]); w_bqk = I("w_bqk", [D, 2048]); w_gi = I("w_gi", [D, 16]); w_gf = I("w_gf", [D, 16])
    gbias = I("gbias", [16, 2]); wc_d = I("wc", [128, 16, 9])
    cos_d = I("cos", [16, 128, 128]); sin_d = I("sins", [16, 128, 128]); ident = I("ident", [128, 128])
    ret_ds = I("ret_ds", [2, 2, 8, NTK])
    retlf_d = I("retlf", [16, NTK]); masks_d = I("masks", [2, 128, 128])
    gr_d = I("gn_ret", [128, 2048]); gm_d = I("gn_ml", [128, 2048])
    w_out0 = I("w_out0", [D, D]); w1_0 = I("w1_0", [D, DFF]); w2_0 = I("w2_0", [DFF, D])
    mu_d = I("mu", [128, 6, KC])
    w_rkv = [I(f"w_rkv{i}", [D, D]) for i in range(3)]
    lw1 = [I(f"lw1_{d}", [D, 128]) for d in range(2)]; lw2 = [I(f"lw2_{d}", [128, D]) for d in range(2)]
    la1 = [I(f"la1_{d}", [D, 128]) for d in range(2)]; la2 = [I(f"la2_{d}", [128, D]) for d in range(2)]
    g1p = I("g1p", [D, 512]); g2p = I("g2p", [512, D])
    vecs_e2 = I("vecs_e2", [7, 128, D]); vecs_g0 = I("vecs_g0", [2, 128, D])
    w_out1 = I("w_out1", [D, D]); w1_1 = I("w1_1", [D, DFF]); w2_1 = I("w2_1", [DFF, D])
    outT = p.dram("outT", [D, 2048], kind="ExternalOutput")

    S_P = S("S_P", [NTK, 10240]); S_BQK = S("S_BQK", [2048, NTK]); S_GI = S("S_GI", [16, NTK]); S_GF = S("S_GF", [16, NTK])
    S_GD = S("S_GD", [16, NTK]); S_GS = S("S_GS", [16, NTK]); S_GE = S("S_GE", [16, NTK])
    SU = S("SU", [48, NTK]); SW = S("SW", [48, NTK]); SA = S("SA", [48, NTK]); SK = S("SK", [48, NTK]); SD = S("SD", [48, NCH])
    SCOL = S("SCOL", [NTK, 144])
    S_BQKtm = S("S_BQKtm", [NTK, 2048]); S_AQK = S("S_AQK", [NTK, 2048])
    S_Y0 = S("S_Y0", [2 * NTK, 16 * 264]); S_MIX = S("S_MIX", [NTK, D])
    hmid = S("hmid", [D, NTK]); S_H1 = S("S_H1", [D, NTK]); S_U1 = S("S_U1", [D, NTK])
    en = ["r", "k", "v", "wl0", "wl1", "al0", "al1", "gate"]
    EO = {n: S("S_e_" + n, [NTK, D]) for n in en}
    e2n = ["w0o", "w1o", "kd0", "kd1", "nkk", "b0", "b1", "bonus"]
    E2O = {n: S("S_e2_" + n, [NTK, D]) for n in e2n}
    S_Y1 = S("S_Y1", [2 * NTK, D])

    modv = p.sbuf("modv", [128, 2, 6, KC, 2], F32)
    ng = p.sbuf("ng_s", [128, 2, 4, KC], F32)
    p.dma('sp', ng, ng[:, :, :, :], ng_d, ng_d.h[:, :, :, :])

    dbg = {}
    stage_ada(p, condT, adaw, adab, modv)
    if upto >= 1:
        stage_inproj(p, xT, modv, ng, w_tm, w_bqk, w_gi, w_gf, S_P, S_BQK, S_GI, S_GF)
    if upto >= 2:
        stage_c1(p, S_GI, S_GF, gbias, S_GD, S_GS, S_GE, S_BQK, wc_d, S_BQKtm, S_P, S_AQK, cos_d, sin_d, ident)
    if upto >= 3:
        if CHUNKED_L0:
            stage_gates2(p, S_GI, S_GF, gbias, retlf_d, S_GE, SU, SW, SA, SK, SD, SCOL, ident)
            scans_l0_chunked(p, S_AQK, S_BQKtm, S_P, SU, SCOL, SD, masks_d, ident, S_Y0, dirs=(0,))
            scans_l0_chunked(p, S_AQK, S_BQKtm, S_P, SU, SCOL, SD, masks_d, ident, S_Y0, dirs=(1,))
        else:
            scans_l0(p, S_AQK, S_BQKtm, S_P, S_GD, S_GS, ret_ds, S_Y0)
    if upto >= 4:
        stage_c3(p, S_Y0, S_GE, S_P, gr_d, gm_d, S_MIX)
    if upto >= 5:
        stage_post(p, 0, GROUPS0, S_MIX, xT, modv, ng, w_out0, w1_0, w2_0, ident, hmid, S_H1, 0, u1_out=S_U1)
    if upto >= 6:
        stage_e(p, S_U1, mu_d, w_rkv, lw1, lw2, la1, la2, g1p, g2p, EO)
    if upto >= 7:
        stage_e2(p, EO, vecs_e2, E2O)
    if upto >= 8:
        scans_l1(p, EO["r"], [E2O["w0o"], E2O["w1o"]], [E2O["kd0"], E2O["kd1"]], E2O["nkk"], [E2O["b0"], E2O["b1"]], EO["v"], S_Y1)
    if upto >= 9:
        stage_g0(p, S_Y1, E2O["bonus"], EO["gate"], vecs_g0, S_MIX)
    if upto >= 10:
        stage_post(p, 1, GROUPS1, S_MIX, S_H1, modv, ng, w_out1, w1_1, w2_1, ident, hmid, outT, NCTX)
    if upto < 10:
        src = {0: None, 1: S_BQK, 2: S_BQK, 3: S_BQK, 4: S_MIX, 5: S_H1, 6: S_U1, 7: S_U1, 8: S_U1, 9: S_MIX}[upto]
        p.begin_stage()
        if src is not None:
            tt = p.sbuf("dbgt", [128, 2048])
            fm = (src.h.shape[0] != NTK)
            for kc in range(KC if fm else 16):
                if fm:
                    rows = min(128, src.h.shape[0] - kc * 128)
                    if rows <= 0:
                        break
                    p.dma('sp', tt, tt[:rows, :], src, src.h[kc * 128:kc * 128 + rows, 256:2304])
                else:
                    p.dma('sp', tt, tt[:, :], src, src.h[256 + kc * 128:256 + (kc + 1) * 128, 0:2048])
                p.dma('sp', outT, outT.h[kc * 128:(kc + 1) * 128, :], tt, tt[:, :])
        p.end_stage()
    p.finish()
    return p


NCH = NTK // 128


def stage_gates2(p, S_GI, S_GF, gbias, retlf_d, S_GE, SU, SW, SA, SK, SD, SCOL, ident):
    p.begin_stage()
    T_ALL = NTK
    R = 48
    gi = p.sbuf("gi", [R, T_ALL]); gf = p.sbuf("gf", [R, T_ALL]); gb = p.sbuf("gb", [16, 2])
    p.op('dve', lambda e: e.memset(gi[:, :], 0.0), writes=[gi])
    p.op('dve', lambda e: e.memset(gf[:, :], 0.0), writes=[gf])
    p.dma('sp', gb, gb[:, :], gbias, gbias.h[:, :])
    for (tile_, src) in ((gi, S_GI), (gf, S_GF)):
        items = [('sp', tile_, tile_[0:8, :], src, src.h[0:8, :])]
        for (c0, n, ap) in _proc_src(src, 8, 8, T_ALL, True):
            items.append(('sp', tile_, tile_[8:16, c0:c0 + n], src, ap))
        p.dma_group(items, allow_slow_non_contiguous=True)
    g1 = p.sbuf("g1", [R, T_ALL]); g2 = p.sbuf("g2", [R, T_ALL]); gm = p.sbuf("gm", [R, T_ALL + 1]); Fx = p.sbuf("Fx", [R, T_ALL + 1])
    bb = p.sbuf("bb", [R, T_ALL]); g3 = p.sbuf("g3", [R, T_ALL]); dl = p.sbuf("dl", [R, NCH]); zz = p.sbuf("zz", [R, T_ALL])
    p.op('dve', lambda e: e.memset(zz[:, :], 0.0), writes=[zz])
    p.op('dve', lambda e: e.tensor_scalar(out=gi[0:16, :], in0=gi[0:16, :], scalar1=gb[:, 0:1], scalar2=0.0, op0=ALU.add, op1=ALU.add), reads=[gi, gb], writes=[gi])
    p.op('dve', lambda e: e.tensor_scalar(out=gf[0:16, :], in0=gf[0:16, :], scalar1=gb[:, 1:2], scalar2=0.0, op0=ALU.add, op1=ALU.add), reads=[gf, gb], writes=[gf])
    p.op('act', lambda e: e.activation(out=g1[0:16, :], in_=gf[0:16, :], func=AF.Abs), reads=[gf], writes=[g1])
    p.op('act', lambda e: e.activation(out=g1[0:16, :], in_=g1[0:16, :], func=AF.Exp, scale=-1.0), reads=[g1], writes=[g1])
    p.op('act', lambda e: e.activation(out=g1[0:16, :], in_=g1[0:16, :], func=AF.Ln, bias=1.0, scale=1.0), reads=[g1], writes=[g1])
    p.op('dve', lambda e: e.tensor_single_scalar(out=g2[0:16, :], in_=gf[0:16, :], scalar=0.0, op=ALU.min), reads=[gf], writes=[g2])
    p.op('dve', lambda e: e.tensor_tensor(out=gf[0:16, :], in0=g2[0:16, :], in1=g1[0:16, :], op=ALU.subtract), reads=[g1, g2], writes=[gf])
    p.dma('sp', gf, gf[32:48, :], retlf_d, retlf_d.h[:, :])
    p.op('dve', lambda e: e.memset(gm[:, 0:1], 0.0), writes=[gm])
    p.op('dve', lambda e: e.tensor_tensor_scan(out=gm[:, 1:T_ALL + 1], data0=gf[:, :], data1=gi[:, :], initial=0.0, op0=ALU.add, op1=ALU.max),
         reads=[gf, gi, gm], writes=[gm])
    p.op('dve', lambda e: e.memset(Fx[:, 0:1], 0.0), writes=[Fx])
    p.op('dve', lambda e: e.tensor_tensor_scan(out=Fx[:, 1:T_ALL + 1], data0=gf[:, :], data1=zz[:, :], initial=0.0, op0=ALU.add, op1=ALU.add),
         reads=[gf, zz, Fx], writes=[Fx])
    v3 = lambda ap: ap.rearrange("p (c l) -> p c l", l=128)
    SH3 = [R, NCH, 128]
    p.op('dve', lambda e: e.tensor_tensor(out=v3(bb[:, :]), in0=v3(Fx[:, 1:T_ALL + 1]), in1=v3(Fx[:, 0:T_ALL])[:, :, 0:1].to_broadcast(SH3), op=ALU.subtract),
         reads=[Fx], writes=[bb])

    def store_nat(dst, tile_, rows48=True):
        items = [('sp', dst, dst.h[0:8, :], tile_, tile_[0:8, :])]
        for (c0, n, ap) in _proc_src(dst, 8, 8, T_ALL, True):
            items.append(('sp', dst, ap, tile_, tile_[8:16, c0:c0 + n]))
        if rows48:
            items.append(('sp', dst, dst.h[32:40, :], tile_, tile_[32:40, :]))
            for (c0, n, ap) in _proc_src(dst, 40, 8, T_ALL, True):
                items.append(('sp', dst, ap, tile_, tile_[40:48, c0:c0 + n]))
        p.dma_group(items, allow_slow_non_contiguous=True)
    m = gm[:, 1:T_ALL + 1]
    me3 = v3(gm[:, 0:T_ALL])[:, :, 0:1].to_broadcast(SH3)
    p.op('dve', lambda e: e.tensor_tensor(out=g1[:, :], in0=bb[:, :], in1=m, op=ALU.subtract), reads=[bb, gm], writes=[g1])
    store_nat(SU, g1)
    p.op('dve', lambda e: e.tensor_tensor(out=v3(g2[:, :]), in0=v3(g1[:, :]), in1=me3, op=ALU.add), reads=[g1, gm], writes=[g2])
    p.op('act', lambda e: e.activation(out=g2[:, :], in_=g2[:, :], func=AF.Exp), reads=[g2], writes=[g2])
    store_nat(SA, g2)
    p.op('dve', lambda e: e.tensor_tensor(out=g3[:, :], in0=gi[:, :], in1=bb[:, :], op=ALU.subtract), reads=[gi, bb], writes=[g3])
    store_nat(SW, g3)
    blm = p.sbuf("blm", [R, NCH])
    p.op('dve', lambda e: e.tensor_tensor(out=blm[:, :], in0=v3(bb[:, :])[:, :, 127], in1=v3(gm[:, 1:T_ALL + 1])[:, :, 127], op=ALU.subtract),
         reads=[bb, gm], writes=[blm])
    p.op('dve', lambda e: e.tensor_tensor(out=v3(zz[:, :]), in0=v3(g3[:, :]), in1=blm[:, :].unsqueeze(2).to_broadcast(SH3), op=ALU.add), reads=[g3, blm], writes=[zz])
    p.op('act', lambda e: e.activation(out=zz[:, :], in_=zz[:, :], func=AF.Exp), reads=[zz], writes=[zz])
    store_nat(SK, zz)
    p.op('dve', lambda e: e.tensor_tensor(out=dl[:, :], in0=blm[:, :], in1=v3(gm[:, 0:T_ALL])[:, :, 0], op=ALU.add), reads=[blm, gm], writes=[dl])
    p.op('act', lambda e: e.activation(out=dl[:, :], in_=dl[:, :], func=AF.Exp), reads=[dl], writes=[dl])
    p.dma('sp', SD, SD.h[:, :], dl, dl[:, :])
    em = p.sbuf("em", [R, T_ALL])
    p.op('act', lambda e: e.activation(out=em[:, :], in_=gm[:, 1:T_ALL + 1], func=AF.Exp, scale=-1.0), reads=[gm], writes=[em])
    store_nat(S_GE, em, rows48=False)
    idt = p.sbuf("idt48", [128, 128]); p.dma('sp', idt, idt[:, :], ident, ident.h[:, :])
    nat = [p.sbuf(f"nat{q}", [R, T_ALL]) for q in range(3)]
    for q, srcd in enumerate((SW, SA, SK)):
        p.dma('sp', nat[q], nat[q][:, :], srcd, srcd.h[:, :])
    pst = [p.psum(f"gps{i}", [128, 3, 48]) for i in range(2)]
    ct = [p.sbuf(f"ct{i}", [128, 3, 48]) for i in range(2)]
    for c in range(NCH):
        ps, cc = pst[c % 2], ct[c % 2]
        for q in range(3):
            p.op('pe', lambda e: e.transpose(ps[:, q, :], nat[q][:, c * 128:(c + 1) * 128], idt[:R, :R]), reads=[nat[q], idt], writes=[ps])
        p.op('act', lambda e: e.copy(cc[:, :, :], ps[:, :, :]), reads=[ps], writes=[cc])
        p.dma('sp', SCOL, SCOL.h[c * 128:(c + 1) * 128, :], cc, cc.h[:, :, :].rearrange("p q g -> p (q g)"))
    p.end_stage()


def scans_l0_chunked(p, S_AQK, S_BQKtm, S_P, SU, SCOL, SD, masks_d, ident, S_Y0, dirs=(0, 1)):
    import concourse.bass as bass
    p.begin_stage()
    idt = p.sbuf("idt", [128, 128]); p.dma('sp', idt, idt[:, :], ident, ident.h[:, :])
    mk_ = [p.sbuf(f"mask{d}", [128, 128]) for d in range(2)]
    for d in range(2):
        p.dma('sp', mk_[d], mk_[d][:, :], masks_d, masks_d.h[d])
    ones_r = p.sbuf("ones_r", [1, 128]); p.op('dve', lambda e: e.memset(ones_r[:, :], 1.0), writes=[ones_r])
    QKr = [p.sbuf(f"QKr{i}", [128, 16, 128]) for i in range(2)]
    QKm = [p.sbuf(f"QKm{i}", [128, 16, 128]) for i in range(2)]
    Vx = [p.sbuf(f"Vx{i}", [128, 16, 257]) for i in range(2)]
    Vb = [p.sbuf(f"Vb{i}", [128, 16, 257], BF16) for i in range(2)]
    for i in range(2):
        p.op('dve', lambda e: e.memset(Vx[i][:, :, 256:257], 1.0), writes=[Vx[i]])
    Yo = [p.sbuf(f"Yo{i}", [128, 16, 264]) for i in range(2)]
    for i in range(2):
        p.op('pool', lambda e: e.memset(Yo[i][:, :, :], 0.0), writes=[Yo[i]])
    Cols = [p.sbuf(f"Cols{i}", [128, 3, 48]) for i in range(2)]
    Dall = p.sbuf("Dall", [128, 48 * NCH])
    p.dma('sp', Dall, Dall[:, :], SD, bass.AP(SD.h.tensor, 0, [[0, 128], [1, 48 * NCH]]))
    Ur = [p.sbuf(f"Ur{i}", [1, 16, 128]) for i in range(2)]
    C = [p.sbuf(f"C{s}", [128, 257]) for s in range(16)]
    Cb = [p.sbuf(f"Cb{s}", [128, 257], BF16) for s in range(16)]
    QT = [p.sbuf(f"QT{i}", [128, 128], BF16) for i in range(2)]; KT = [p.sbuf(f"KT{i}", [128, 128], BF16) for i in range(2)]
    Wt = [p.sbuf(f"Wt{i}", [128, 128]) for i in range(2)]; Pm = [p.sbuf(f"Pm{i}", [128, 128], BF16) for i in range(2)]
    Bs = [p.sbuf(f"Bs{i}", [128, 257]) for i in range(2)]; Kk = [p.sbuf(f"Kk{i}", [128, 128], BF16) for i in range(2)]
    ps_t = [p.psum(f"pst{i}", [128, 512]) for i in range(2)]
    ps_su = [p.psum(f"pssu{i}", [128, 2, 128]) for i in range(2)]
    ps_a = p.psum("psa", [128, 257]); ps_b = p.psum("psb", [128, 257]); ps_c = p.psum("psc", [128, 257])
    SCL = 128.0 ** -0.5
    it = 0
    tens = lambda t_: t_.h.tensor
    import os
    LVL = int(os.environ.get("CHK_LEVEL", "9")); NCHL = int(os.environ.get("CHK_NCH", str(NCH)))
    for d in dirs:
        for s in range(16):
            p.op('dve', lambda e: e.memset(C[s][:, :], 0.0), writes=[C[s]])
            p.op('pool', lambda e: e.memset(Cb[s][:, :], 0.0), writes=[Cb[s]])
        for c in range(NCHL):
            cn = c if d == 0 else ((1 - c) if c < 2 else (19 - c))
            r0 = cn * 128
            b_ = (d * NCH + c) % 2
            qr, qm, vx, vb, yo = QKr[b_], QKm[b_], Vx[b_], Vb[b_], Yo[b_]
            cols, ur = Cols[b_], Ur[b_]
            p.dma('sp', qr, qr.h[:, :, :].rearrange("p h e -> p (h e)"), S_AQK, S_AQK.h[r0:r0 + 128, :])
            p.dma('sp', qm, qm.h[:, :, :].rearrange("p h e -> p (h e)"), S_BQKtm, S_BQKtm.h[r0:r0 + 128, :])
            p.dma_group([('sp', vx, vx[:, 0:8, 0:256], S_P, S_P.h[r0:r0 + 128, 2048:4096].rearrange("p (h e) -> p h e", e=256)),
                         ('sp', vx, vx[:, 8:16, 0:256], S_P, S_P.h[r0:r0 + 128, 6144:8192].rearrange("p (h e) -> p h e", e=256))])
            p.op('act', lambda e: e.copy(vb[:, :, :], vx[:, :, :]), reads=[vx], writes=[vb])
            p.dma('sp', cols, cols.h[:, :, :].rearrange("p q g -> p (q g)"), SCOL, SCOL.h[r0:r0 + 128, :])
            p.dma_group([('sp', ur, ur[0:1, 0:8, :], SU, bass.AP(tens(SU), (32 + d * 8) * NTK + r0, [[0, 1], [NTK, 8], [1, 128]])),
                         ('sp', ur, ur[0:1, 8:16, :], SU, bass.AP(tens(SU), (d * 8) * NTK + r0, [[0, 1], [NTK, 8], [1, 128]]))])
            for s in range(16 if LVL >= 1 else 0):
                h = s % 8
                qk = qr if s < 8 else qm
                g = (32 if s < 8 else 0) + d * 8 + h
                wcol, acol, kcol = cols[:, 0, g:g + 1], cols[:, 1, g:g + 1], cols[:, 2, g:g + 1]
                dcol = Dall[:, g * NCH + c:g * NCH + c + 1]
                i2 = it % 2
                it += 1
                pt, psu = ps_t[i2], ps_su[i2]
                qT, kT, wt, pm, bs, kk = QT[i2], KT[i2], Wt[i2], Pm[i2], Bs[i2], Kk[i2]
                p.op('pe', lambda e: e.transpose(pt[:, 0:128], qk[:, h, :], idt[:, :]), reads=[qk, idt], writes=[pt])
                p.op('pe', lambda e: e.transpose(pt[:, 128:256], qk[:, 8 + h, :], idt[:, :]), reads=[qk, idt], writes=[pt])
                p.op('act', lambda e: e.activation(out=qT[:, :], in_=pt[:, 0:128], func=AF.Identity, scale=SCL), reads=[pt], writes=[qT])
                p.op('dve', lambda e: e.tensor_copy(kT[:, :], pt[:, 128:256]), reads=[pt], writes=[kT])
                p.op('pe', lambda e: e.matmul(psu[:, 0, :], kT[:, :], qT[:, :], start=True, stop=True), reads=[kT, qT], writes=[psu])
                if LVL < 2:
                    continue
                p.op('pe', lambda e: e.matmul(psu[:, 1, :], ones_r[0:1, :], ur[0:1, s, :], start=True, stop=False), reads=[ones_r, ur], writes=[psu])
                p.op('pe', lambda e: e.matmul(psu[:, 1, :], idt[:, :], mk_[d][:, :], start=False, stop=True), reads=[idt, mk_[d]], writes=[psu])
                p.op('act', lambda e: e.activation(out=wt[:, :], in_=psu[:, 1, :], func=AF.Exp, bias=wcol, scale=1.0), reads=[psu, cols], writes=[wt])
                p.op('dve', lambda e: e.tensor_tensor(out=pm[:, :], in0=psu[:, 0, :], in1=wt[:, :], op=ALU.mult), reads=[psu, wt], writes=[pm])
                if LVL < 3:
                    continue
                p.op('pe', lambda e: e.matmul(ps_a[:, :], pm[:, :], vb[:, s, :], start=True, stop=True), reads=[pm, vb], writes=[ps_a])
                p.op('pe', lambda e: e.matmul(ps_b[:, :], qT[:, :], Cb[s][:, :], start=True, stop=True), reads=[qT, Cb[s]], writes=[ps_b])
                p.op('act', lambda e: e.activation(out=bs[:, :], in_=ps_b[:, :], func=AF.Identity, scale=acol), reads=[ps_b, cols], writes=[bs])
                p.op('dve', lambda e: e.tensor_tensor(out=yo[:, s, 0:257], in0=ps_a[:, :], in1=bs[:, :], op=ALU.add), reads=[ps_a, bs], writes=[yo])
                p.op('act', lambda e: e.activation(out=kk[:, :], in_=qk[:, 8 + h, :], func=AF.Identity, scale=kcol), reads=[qk, cols], writes=[kk])
                p.op('pe', lambda e: e.matmul(ps_c[:, :], kk[:, :], vb[:, s, :], start=True, stop=True), reads=[kk, vb], writes=[ps_c])
                p.op('dve', lambda e: e.scalar_tensor_tensor(out=C[s][:, :], in0=C[s][:, :], scalar=dcol, in1=ps_c[:, :], op0=ALU.mult, op1=ALU.add),
                     reads=[C[s], Dall, ps_c], writes=[C[s]])
                p.op('act', lambda e: e.copy(Cb[s][:, :], C[s][:, :]), reads=[C[s]], writes=[Cb[s]])
            p.dma('sp', S_Y0, S_Y0.h[d * NTK + r0:d * NTK + r0 + 128, :], yo, yo.h[:, :, :].rearrange("p s c -> p (s c)"))
    p.end_stage()

import time as _time

N_CORES = 8


def _vl(v):
    return np.ascontiguousarray(np.asarray(v, np.float32).reshape(KC, 128).T)


def fused_shared_inputs(ada_w, ada_b, norm_g, mlp_w_in, mlp_w_out, ev_w_in, ev_b_gate, ev_conv_qk,
                        ev_gn_ret, ev_gn_mlstm, ev_w_out, od_mu, od_w_rkv, od_w0, od_w1, od_w2, od_a0, od_a1, od_a2,
                        od_g1, od_g2, od_k_k, od_k_a, od_r_k, od_lnx_w, od_lnx_b, od_w_out):
    f32 = np.float32
    A = lambda v: np.ascontiguousarray(np.asarray(v, dtype=f32))
    sh = {}
    sh["adaw0"] = A(ada_w[0]); sh["adaw1"] = A(ada_w[1])
    sh["adab"] = A(np.stack([A(ada_b[l]).reshape(192, 128).T for l in range(2)], axis=1))
    sh["ng"] = A(np.stack([np.stack([_vl(norm_g[l, i]) for i in range(4)], axis=1) for l in range(2)], axis=1))
    w_in = A(ev_w_in[0])
    cols = np.concatenate([np.arange(0, 6144), np.arange(8192, 12288)])
    sh["w_tm"] = A(w_in[:, cols]); sh["w_bqk"] = A(w_in[:, 6144:8192])
    gi_cols = 12288 + np.concatenate([np.arange(0, 8), np.arange(16, 24)])
    gf_cols = 12288 + np.concatenate([np.arange(8, 16), np.arange(24, 32)])
    sh["w_gi"] = A(w_in[:, gi_cols]); sh["w_gf"] = A(w_in[:, gf_cols])
    bg = A(ev_b_gate[0])
    sh["gbias"] = A(np.stack([bg[gi_cols - 12288], bg[gf_cols - 12288]], axis=1))
    conv = A(ev_conv_qk[0]).reshape(9, 2048)
    sh["wc"] = A(conv.T.reshape(16, 128, 9).transpose(1, 0, 2))
    inv = np.power(f32(10000.0), -np.arange(32, dtype=f32) / f32(32))
    pos = np.arange(2048)
    rows = (pos // 64).astype(f32); colsp = (pos % 64).astype(f32)
    ar = rows[:, None] * inv[None, :]; ac = colsp[:, None] * inv[None, :]
    sh["cos"] = A(np.concatenate([np.cos(ar), np.cos(ar), np.cos(ac), np.cos(ac)], axis=1).reshape(16, 128, 128))
    sh["sins"] = A(np.concatenate([-np.sin(ar), np.sin(ar), -np.sin(ac), np.sin(ac)], axis=1).reshape(16, 128, 128))
    sh["ident"] = np.eye(128, dtype=f32)
    hh_ = np.arange(8, dtype=f32) / f32(7)
    lg = np.log1p(-np.exp2(-(f32(5.0) + f32(7.0) * hh_))).astype(f32)
    gam = [np.exp(lg).astype(f32), np.exp(lg[::-1]).astype(f32)]
    rds = np.empty((2, 2, 8, NTK), f32)
    for d in range(2):
        rds[d, 0] = gam[d][:, None]
        rds[d, 1] = f32(128.0 ** -0.5)
    sh["ret_ds"] = rds
    rlf = np.empty((16, NTK), f32)
    for d in range(2):
        lgd = lg if d == 0 else lg[::-1]
        rlf[d * 8:(d + 1) * 8] = lgd[:, None]
    sh["retlf"] = rlf
    jj = np.arange(128)[:, None]; ii = np.arange(128)[None, :]
    NEG = f32(-1e30)
    sh["masks"] = np.stack([np.where(jj <= ii, f32(0), NEG), np.where(jj >= ii, f32(0), NEG)]).astype(f32)
    sh["gn_ret"] = A(np.tile(A(ev_gn_ret[0]), (128, 1))); sh["gn_ml"] = A(np.tile(A(ev_gn_mlstm[0]), (128, 1)))
    sh["w_out0"] = A(ev_w_out[0]); sh["w1_0"] = A(mlp_w_in[0]); sh["w2_0"] = A(mlp_w_out[0])
    sh["mu"] = A(np.stack([_vl(od_mu[0, i]) for i in range(6)], axis=1))
    for i in range(3):
        sh[f"w_rkv{i}"] = A(od_w_rkv[0, i])
    for d in range(2):
        sh[f"lw1_{d}"] = A(od_w1[0, d]); sh[f"lw2_{d}"] = A(od_w2[0, d]); sh[f"la1_{d}"] = A(od_a1[0, d]); sh[f"la2_{d}"] = A(od_a2[0, d])
    g1p = np.zeros((D, 512), f32); g1p[:, :480] = A(od_g1[0]); g2p = np.zeros((512, D), f32); g2p[:480] = A(od_g2[0])
    sh["g1p"] = g1p; sh["g2p"] = g2p
    sh["vecs_e2"] = A(np.stack([np.tile(A(v), (128, 1)) for v in [od_k_k[0], od_k_a[0], od_r_k[0], od_w0[0, 0], od_w0[0, 1], od_a0[0, 0], od_a0[0, 1]]]))
    sh["vecs_g0"] = A(np.stack([np.tile(A(od_lnx_w[0]), (128, 1)), np.tile(A(od_lnx_b[0]), (128, 1))]))
    sh["w_out1"] = A(od_w_out[0]); sh["w1_1"] = A(mlp_w_in[1]); sh["w2_1"] = A(mlp_w_out[1])
    return sh


def fused_core_inputs(b, x, c, ctx, c_ctx):
    f32 = np.float32
    xT = np.ascontiguousarray(np.concatenate([np.asarray(ctx[b], f32), np.asarray(x[b], f32)], axis=0).T)
    c2 = np.stack([np.asarray(c[b], f32), np.asarray(c_ctx, f32)], axis=0)
    condT = np.ascontiguousarray(c2.T.reshape(KC, 128, 2).transpose(1, 0, 2))
    return dict(xT=xT, condT=condT)


def kernel(x, c, ctx, c_ctx, **w):
    t0 = _time.time()
    sh = fused_shared_inputs(**w)
    p = build_fused(10)
    print(f"[kernel] built fused program: {p.n_ops} ops in {_time.time() - t0:.1f}s", flush=True)
    in_maps = []
    for j in range(N_CORES):
        m = dict(sh)
        m.update(fused_core_inputs(j % 4, x, c, ctx, c_ctx))
        in_maps.append(m)
    res = run_bass_kernel_spmd(p.nc, in_maps, core_ids=list(range(N_CORES)))
    print(f"[kernel] fused launch done {_time.time() - t0:.1f}s", flush=True)
    out = np.empty((4, 2048, D), np.float32)
    for b in range(4):
        out[b] = res.results[b]["outT"].T
    return out
```
